# Optimizing a Trainium2 kernel written in Bass

```python
import math
import jax
import jax.numpy as jnp
from jax import lax
import numpy as np

D_MODEL = 1024
BATCH = 8
SEQ = 2048
DEPTH = 2
DEC_BATCH = 128
DEC_SEQ = 4
PAST_LEN = 16384
PAGE_SIZE = 128

EPS = 1e-6
CHUNK = 64
CONV_W = 4
N_BRANCH = 3
BRANCH_W = D_MODEL // 2

GLA_DK = 128
GLA_DV = 128
GLA_HEADS = BRANCH_W // GLA_DV
GLA_RANK = 16
GLA_GATE_TEMP = 16.0

SSD_HEADDIM = 64
SSD_HEADS = BRANCH_W // SSD_HEADDIM
SSD_GROUPS = 2
SSD_REP = SSD_HEADS // SSD_GROUPS
SSD_STATE = 128
SSD_INNER = SSD_HEADS * SSD_HEADDIM
SSD_CONV_CH = SSD_INNER + 2 * SSD_GROUPS * SSD_STATE

GDN_DK = 128
GDN_DV = 128
GDN_HEADS = BRANCH_W // GDN_DV
GDN_CONV_CH = GDN_HEADS * (2 * GDN_DK + GDN_DV)

FFN_HIDDEN = -(-(8 * D_MODEL) // (3 * 256)) * 256

IN_WIDTHS = (GLA_HEADS * GLA_DK, GLA_HEADS * GLA_DK, GLA_HEADS * GLA_DV, GLA_HEADS * GLA_DV, GLA_RANK,
             SSD_INNER, SSD_CONV_CH, SSD_HEADS,
             GDN_CONV_CH, GDN_HEADS, GDN_HEADS, GDN_HEADS * GDN_DV,
             N_BRANCH * D_MODEL)
N_IN = sum(IN_WIDTHS)
SPLIT_IDX = tuple(sum(IN_WIDTHS[:i + 1]) for i in range(len(IN_WIDTHS) - 1))

kernel_name = 'hybrid_gla_ssd_gdn_decoder_step'


def _rms_norm(x, g):
    xf = x.astype(jnp.float32)
    y = xf * lax.rsqrt(jnp.mean(xf * xf, axis=-1, keepdims=True) + EPS)
    return (y * g.astype(jnp.float32)).astype(x.dtype)


def _l2_norm(x):
    return x * lax.rsqrt(jnp.sum(x * x, axis=-1, keepdims=True) + EPS)


def _causal_conv(x, buf, w, b=None):
    L = x.shape[1]
    xp = jnp.concatenate([buf.astype(x.dtype), x], axis=1)
    y = xp[:, 0:L] * w[0]
    for j in range(1, CONV_W):
        y = y + xp[:, j:j + L] * w[j]
    if b is not None:
        y = y + b
    return y, xp[:, L:]


def _to_chunks(t, c):
    return jnp.moveaxis(t.reshape(t.shape[0], t.shape[1] // c, c, *t.shape[2:]), 1, 0)


def _from_chunks(t):
    t = jnp.moveaxis(t, 0, 1)
    return t.reshape(t.shape[0], t.shape[1] * t.shape[2], *t.shape[3:])


def _gla_chunked(q, k, v, log_a, S0):
    C = math.gcd(q.shape[1], CHUNK)
    mask = jnp.tril(jnp.ones((C, C), bool))

    def step(S, inp):
        qi, ki, vi, gi = inp
        b = jnp.cumsum(gi, axis=1)
        qd = qi * jnp.exp(b)
        kd = ki * jnp.exp(-b)
        A = jnp.where(mask, jnp.einsum('bthk,bshk->bhts', qd, kd), 0.0)
        o = jnp.einsum('bhts,bshv->bthv', A, vi) + jnp.einsum('bthk,bhkv->bthv', qd, S)
        bl = b[:, -1]
        S = S * jnp.exp(bl)[..., None] + jnp.einsum('bshk,bshv->bhkv', ki * jnp.exp(bl[:, None] - b), vi)
        return S, o

    S, o = lax.scan(step, S0.astype(jnp.float32), tuple(_to_chunks(t, C) for t in (q, k, v, log_a)))
    return _from_chunks(o), S


def _ssd_chunked(x, dt, A, Bm, Cm, S0):
    C = math.gcd(x.shape[1], CHUNK)
    mask = jnp.tril(jnp.ones((C, C), bool))[None, :, :, None, None]

    def step(S, inp):
        xi, dti, Bi, Ci = inp
        cum = jnp.cumsum(dti * A, axis=1)
        seg = cum[:, :, None] - cum[:, None, :]
        Lm = jnp.exp(jnp.where(mask, seg, -jnp.inf))
        CB = jnp.einsum('btgn,bsgn->btsg', Ci, Bi)
        M = CB[..., None] * Lm * dti[:, None]
        y = jnp.einsum('btsgr,bsgrp->btgrp', M, xi)
        y = y + jnp.einsum('btgn,bgrpn->btgrp', Ci, S) * jnp.exp(cum)[..., None]
        last = cum[:, -1]
        wts = jnp.exp(last[:, None] - cum) * dti
        S = S * jnp.exp(last)[..., None, None] + jnp.einsum('bsgn,bsgr,bsgrp->bgrpn', Bi, wts, xi)
        return S, y

    S, y = lax.scan(step, S0.astype(jnp.float32), tuple(_to_chunks(t, C) for t in (x, dt, Bm, Cm)))
    return _from_chunks(y), S


def _gdn_chunked(q, k, v, g, beta, S0):
    C = math.gcd(q.shape[1], CHUNK)
    V = v.shape[-1]
    mask = jnp.tril(jnp.ones((C, C), bool))
    strict = jnp.tril(jnp.ones((C, C), bool), k=-1)
    eye = jnp.eye(C, dtype=jnp.float32)

    def step(S, inp):
        qi, ki, vi, gi, bi = inp
        cum = jnp.cumsum(gi, axis=1)
        cum_h = jnp.swapaxes(cum, 1, 2)
        seg = cum_h[..., :, None] - cum_h[..., None, :]
        Lm = jnp.exp(jnp.where(mask, seg, -jnp.inf))
        Akk = jnp.where(strict, jnp.einsum('bthk,bshk,bsh->bhts', ki, ki, bi) * Lm, 0.0)
        Akk = jnp.einsum('bhts,bth->bhts', Akk, bi)
        IA = Akk + eye
        vb = jnp.einsum('bthv,bth->bhtv', vi, bi)
        kbd = jnp.einsum('bthk,bth->bhtk', ki, bi * jnp.exp(cum))
        sol = lax.linalg.triangular_solve(IA, jnp.concatenate([vb, kbd], axis=-1),
                                          left_side=True, lower=True, unit_diagonal=True)
        U, W = sol[..., :V], sol[..., V:]
        vnew = U - jnp.einsum('bhtk,bhkv->bhtv', W, S)
        Aqk = jnp.where(mask, jnp.einsum('bthk,bshk->bhts', qi, ki) * Lm, 0.0)
        o = (jnp.einsum('bthk,bhkv->bthv', qi * jnp.exp(cum)[..., None], S)
             + jnp.einsum('bhts,bhsv->bthv', Aqk, vnew))
        last = cum_h[..., -1]
        S = (S * jnp.exp(last)[..., None, None]
             + jnp.einsum('bshk,bhs,bhsv->bhkv', ki, jnp.exp(last[..., None] - cum_h), vnew))
        return S, o

    S, o = lax.scan(step, S0.astype(jnp.float32), tuple(_to_chunks(t, C) for t in (q, k, v, g, beta)))
    return _from_chunks(o), S


def _token_mixers(h, p, l, st):
    s_gla, s_ssd, cv_ssd, s_gdn, cv_gdn = st
    Bn, L, _ = h.shape
    u = jnp.einsum('bld,dn->bln', h, p['w_in'][l]).astype(jnp.float32)
    (gq, gk, gv, gr, glr, sz, sxbc, sdt, dqkv, da, db, dg, mg) = jnp.split(u, SPLIT_IDX, axis=-1)

    q = gq.reshape(Bn, L, GLA_HEADS, GLA_DK) * (GLA_DK ** -0.5)
    k = gk.reshape(Bn, L, GLA_HEADS, GLA_DK)
    v = gv.reshape(Bn, L, GLA_HEADS, GLA_DV)
    logit = jnp.einsum('blr,rn->bln', glr, p['w_gla_gate'][l]) + p['b_gla_gate'][l]
    log_a = (jax.nn.log_sigmoid(logit.astype(jnp.float32)) / GLA_GATE_TEMP).reshape(Bn, L, GLA_HEADS, GLA_DK)
    o, s_gla_new = _gla_chunked(q, k, v, log_a, s_gla)
    y_gla = (_rms_norm(o, p['g_gla_norm'][l]) * jax.nn.silu(gr.reshape(Bn, L, GLA_HEADS, GLA_DV))).reshape(Bn, L, BRANCH_W)

    xbc, cv_ssd_new = _causal_conv(sxbc, cv_ssd, p['w_ssd_conv'][l], p['b_ssd_conv'][l])
    xbc = jax.nn.silu(xbc)
    sx, sB, sC = jnp.split(xbc, (SSD_INNER, SSD_INNER + SSD_GROUPS * SSD_STATE), axis=-1)
    xs = sx.reshape(Bn, L, SSD_GROUPS, SSD_REP, SSD_HEADDIM)
    Bm = sB.reshape(Bn, L, SSD_GROUPS, SSD_STATE)
    Cm = sC.reshape(Bn, L, SSD_GROUPS, SSD_STATE)
    dt = jax.nn.softplus(sdt + p['ssd_dt_bias'][l]).reshape(Bn, L, SSD_GROUPS, SSD_REP)
    A = -jnp.exp(p['ssd_a_log'][l].astype(jnp.float32)).reshape(SSD_GROUPS, SSD_REP)
    S0 = s_ssd.reshape(Bn, SSD_GROUPS, SSD_REP, SSD_HEADDIM, SSD_STATE)
    ys, S_new = _ssd_chunked(xs, dt, A, Bm, Cm, S0)
    ys = ys + xs * p['ssd_d'][l].reshape(SSD_GROUPS, SSD_REP)[..., None]
    ys = ys * jax.nn.silu(sz.reshape(Bn, L, SSD_GROUPS, SSD_REP, SSD_HEADDIM))
    ys = _rms_norm(ys.reshape(Bn, L, SSD_GROUPS, SSD_REP * SSD_HEADDIM),
                   p['g_ssd_norm'][l].reshape(SSD_GROUPS, SSD_REP * SSD_HEADDIM))
    y_ssd = ys.reshape(Bn, L, BRANCH_W)
    s_ssd_new = S_new.reshape(Bn, SSD_HEADS, SSD_HEADDIM, SSD_STATE)

    qkv, cv_gdn_new = _causal_conv(dqkv, cv_gdn, p['w_gdn_conv'][l])
    qkv = jax.nn.silu(qkv)
    dq, dk, dv = jnp.split(qkv, (GDN_HEADS * GDN_DK, 2 * GDN_HEADS * GDN_DK), axis=-1)
    qd = _l2_norm(dq.reshape(Bn, L, GDN_HEADS, GDN_DK)) * (GDN_DK ** -0.5)
    kd = _l2_norm(dk.reshape(Bn, L, GDN_HEADS, GDN_DK))
    vd = dv.reshape(Bn, L, GDN_HEADS, GDN_DV)
    g = -jnp.exp(p['gdn_a_log'][l].astype(jnp.float32)) * jax.nn.softplus(da + p['gdn_dt_bias'][l])
    beta = jax.nn.sigmoid(db)
    od, s_gdn_new = _gdn_chunked(qd, kd, vd, g, beta, s_gdn)
    y_gdn = (_rms_norm(od, p['g_gdn_norm'][l]) * jax.nn.silu(dg.reshape(Bn, L, GDN_HEADS, GDN_DV))).reshape(Bn, L, BRANCH_W)

    ys_all = jnp.stack([y_gla, y_ssd, y_gdn]).astype(h.dtype)
    yb = jnp.einsum('nblw,nwd->nbld', ys_all, p['w_branch'][l])
    gates = jax.nn.sigmoid(mg).reshape(Bn, L, N_BRANCH, D_MODEL).astype(h.dtype)
    merged = jnp.einsum('blnd,nbld->bld', gates, yb)
    out = jnp.einsum('bld,de->ble', merged, p['w_out'][l])
    return out, (s_gla_new, s_ssd_new, cv_ssd_new, s_gdn_new, cv_gdn_new)


def _layer(x, c, p, l, st):
    mod = jnp.einsum('bd,de->be', jax.nn.silu(c), p['w_ada'][l]) + p['b_ada'][l]
    sh_m, sc_m, gt_m, sh_f, sc_f, gt_f = jnp.split(mod[:, None, :], 6, axis=-1)
    h = _rms_norm(x, p['g_pre_mix'][l]) * (1.0 + sc_m) + sh_m
    mix, new_st = _token_mixers(h, p, l, st)
    x = x + gt_m * _rms_norm(mix, p['g_post_mix'][l])
    h = _rms_norm(x, p['g_pre_ffn'][l]) * (1.0 + sc_f) + sh_f
    a, b = jnp.split(jnp.einsum('bld,df->blf', h, p['w_ffn_in'][l]), 2, axis=-1)
    f = jnp.einsum('blf,fd->bld', jax.nn.silu(a) * b, p['w_ffn_out'][l])
    x = x + gt_f * _rms_norm(f, p['g_post_ffn'][l])
    return x, new_st


def _trunk(x, c, p, states):
    outs = ([], [], [], [], [])
    for l in range(DEPTH):
        x, ns = _layer(x, c, p, l, tuple(s[l] for s in states))
        for lst, s in zip(outs, ns):
            lst.append(s)
    return x, tuple(jnp.stack(lst) for lst in outs)


def _zero_states(n):
    f = jnp.float32
    return (jnp.zeros((DEPTH, n, GLA_HEADS, GLA_DK, GLA_DV), f),
            jnp.zeros((DEPTH, n, SSD_HEADS, SSD_HEADDIM, SSD_STATE), f),
            jnp.zeros((DEPTH, n, CONV_W - 1, SSD_CONV_CH), f),
            jnp.zeros((DEPTH, n, GDN_HEADS, GDN_DK, GDN_DV), f),
            jnp.zeros((DEPTH, n, CONV_W - 1, GDN_CONV_CH), f))


def setup_inputs(seed: int = 0) -> dict:
    key = jax.random.key(seed)
    kit = iter(jax.random.split(key, 64))
    f32 = jnp.float32
    D = D_MODEL

    def nrm(shape, scale):
        return jax.random.normal(next(kit), shape, f32) * scale

    def gain(shape):
        return 1.0 + nrm(shape, 0.05)

    def a_log(shape):
        return jnp.log(jax.random.uniform(next(kit), shape, f32, 1.0, 16.0))

    def dt_bias(shape):
        u = jax.random.uniform(next(kit), shape, f32)
        dt = jnp.exp(u * (math.log(0.1) - math.log(0.001)) + math.log(0.001))
        return dt + jnp.log(-jnp.expm1(-dt))

    return {
        'x_prompt': nrm((BATCH, SEQ, D), 1.0),
        'x_sample': nrm((DEC_BATCH, DEC_SEQ, D), 1.0),
        'state_gla': nrm((DEPTH, DEC_BATCH, GLA_HEADS, GLA_DK, GLA_DV), 0.5),
        'state_ssd': nrm((DEPTH, DEC_BATCH, SSD_HEADS, SSD_HEADDIM, SSD_STATE), 0.5),
        'cache_ssd_conv': nrm((DEPTH, DEC_BATCH, CONV_W - 1, SSD_CONV_CH), 1.0),
        'state_gdn': nrm((DEPTH, DEC_BATCH, GDN_HEADS, GDN_DK, GDN_DV), 0.5),
        'cache_gdn_conv': nrm((DEPTH, DEC_BATCH, CONV_W - 1, GDN_CONV_CH), 1.0),
        'c_prompt': nrm((BATCH, D), 1.0),
        'c_sample': nrm((DEC_BATCH, D), 1.0),
        'w_ada': nrm((DEPTH, D, 6 * D), 0.5 * D ** -0.5),
        'b_ada': nrm((DEPTH, 6 * D), 0.02),
        'g_pre_mix': gain((DEPTH, D)),
        'g_post_mix': gain((DEPTH, D)),
        'g_pre_ffn': gain((DEPTH, D)),
        'g_post_ffn': gain((DEPTH, D)),
        'w_in': nrm((DEPTH, D, N_IN), D ** -0.5),
        'w_gla_gate': nrm((DEPTH, GLA_RANK, GLA_HEADS * GLA_DK), GLA_RANK ** -0.5),
        'b_gla_gate': nrm((DEPTH, GLA_HEADS * GLA_DK), 0.1),
        'g_gla_norm': gain((DEPTH, GLA_DV)),
        'w_ssd_conv': nrm((DEPTH, CONV_W, SSD_CONV_CH), CONV_W ** -0.5),
        'b_ssd_conv': nrm((DEPTH, SSD_CONV_CH), 0.05),
        'ssd_dt_bias': dt_bias((DEPTH, SSD_HEADS)),
        'ssd_a_log': a_log((DEPTH, SSD_HEADS)),
        'ssd_d': gain((DEPTH, SSD_HEADS)),
        'g_ssd_norm': gain((DEPTH, SSD_INNER)),
        'w_gdn_conv': nrm((DEPTH, CONV_W, GDN_CONV_CH), CONV_W ** -0.5),
        'gdn_dt_bias': dt_bias((DEPTH, GDN_HEADS)),
        'gdn_a_log': a_log((DEPTH, GDN_HEADS)),
        'g_gdn_norm': gain((DEPTH, GDN_DV)),
        'w_branch': nrm((DEPTH, N_BRANCH, BRANCH_W, D), BRANCH_W ** -0.5),
        'w_out': nrm((DEPTH, D, D), D ** -0.5),
        'w_ffn_in': nrm((DEPTH, D, 2 * FFN_HIDDEN), D ** -0.5),
        'w_ffn_out': nrm((DEPTH, FFN_HIDDEN, D), FFN_HIDDEN ** -0.5),
    }


def reference(x_prompt, x_sample, state_gla, state_ssd, cache_ssd_conv, state_gdn, cache_gdn_conv,
              c_prompt, c_sample, w_ada, b_ada, g_pre_mix, g_post_mix, g_pre_ffn, g_post_ffn,
              w_in, w_gla_gate, b_gla_gate, g_gla_norm, w_ssd_conv, b_ssd_conv, ssd_dt_bias,
              ssd_a_log, ssd_d, g_ssd_norm, w_gdn_conv, gdn_dt_bias, gdn_a_log, g_gdn_norm,
              w_branch, w_out, w_ffn_in, w_ffn_out):
    p = dict(w_ada=w_ada, b_ada=b_ada, g_pre_mix=g_pre_mix, g_post_mix=g_post_mix,
             g_pre_ffn=g_pre_ffn, g_post_ffn=g_post_ffn, w_in=w_in, w_gla_gate=w_gla_gate,
             b_gla_gate=b_gla_gate, g_gla_norm=g_gla_norm, w_ssd_conv=w_ssd_conv,
             b_ssd_conv=b_ssd_conv, ssd_dt_bias=ssd_dt_bias, ssd_a_log=ssd_a_log, ssd_d=ssd_d,
             g_ssd_norm=g_ssd_norm, w_gdn_conv=w_gdn_conv, gdn_dt_bias=gdn_dt_bias,
             gdn_a_log=gdn_a_log, g_gdn_norm=g_gdn_norm, w_branch=w_branch, w_out=w_out,
             w_ffn_in=w_ffn_in, w_ffn_out=w_ffn_out)
    y_prompt, (pg, ps, pcs, pd, pcd) = _trunk(x_prompt, c_prompt, p, _zero_states(x_prompt.shape[0]))
    y_sample, (sg, ss, scs, sd, scd) = _trunk(
        x_sample, c_sample, p, (state_gla, state_ssd, cache_ssd_conv, state_gdn, cache_gdn_conv))
    return (y_prompt, y_sample, pg, ps, pcs, pd, pcd, sg, ss, scs, sd, scd)
```

```python
import math
from contextlib import ExitStack
import numpy as np
import concourse.bass as bass
import concourse.mybir as mybir
from concourse.bass_utils import run_bass_kernel_spmd

F32 = mybir.dt.float32
F32R = mybir.dt.float32r
AF = mybir.ActivationFunctionType
ALU = mybir.AluOpType
AX = mybir.AxisListType

ENGS = ("pe", "act", "dve", "pool", "sp")
EPS = 1e-6
NCORE = 8
SEQ = 2048
NSEQ = 16
LS = 4
D = 1024
NIN = 8736
O_GQ, O_GK, O_GV, O_GR, O_GLR = 0, 512, 1024, 1536, 2048
O_SZ, O_SXBC, O_SDT = 2064, 2576, 3600
O_DQKV, O_DA, O_DB, O_DG, O_MG = 3608, 5144, 5148, 5152, 5664
FH = 2816
RV_GLAN, RV_SDTB, RV_SALOG, RV_SD, RV_SNORM, RV_DDTB, RV_DALOG, RV_DNORM, RV_N = 0, 128, 136, 144, 152, 664, 668, 672, 800
M_UI, M_SL, M_SU, M_BLK, M_GUI, M_GSL, M_ID = 0, 1, 2, 3, 4, 5, 6
M_GSU, M_GBLK = 11, 12
M_SOFF = 7
M_IND = 13
NMASK = 14


class V:
    __slots__ = ("buf", "ap")

    def __init__(self, buf, ap):
        self.buf = buf
        self.ap = ap

    def __getitem__(self, k):
        return V(self.buf, self.ap[k])

    def f32(self):
        return V(self.buf, self.ap.bitcast(F32))

    def r(self):
        return V(self.buf, self.ap.bitcast(F32R))

    def bc(self, shape):
        return V(self.buf, self.ap.broadcast_to(list(shape)))

    def un(self, axis):
        return V(self.buf, self.ap.unsqueeze(axis))

    def re(self, pat, **kw):
        return V(self.buf, self.ap.rearrange(pat, **kw))

    @property
    def shape(self):
        return self.ap.shape


class Buf:
    __slots__ = ("base", "name", "w", "r", "dsem", "dcnt", "rr", "fam", "dkey")

    def __init__(self, base_ap, name, rr=False):
        self.base = base_ap
        self.name = name
        self.rr = rr
        self.fam = []
        self.dkey = None
        self.w = None
        self.r = {}
        self.dsem = None
        self.dcnt = 0

    def __getitem__(self, k):
        return V(self, self.base[k])

    def v(self):
        return V(self, self.base)


class KSplit:
    def __init__(self, sched, buf):
        self.buf = buf
        self.ch = [sched.sub(buf, buf.base[:, k, :], "%s_k%d" % (buf.name, k)) for k in range(buf.base.shape[1])]

    def __getitem__(self, key):
        if isinstance(key, tuple) and len(key) == 3 and isinstance(key[1], int):
            return self.ch[key[1]][key[0], key[2]]
        return self.buf[key]

    def v(self):
        return self.buf.v()


class Ins:
    __slots__ = ("eng", "fn", "deps", "need", "val", "dma", "dbuf", "dval", "pos", "dk")

    def __init__(self, eng, fn):
        self.eng = eng
        self.fn = fn
        self.deps = []
        self.need = False
        self.val = 0
        self.dma = False
        self.dbuf = None
        self.dval = 0
        self.pos = 0


class Sched:
    def __init__(self, nc, es):
        self.nc = nc
        self.es = es
        self.q = {e: [] for e in ENGS}
        self.order = []
        self.nalias = 0
        self.dcnt = {}
        self.zsrc = self.sb("zsrc", [128, 1])
        self.memset(self.zsrc.v(), 0.0)

    def sb(self, name, shape, rr=False):
        t = self.es.enter_context(self.nc.sbuf_tensor(name, list(shape), F32))
        return Buf(t[:], name, rr)

    def psum(self, name, shape, dt=F32):
        t = self.es.enter_context(self.nc.psum_tensor(name, list(shape), dt))
        return Buf(t[:], name)

    def sub(self, parent, ap, name):
        b = Buf(ap, name, parent.rr)
        b.fam = [parent]
        parent.fam.append(b)
        return b

    def region(self, ap, name, parents=(), rr=False):
        b = Buf(ap, name, rr)
        for p0 in parents:
            for p in [p0] + p0.fam:
                if p.w is not None:
                    self.nalias += 1
                    b.r[("a", self.nalias)] = p.w
                for x in p.r.values():
                    self.nalias += 1
                    b.r[("a", self.nalias)] = x
        return b

    def _add(self, eng, fn, reads, writes, dma=False, dbuf=None):
        ins = Ins(eng, fn)
        ins.dma = dma
        deps = []
        for b0 in reads:
            for b in [b0] + b0.fam:
                if b.w is not None:
                    deps.append(b.w)
        for b0 in writes:
            for b in [b0] + b0.fam:
                if b.w is not None:
                    deps.append(b.w)
                deps.extend(b.r.values())
        seen = set()
        for d in deps:
            if id(d) not in seen and d is not ins:
                seen.add(id(d))
                ins.deps.append(d)
                d.need = True
        if dma:
            ins.dbuf = dbuf
            ins.dk = dbuf.dkey if dbuf.dkey is not None else id(dbuf)
            self.dcnt[ins.dk] = self.dcnt.get(ins.dk, 0) + 16
            ins.dval = self.dcnt[ins.dk]
        for b in reads:
            b.r[("d", id(ins)) if dma else eng] = ins
        for b in writes:
            b.w = ins
            b.r = {}
        ins.pos = len(self.order)
        self.q[eng].append(ins)
        self.order.append(ins)
        return ins

    @staticmethod
    def _bufs(vs):
        out = []
        for v in vs:
            if isinstance(v, V) and v.buf not in out:
                out.append(v.buf)
        return out

    def op(self, eng, fn, outs, ins):
        return self._add(eng, fn, self._bufs(ins), self._bufs(outs))

    @staticmethod
    def _o(v):
        return v.ap.bitcast(F32R) if v.buf.rr else v.ap

    def mm(self, out, lhsT, rhs, start=True, stop=True, tp=None):
        o, a, b = out.ap, lhsT.ap.bitcast(F32R), rhs.ap.bitcast(F32R)
        assert lhsT.buf.rr and rhs.buf.rr, (lhsT.buf.name, rhs.buf.name)
        if tp is None:
            return self.op("pe", lambda e: e.matmul(o, a, b, start=start, stop=stop), [out], [lhsT, rhs])
        return self.op("pe", lambda e: e.matmul(o, a, b, start=start, stop=stop, tile_position=tp), [out], [lhsT, rhs])

    def mm32(self, out, lhsT, rhs, start=True, stop=True, tp=None):
        o, a, b = out.ap, lhsT.ap, rhs.ap
        if tp is None:
            return self.op("pe", lambda e: e.matmul(o, a, b, start=start, stop=stop), [out], [lhsT, rhs])
        return self.op("pe", lambda e: e.matmul(o, a, b, start=start, stop=stop, tile_position=tp), [out], [lhsT, rhs])

    def act(self, out, in_, func, bias=0.0, scale=1.0, accum=None):
        o, i = self._o(out), in_.ap
        b = bias.ap if isinstance(bias, V) else bias
        s = scale.ap if isinstance(scale, V) else scale
        ac = accum.ap if accum is not None else None
        outs = [out] + ([accum] if accum is not None else [])
        return self.op("act", lambda e: e.activation(o, i, func, bias=b, scale=s, accum_out=ac),
                       outs, [in_, bias, scale])

    def tt(self, out, a, b, op, eng="dve"):
        o, x, y = self._o(out), a.ap, b.ap
        return self.op(eng, lambda e: e.tensor_tensor(o, x, y, op), [out], [a, b])

    def ts(self, out, a, s1, op0, s2=None, op1=None, eng="dve"):
        o, x = self._o(out), a.ap
        p1 = s1.ap if isinstance(s1, V) else s1
        p2 = s2.ap if isinstance(s2, V) else s2
        if op1 is None:
            return self.op(eng, lambda e: e.tensor_scalar(o, x, p1, None, op0), [out], [a, s1])
        return self.op(eng, lambda e: e.tensor_scalar(o, x, p1, p2, op0, op1), [out], [a, s1, s2])

    def stt(self, out, a, sc, b, op0, op1):
        o, x, y = self._o(out), a.ap, b.ap
        p = sc.ap if isinstance(sc, V) else sc
        return self.op("dve", lambda e: e.scalar_tensor_tensor(o, x, p, y, op0, op1), [out], [a, sc, b])

    def red(self, out, in_, op=None):
        o, i = self._o(out), in_.ap
        op = op or ALU.add
        return self.op("dve", lambda e: e.tensor_reduce(o, i, AX.X, op), [out], [in_])

    def cp(self, out, in_, eng="dve"):
        o, i = self._o(out), in_.ap
        if eng == "act":
            return self.op("act", lambda e: e.copy(o, i), [out], [in_])
        return self.op(eng, lambda e: e.tensor_copy(o, i), [out], [in_])

    def recip(self, out, in_):
        o, i = self._o(out), in_.ap
        return self.op("dve", lambda e: e.reciprocal(o, i), [out], [in_])

    def memset(self, out, val, eng="dve"):
        if out.buf.rr:
            z = self.zsrc
            shp = list(out.ap.shape)
            zin = bass.AP(z.base.tensor, z.base.offset, [[z.base.ap[0][0], shp[0]]] + [[0, n] for n in shp[1:]])
            return self.ts(out, V(z, zin), float(val), ALU.add)
        o = out.ap
        return self.op(eng, lambda e: e.memset(o, val), [out], [])

    def dma_in(self, eng, out, src_ap, **kw):
        o = self._o(out)
        if out.buf.rr:
            eng = "pool"
        return self._add(eng, lambda e: e.dma_start(out=o, in_=src_ap, **kw), [], [out.buf], dma=True, dbuf=out.buf)

    def dma_out(self, eng, dst_ap, in_, **kw):
        i = in_.ap
        return self._add(eng, lambda e: e.dma_start(out=dst_ap, in_=i, **kw), [in_.buf], [], dma=True, dbuf=in_.buf)

    def emit(self):
        nc, es = self.nc, self.es
        sems = {e: es.enter_context(nc.semaphore("s_" + e)) for e in ENGS}
        dsems = {}
        for ins in self.order:
            if ins.dma and ins.dk not in dsems:
                dsems[ins.dk] = es.enter_context(nc.semaphore("d%d" % len(dsems)))
        for e in ENGS:
            c = 0
            for ins in self.q[e]:
                if not ins.dma and ins.need:
                    c += 1
                    ins.val = c
        hist = {}
        for ins in self.order:
            if ins.dma:
                hist.setdefault(ins.dk, []).append((ins.pos, ins.dval))
        engobj = {"pe": nc.tensor, "act": nc.scalar, "dve": nc.vector, "pool": nc.gpsimd, "sp": nc.sync}
        block = es.enter_context(nc.Block())

        def run(e):
            eo = engobj[e]
            waited = {}
            for ins in self.q[e]:
                need = {}
                for d in ins.deps:
                    if d.dma:
                        hl = hist[d.dk]
                        lo, hi = 0, len(hl)
                        while lo < hi:
                            mid = (lo + hi) // 2
                            if hl[mid][0] < ins.pos:
                                lo = mid + 1
                            else:
                                hi = mid
                        v = hl[lo - 1][1] if lo > 0 else 0
                        key = ("d", d.dk)
                        sem = dsems[d.dk]
                    else:
                        if d.eng == e and e == "pe":
                            continue
                        v = d.val
                        key = d.eng
                        sem = sems[d.eng]
                    if waited.get(key, 0) >= v:
                        continue
                    if key not in need or need[key][1] < v:
                        need[key] = (sem, v)
                for key, (sem, v) in need.items():
                    eo.wait_ge(sem, v)
                    waited[key] = v
                bi = ins.fn(eo)
                if ins.dma:
                    bi.then_inc(dsems[ins.dk], 16)
                elif ins.need:
                    bi.then_inc(sems[e], 1)
            if e == "sp":
                for k, sem in dsems.items():
                    eo.wait_ge(sem, self.dcnt[k])

        @block.tensor
        def _(x):
            run("pe")

        @block.scalar
        def _(x):
            run("act")

        @block.vector
        def _(x):
            run("dve")

        @block.gpsimd
        def _(x):
            run("pool")

        @block.sync
        def _(x):
            run("sp")


def _masks():
    r = np.arange(128)[:, None]
    c = np.arange(128)[None, :]
    m = np.zeros((NMASK, 128, 128), np.float32)
    m[M_UI] = (r <= c)
    m[M_SL] = (r > c)
    m[M_SU] = (r < c)
    m[M_BLK] = 1.0
    blk64 = ((r // 64) == (c // 64))
    m[M_GUI] = m[M_UI] * blk64
    m[M_GSL] = m[M_SL] * blk64
    m[M_GSU] = m[M_SU] * blk64
    m[M_GBLK] = blk64
    m[M_ID] = (r == c)
    same = ((r // LS) == (c // LS)) & (r < 64) & (c < 64)
    for k in (M_UI, M_SL, M_SU, M_BLK):
        m[M_SOFF + k] = m[k] * same
    m[M_IND] = ((r // LS) == c) & (c < NSEQ) & (r < 64)
    return m


class Pass:
    def __init__(self, samp, idx, npass):
        self.samp = samp
        self.idx = idx
        self.T = 64 if samp else 512
        self.C = 64 if samp else 128
        self.NCH = 1 if samp else 4
        self.moff = M_SOFF if samp else 0
        self.first = (idx == 0)
        self.last = samp or idx == npass - 1


def build(nlayers=2, npass=4, do_sample=True, stages=("gla", "ssd", "gdn", "ffn"), dbg=None):
    nc = bass.Bass("TRN2", target_bir_lowering=False)

    def din(name, shape):
        return nc.dram_tensor(name, list(shape), F32, kind="ExternalInput").ap()

    def dout(name, shape):
        return nc.dram_tensor(name, list(shape), F32, kind="ExternalOutput").ap()

    xTp = din("xTp", [D, SEQ])
    xTs = din("xTs", [D, 64])
    cT = din("cT", [D, 17])
    st_gla = din("st_gla", [2, NSEQ, 4, 128, 128])
    st_ssd = din("st_ssd", [2, NSEQ, 8, 64, 128])
    cv_ssd = din("cv_ssd", [2, NSEQ, 3, 1024])
    st_gdn = din("st_gdn", [2, NSEQ, 4, 128, 128])
    cv_gdn = din("cv_gdn", [2, NSEQ, 3, 1536])
    w_ada = din("w_ada", [2, D, 6144])
    b_adaT = din("b_adaT", [2, 128, 48])
    gainsT = din("gainsT", [2, 128, 4, 8])
    w_in = din("w_in", [2, D, NIN])
    w_gate = din("w_gate", [2, 16, 512])
    b_gate = din("b_gate", [2, 1, 512])
    rowvec = din("rowvec", [2, 1, RV_N])
    w_sconvT = din("w_sconvT", [2, 128, 8, 4])
    b_sconvT = din("b_sconvT", [2, 128, 8])
    w_dconvT = din("w_dconvT", [2, 128, 12, 4])
    w_branch = din("w_branch", [2, 3, 512, D])
    w_out = din("w_out", [2, D, D])
    w_ffn_in = din("w_ffn_in", [2, D, 2 * FH])
    w_ffn_out = din("w_ffn_out", [2, FH, D])
    cst = din("cst", [NMASK, 128, 128])

    yTp = dout("yTp", [D, SEQ])
    yTs = dout("yTs", [D, 64])
    o_pg = dout("o_pg", [2, 4, 128, 128])
    o_ps = dout("o_ps", [2, 8, 64, 128])
    o_pcs = dout("o_pcs", [2, 3, 1024])
    o_pd = dout("o_pd", [2, 4, 128, 128])
    o_pcd = dout("o_pcd", [2, 3, 1536])
    o_sg = dout("o_sg", [2, NSEQ, 4, 128, 128])
    o_ss = dout("o_ss", [2, NSEQ, 8, 64, 128])
    o_scs = dout("o_scs", [2, NSEQ, 3, 1024])
    o_sd = dout("o_sd", [2, NSEQ, 4, 128, 128])
    o_scd = dout("o_scd", [2, NSEQ, 3, 1536])
    dbg_out = {}
    if dbg:
        for name, shape in dbg.items():
            dbg_out[name] = dout("dbg_" + name, shape)

    es = ExitStack()
    S = Sched(nc, es)
    es.enter_context(nc.allow_low_precision(reason="fp32r-rounded PE operands"))

    def dump(name, v):
        if name in dbg_out:
            S.dma_out("sp", dbg_out[name], v)

    msk = S.sb("msk", [128, NMASK, 128], True)
    S.dma_in("pool", msk.v(), cst.rearrange("m p c -> p m c"))
    xT = KSplit(S, S.sb("xT", [128, 8, 512]))
    hT = KSplit(S, S.sb("hT", [128, 8, 512], True))
    NSLOT = 23
    arena_t = es.enter_context(nc.sbuf_tensor("arena", [128, NSLOT, 512], F32))
    wbufs = [S.sb("wb%d" % i, [128, 8, 256], True) for i in range(3)]
    whalf = []
    for i in range(3):
        for k in range(2):
            whalf.append(S.sub(wbufs[i], wbufs[i].base[:, :, k * 128:(k + 1) * 128], "wh%d_%d" % (i, k)))
    banks = [S.psum("pb%d" % i, [128, 512], F32) for i in range(8)]
    dmod = [S.sb("dmod%d" % l, [128, 6, 8, 17]) for l in range(2)]
    gains = S.sb("gains", [128, 2, 4, 8])
    badd = S.sb("badd", [128, 2, 48])
    rv = [S.sb("rv%d" % l, [128, RV_N]) for l in range(2)]
    negA = S.sb("negA", [128, 2, 12])
    wsc = S.sb("wsc", [128, 2, 8, 4])
    bsc = S.sb("bsc", [128, 2, 8])
    wdc = S.sb("wdc", [128, 2, 12, 4])
    wga = S.sb("wga", [17, 512], True)
    glrA = S.sb("glrA", [17, 512], True)
    sT = S.sb("sT", [128, 8, 18], True)
    S_gla = [S.sb("S_gla%d" % l, [128, 4, 128], True) for l in range(2)]
    S_ssd = [S.sb("S_ssd%d" % l, [128, 8, 64], True) for l in range(2)]
    S_gdn = [S.sb("S_gdn%d" % l, [128, 4, 128], True) for l in range(2)]
    hist_s = [S.sb("hist_s%d" % l, [128, 8, 3]) for l in range(2)]
    hist_d = [S.sb("hist_d%d" % l, [128, 12, 3]) for l in range(2)]
    mergedT = KSplit(S, S.sb("mergedT", [128, 8, 512], True))
    yT = S.sb("yT", [128, 4, 512], True)
    scr = [S.sb("scr%d" % i, [128, 512], True) for i in range(7)]
    pl = [S.sb("pl%d" % i, [128, 512]) for i in range(5)]
    big = [S.sb("big%d" % i, [128, 8, 128], i != 1) for i in range(3)]
    hist_in = V(big[1], big[1].base.rearrange("p a b -> p (a b)")[:, 0:576].rearrange("p (t b j) -> p t b j", b=NSEQ, j=3))
    modt = V(big[1], big[1].base.rearrange("p a b -> p (a b)")[:, 0:816].rearrange("p (j n) -> p j n", n=17))
    sm = S.sb("sm", [128, 128])
    padz = S.sb("padz", [128, NSEQ, 64], True)
    zt = S.sb("zt", [128, 64], True)
    elast_all = S.sb("elast_all", [128, NSEQ, 8])
    gz = S.sb("gz", [64, NSEQ, 8], True)
    ecol = S.sb("ecol", [128, 4, NSEQ])
    smallT = S.sb("smallT", [128, 32])
    smr = S.sb("smr", [128, 16], True)
    sm2 = S.sb("sm2", [128, 16])
    sm_b = S.sb("sm_b", [128, 128])
    smr_b = S.sb("smr_b", [128, 16], True)
    yT_h = [S.sub(yT, yT.base[:, 2 * k:2 * k + 2, :], "yT_h%d" % k) for k in range(2)]

    gsub = {}
    tails = []
    state = {"arsem": 0, "ps": 0, "wb": 0, "arena": [], "ev": 0, "pad": 0, "pinned": set()}

    def PS(pin=False):
        while (state["ps"] % 8) in state["pinned"]:
            state["ps"] += 1
        k = state["ps"] % 8
        state["ps"] += 1
        if pin:
            state["pinned"].add(k)
        return banks[k]

    def unpin(b):
        state["pinned"].discard(banks.index(b))

    def carve(name, s0, n):
        assert s0 + n <= NSLOT, (name, s0, n)
        parents = [b for (b, a0, a1) in state["arena"] if a0 < s0 + n and s0 < a1]
        state["arena"] = [(b, a0, a1) for (b, a0, a1) in state["arena"] if not (a0 < s0 + n and s0 < a1)]
        b = S.region(arena_t[:, s0:s0 + n, :], name, parents, True)
        b.dkey = ("ar", state["arsem"] % 24)
        state["arsem"] += 1
        state["arena"].append((b, s0, s0 + n))
        return b

    class Bump:
        def __init__(self):
            self.n = 0

        def fm(self, name, ntile, pas):
            if pas.samp:
                ns = (ntile * 64 + 511) // 512
                b = carve(name, self.n, ns)
                self.n += ns
                v = V(b, b.base.rearrange("p s c -> p (s c)")[:, 0:ntile * 64].rearrange("p (n t) -> p n t", t=64))
                return v
            b = carve(name, self.n, ntile)
            self.n += ntile
            return b.v()

        def raw(self, name, n):
            b = carve(name, self.n, n)
            self.n += n
            return b

        def tm(self, name, n=1):
            out = []
            for i in range(n):
                b = carve("%s%d" % (name, i), self.n, 1)
                self.n += 1
                out.append(b[:, 0, :])
            return out

    def M(pas, idx, rows=None, cols=None):
        C = pas.C
        k = idx if idx in (M_ID, M_IND) else pas.moff + idx
        return msk[0:(rows or C), k, 0:(cols or C)]

    def wload(src3, nk, ncols):
        if ncols <= 128:
            b = whalf[state["wb"] % 6]
            state["wb"] += 1
        else:
            if state["wb"] % 2:
                state["wb"] += 1
            b = wbufs[(state["wb"] % 6) // 2]
            state["wb"] += 2
        S.dma_in("pool", b[:, 0:nk, 0:ncols], src3)
        return b

    def w3(w2d):
        return w2d.rearrange("(kc p) n -> p kc n", p=128)

    def run_gens(gens):
        gens = list(gens)
        while gens:
            for g_ in list(gens):
                try:
                    next(g_)
                except StopIteration:
                    gens.remove(g_)

    def evac(out, in_):
        state["ev"] += 1
        S.cp(out, in_, "act" if state["ev"] % 2 else "dve")

    def proj_fm(w2d, nk, ncols, rhs_fn, T, sink):
        wb = wload(w3(w2d), nk, ncols)
        for j in range((ncols + 127) // 128):
            m = min(128, ncols - j * 128)
            ps = PS()
            for kc in range(nk):
                S.mm(ps[0:m, 0:T], wb[:, kc, j * 128:j * 128 + m], rhs_fn(kc), kc == 0, kc == nk - 1)
            sink(j, ps[0:m, 0:T], m)

    def proj_fm_wide(w2d, ncols, T, sink):
        for b0 in range(0, ncols, 256):
            nb = min(256, ncols - b0)
            proj_fm(w2d[:, b0:b0 + nb], 8, nb, lambda kc: hT[:, kc, 0:T], T,
                    lambda j, ps, m, b0=b0: sink(b0 // 128 + j, ps))

    def proj_tm(w2d, ncols, pas, sink):
        wb = wload(w3(w2d), 8, ncols)
        for c in range(pas.NCH):
            ps = PS()
            for kc in range(8):
                S.mm(ps[0:pas.C, 0:ncols], hT[:, kc, c * pas.C:(c + 1) * pas.C], wb[:, kc, 0:ncols], kc == 0, kc == 7)
            sink(c, ps[0:pas.C, 0:ncols])

    def proj_tm_wide(w2d, ncols, pas, dst, rnd):
        for b0 in range(0, ncols, 256):
            nb = min(256, ncols - b0)

            def sink(c, ps, b0=b0, nb=nb):
                o = dst[c][0:pas.C, b0:b0 + nb]
                evac(o if rnd else o, ps)
            proj_tm(w2d[:, b0:b0 + nb], nb, pas, sink)

    S.dma_in("sp", gains.v(), gainsT.rearrange("l p w k -> p l w k"))
    S.dma_in("sp", badd.v(), b_adaT.rearrange("l p j -> p l j"))
    S.dma_in("sp", wsc.v(), w_sconvT.rearrange("l p t j -> p l t j"))
    S.dma_in("sp", bsc.v(), b_sconvT.rearrange("l p t -> p l t"))
    S.dma_in("sp", wdc.v(), w_dconvT.rearrange("l p t j -> p l t j"))
    S.memset(glrA.v(), 1.0)
    S.memset(padz.v(), 0.0)
    S.memset(zt.v(), 0.0)
    for l in range(2):
        S.dma_in("sp", rv[l].v(), rowvec[l].partition_broadcast(128).rearrange("p o n -> p (o n)"))
        S.act(negA[:, l, 0:8], rv[l][:, RV_SALOG:RV_SALOG + 8], AF.Exp)
        S.act(negA[:, l, 8:12], rv[l][:, RV_DALOG:RV_DALOG + 4], AF.Exp)
        S.memset(hist_s[l].v(), 0.0)
        S.memset(hist_d[l].v(), 0.0)
        S.memset(S_gla[l].v(), 0.0)
        S.memset(S_ssd[l].v(), 0.0)
        S.memset(S_gdn[l].v(), 0.0)
    S.ts(negA.v(), negA.v(), -1.0, ALU.mult)
    S.ts(wsc.v(), wsc.v(), 0.5, ALU.mult)
    S.ts(bsc.v(), bsc.v(), 0.5, ALU.mult)
    S.ts(wdc.v(), wdc.v(), 0.5, ALU.mult)
    cin = pl[4][:, 0:136]
    ce = pl[1][:, 0:136]
    S.dma_in("sp", cin.re("p (k n) -> p k n", n=17), cT.rearrange("(k p) n -> p k n", p=128))
    S.act(ce, cin, AF.Exp, scale=-1.0)
    S.ts(ce, ce, 1.0, ALU.add)
    S.recip(ce, ce)
    S.memset(sT.v(), 0.0)
    S.tt(sT[:, :, 0:17], cin.re("p (k n) -> p k n", n=17), ce.re("p (k n) -> p k n", n=17), ALU.mult)
    for l in range(nlayers):
        ps = None
        for jb in range(24):
            wb = wload(w3(w_ada[l][:, jb * 256:(jb + 1) * 256]), 8, 256)
            for jj in range(2):
                j = jb * 2 + jj
                if j % 24 == 0:
                    ps = PS()
                for kc in range(8):
                    S.mm(ps[:, (j % 24) * 18:(j % 24) * 18 + 18], wb[:, kc, jj * 128:(jj + 1) * 128], sT[:, kc, :], kc == 0, kc == 7)
                if j % 24 == 23:
                    j0 = j - 23
                    S.tt(modt[:, j0:j0 + 24, :], ps[:, 0:432].re("p (j n) -> p j n", n=18)[:, :, 0:17],
                         badd[:, l, j0:j0 + 24].un(2).bc([128, 24, 17]), ALU.add)
        dm = dmod[l]
        for (which, sc0, sh0, gt0, gpre, gpost) in ((0, 8, 0, 16, 0, 1), (3, 32, 24, 40, 2, 3)):
            S.ts(dm[:, which, :, :], modt[:, sc0:sc0 + 8, :], 1.0, ALU.add)
            S.tt(dm[:, which, :, :], dm[:, which, :, :], gains[:, l, gpre, :].un(2).bc([128, 8, 17]), ALU.mult)
            S.cp(dm[:, which + 1, :, :], modt[:, sh0:sh0 + 8, :])
            S.tt(dm[:, which + 2, :, :], modt[:, gt0:gt0 + 8, :], gains[:, l, gpost, :].un(2).bc([128, 8, 17]), ALU.mult)
    dump("dmod0", dmod[0].v())

    def rms_fm(src_fn, T):
        ps = PS()
        for kc in range(8):
            sq = scr[kc % 2]
            S.act(sq[:, 0:T], src_fn(kc), AF.Square)
            S.mm(ps[:, 0:T], msk[:, M_BLK, :], sq[:, 0:T], kc == 0, kc == 7)
        S.act(pl[1][:, 0:T], ps[:, 0:T], AF.Ln, bias=EPS, scale=1.0 / D)
        S.act(pl[0][:, 0:T], pl[1][:, 0:T], AF.Exp, scale=-0.5)
        return pl[0][:, 0:T]

    def s3(v):
        return v.re("p (b j) -> p b j", j=LS)

    def modcol(l, which, kc, pas):
        if not pas.samp:
            return dmod[l][:, which, kc, 0:1]
        return dmod[l][:, which, kc, 1:17].un(2).bc([128, NSEQ, LS])

    def prenorm(l, which, pas):
        T = pas.T
        rstd = rms_fm(lambda kc: xT[:, kc, 0:T], T)
        for kc in range(8):
            t = pl[2 + kc % 3][:, 0:T]
            S.tt(t, xT[:, kc, 0:T], rstd, ALU.mult)
            if not pas.samp:
                S.act(hT[:, kc, 0:T], t, AF.Identity, bias=modcol(l, which + 1, kc, pas), scale=modcol(l, which, kc, pas))
            else:
                S.tt(s3(t), s3(t), modcol(l, which, kc, pas), ALU.mult)
                S.tt(s3(hT[:, kc, 0:T]), s3(t), modcol(l, which + 1, kc, pas), ALU.add)

    def postnorm_residual(l, which, pas, src):
        T = pas.T
        rstd = rms_fm(lambda kc: src[:, kc, 0:T], T)
        for kc in range(8):
            t = pl[2 + kc % 3][:, 0:T]
            S.tt(t, src[:, kc, 0:T], rstd, ALU.mult)
            if not pas.samp:
                S.stt(xT[:, kc, 0:T], t, modcol(l, which + 2, kc, pas), xT[:, kc, 0:T], ALU.mult, ALU.add)
            else:
                S.tt(s3(t), s3(t), modcol(l, which + 2, kc, pas), ALU.mult)
                S.tt(xT[:, kc, 0:T], xT[:, kc, 0:T], t, ALU.add)

    def sigmoid_from(out, src, scratch):
        S.act(scratch, src, AF.Tanh, scale=0.5)
        S.ts(out, scratch, 0.5, ALU.mult, 0.5, ALU.add)

    def silu_inplace(x, scratch, rnd=False):
        S.act(scratch, x, AF.Tanh, scale=0.5)
        S.stt(scratch, scratch, 1.0, x, ALU.add, ALU.mult)
        S.ts(x, scratch, 0.5, ALU.mult)

    def transposes(src, C, nblk, width=128):
        ps = PS()
        for j in range(nblk):
            S.mm(ps[0:width, j * C:(j + 1) * C], src[0:C, j * width:(j + 1) * width], msk[0:C, M_ID, 0:C], True, True)
        return ps[0:width, 0:nblk * C].re("p (j c) -> p j c", c=C)

    def branch_merge(l, n, pas, first):
        T = pas.T
        for e in range(8):
            wbg = wload(w3(w_in[l][:, O_MG + n * 1024 + e * 128:O_MG + n * 1024 + (e + 1) * 128]), 8, 128)
            wbb = wload(w3(w_branch[l, n][:, e * 128:(e + 1) * 128]), 4, 128)
            pg = PS()
            for kc in range(8):
                S.mm(pg[:, 0:T], wbg[:, kc, :], hT[:, kc, 0:T], kc == 0, kc == 7)
            py = PS()
            for wc in range(4):
                S.mm(py[:, 0:T], wbb[:, wc, :], yT[:, wc, 0:T], wc == 0, wc == 3)
            sg = pl[2 + e % 3][:, 0:T]
            S.act(sg, pg[:, 0:T], AF.Tanh, scale=0.5)
            S.stt(sg, sg, 1.0, py[:, 0:T], ALU.add, ALU.mult)
            if first:
                S.ts(mergedT[:, e, 0:T], sg, 0.5, ALU.mult)
            else:
                S.stt(mergedT[:, e, 0:T], sg, 0.5, mergedT[:, e, 0:T], ALU.mult, ALU.add)
            yield

    def gate_norm_out(pas, o_in, gate_tm, gn_bc, ngrp, gw, ycur):
        C = pas.C
        c1 = pl[0][0:C, :]
        c2 = pl[3][0:C, :]
        st1 = sm[0:C, 64:64 + ngrp]
        st2 = sm[0:C, 96:96 + ngrp]
        S.act(c1, o_in, AF.Square)
        S.red(st1, c1.re("p (g w) -> p g w", w=gw))
        S.act(st2, st1, AF.Ln, bias=EPS, scale=1.0 / gw)
        S.act(st1, st2, AF.Exp, scale=-0.5)
        S.tt(c1.re("p (g w) -> p g w", w=gw), o_in.re("p (g w) -> p g w", w=gw), st1.un(2).bc([C, ngrp, gw]), ALU.mult)
        S.tt(c1.re("p (g w) -> p g w", w=gn_bc.shape[-1]), c1.re("p (g w) -> p g w", w=gn_bc.shape[-1]),
             gn_bc.un(1).bc([C, 512 // gn_bc.shape[-1], gn_bc.shape[-1]]), ALU.mult)
        S.act(c2, gate_tm, AF.Tanh, scale=0.5)
        S.stt(c2, c2, 1.0, gate_tm, ALU.add, ALU.mult)
        S.stt(ycur, c2, 0.5, c1, ALU.mult, ALU.mult)

    def y_to_yT(pas, c, ycur):
        C = pas.C
        ps = transposes(ycur, C, 4)
        evac(yT[:, 0:4, c * C:(c + 1) * C], ps)

    def build_pad(src3):
        return build_pad_into(padz, padz.base, src3)

    def build_pad_into(buf, base, src3):
        dst = bass.AP(base.tensor, base.offset, [list(base.ap[0]), [64 + LS, NSEQ], [1, LS]])
        S.cp(V(buf, dst), src3)
        return V(buf, bass.AP(base.tensor, base.offset, [list(base.ap[0]), [64, NSEQ], [1, 64]]))

    def cum3(pas, G, n, ex):
        C = pas.C
        pc = PS()
        S.mm(pc[0:C, 0:n], M(pas, M_UI), G, True, True)
        S.mm(pc[0:C, n:2 * n], M(pas, M_SL), G, True, True)
        S.mm(pc[0:C, 2 * n:3 * n], M(pas, M_BLK), G, True, True)
        S.act(ex[0:C, 0:3 * n], pc[0:C, 0:3 * n], AF.Exp)

    def seq_totals(G, n):
        S.tt(gz[:, :, 0:n], msk[0:64, M_IND, 0:NSEQ].un(2).bc([64, NSEQ, n]), G.un(1).bc([64, NSEQ, n]), ALU.mult)
        pe_ = PS()
        S.mm(pe_[:, 0:NSEQ * n].re("p (i n) -> p i n", n=n), msk[0:64, M_BLK, :], gz[:, :, 0:n], True, True)
        S.act(elast_all[:, :, 0:n], pe_[:, 0:NSEQ * n].re("p (i n) -> p i n", n=n), AF.Exp)

    def conv_fm(pas, l, ps, raw, out, wcol, bcol, hist_v, scratch):
        T = pas.T
        S.cp(raw, ps, "act")
        S.act(out, ps, AF.Identity, bias=(bcol if bcol is not None else 0.0), scale=wcol(3))
        yield
        if not pas.samp:
            for s_ in (1, 2, 3):
                S.stt(out[:, s_:T], raw[:, 0:T - s_], wcol(3 - s_), out[:, s_:T], ALU.mult, ALU.add)
                S.stt(out[:, 0:s_], hist_v[:, 3 - s_:3], wcol(3 - s_), out[:, 0:s_], ALU.mult, ALU.add)
        else:
            r3, o3 = s3(raw), s3(out)
            for s_ in (1, 2, 3):
                S.stt(o3[:, :, s_:LS], r3[:, :, 0:LS - s_], wcol(3 - s_), o3[:, :, s_:LS], ALU.mult, ALU.add)
                S.stt(o3[:, :, 0:s_], hist_v[:, :, 3 - s_:3], wcol(3 - s_), o3[:, :, 0:s_], ALU.mult, ALU.add)
        yield
        S.act(scratch, out, AF.Tanh)
        S.stt(out, scratch, 1.0, out, ALU.add, ALU.mult)

    def load_hist_sample(bp, cv_l, ntile, name):
        n0 = bp.n
        nsl = (ntile * 128 + 511) // 512
        cvb_ = bp.raw(name, nsl)
        bp.n = n0
        cv2 = V(cvb_, cvb_.base.rearrange("p s c -> p (s c)"))
        S.dma_in("pool", cv2[0:48, 0:ntile * 128], cv_l.rearrange("b j c -> (b j) c"))
        for t0 in range(0, ntile, 4):
            ps = PS()
            for k in range(4):
                t = t0 + k
                S.mm(ps[:, k * 48:(k + 1) * 48], cv2[0:48, t * 128:(t + 1) * 128], msk[0:48, M_ID, 0:48], True, True)
            evac(hist_in[:, t0:t0 + 4, :, :].re("p t b j -> p t (b j)"), ps[:, 0:192].re("p (t n) -> p t n", n=48))

    def proj_conv(w2d, ncols, pas, sink, cache_out):
        T = pas.T
        win = []
        for bi, b0 in enumerate(range(0, ncols, 256)):
            wb = wload(w3(w2d[:, b0:b0 + 256]), 8, 256)
            for j in range(2):
                ps = PS()
                for kc in range(8):
                    S.mm(ps[:, 0:T], wb[:, kc, j * 128:(j + 1) * 128], hT[:, kc, 0:T], kc == 0, kc == 7)
                win.insert(0, sink(b0 // 128 + j, ps[:, 0:T]))
                for g_ in list(win):
                    try:
                        next(g_)
                    except StopIteration:
                        win.remove(g_)
            if pas.samp:
                ps2 = PS()
                for kc in range(8):
                    S.mm(ps2[0:64, 0:256], hT[:, kc, 0:64], wb[:, kc, 0:256], kc == 0, kc == 7)
                stg = scr[2 + bi % 2]
                evac(stg[0:64, 0:256], ps2[0:64, 0:256])
                for j in range(1, LS):
                    S.dma_out("sp", cache_out[:, j - 1, b0:b0 + 256], stg[j:64:LS, 0:256])
        run_gens(win)

    def gla(pas, l):
        T, C, NCH = pas.T, pas.C, pas.NCH
        bp = Bump()
        qT_ = bp.fm("g_qT", 4, pas)
        kT_ = bp.fm("g_kT", 4, pas)
        ktm = bp.tm("g_k", NCH)
        vtm = bp.tm("g_v", NCH)
        grt = bp.tm("g_gr", NCH)
        W = w_in[l]
        proj_fm_wide(W[:, O_GQ:O_GQ + 512], 512, T, lambda t, ps: evac(qT_[:, t, 0:T], ps))
        proj_fm_wide(W[:, O_GK:O_GK + 512], 512, T, lambda t, ps: evac(kT_[:, t, 0:T], ps))
        proj_tm_wide(W[:, O_GK:O_GK + 512], 512, pas, ktm, False)
        proj_tm_wide(W[:, O_GV:O_GV + 512], 512, pas, vtm, True)
        proj_tm_wide(W[:, O_GR:O_GR + 512], 512, pas, grt, False)
        proj_fm(W[:, O_GLR:O_GLR + 16], 8, 16, lambda kc: hT[:, kc, 0:T], T,
                lambda j, ps, m: evac(glrA[0:16, 0:T], ps))
        S.dma_in("pool", wga[0:16, :], w_gate[l])
        S.dma_in("pool", wga[16:17, :], b_gate[l])
        Sg = S_gla[l]
        sts = None
        if pas.samp:
            sts = bp.tm("g_S", NSEQ)
            for i in range(NSEQ):
                S.dma_in("pool", sts[i].re("p (h v) -> p h v", v=128), st_gla[l, i].rearrange("h k v -> k h v"))
        spl, ebx, enb, qdT, kdT, AmT, kd2, ycur = scr[2], pl[1], pl[2], scr[3], scr[4], scr[5], scr[6], scr[0]
        if not pas.samp:
            def gs3(key, parent, ap):
                kk = (key, id(parent))
                if kk not in gsub:
                    gsub[kk] = S.sub(parent, ap, key)
                return gsub[kk]
            gb_ = {}
            for hp in range(2):
                hc = slice(hp * 2 * C, (hp + 1) * 2 * C)
                gb_[hp] = dict(
                    EB=gs3("gEB%d" % hp, ebx, ebx.base[:, hc]),
                    ENB=gs3("gENB%d" % hp, enb, enb.base[:, hc]),
                    QD=gs3("gQD%d" % hp, qdT, qdT.base[:, hc]),
                    KD=gs3("gKD%d" % hp, kdT, kdT.base[:, hc]),
                    AM=gs3("gAM%d" % hp, AmT, AmT.base[:, hc]),
                    YC=gs3("gYC%d" % hp, ycur, ycur.base[:, hp * 256:(hp + 1) * 256]),
                    C1=gs3("gC1%d" % hp, pl[0], pl[0].base[:, hp * 256:(hp + 1) * 256]),
                    C2=gs3("gC2%d" % hp, pl[3], pl[3].base[:, hp * 256:(hp + 1) * 256]),
                    SG=gs3("gSG%d" % hp, Sg, Sg.base[:, 2 * hp:2 * hp + 2, :]),
                    SM=gs3("gSM%d" % hp, sm2, sm2.base[:, hp * 4:hp * 4 + 4]),
                )

        def gla_chain(hp, c, cs):
            B = gb_[hp]
            h0 = 2 * hp
            hv = slice(h0 * 128, (h0 + 2) * 128)
            H2 = 2 * C

            def v3(b):
                return b[:, :].re("p (h c) -> p h c", c=C)
            eb3, enb3, qd3, kd3 = v3(B["EB"]), v3(B["ENB"]), v3(B["QD"]), v3(B["KD"])
            Am3 = B["AM"][0:C, :].re("p (h c) -> p h c", c=C)
            pb = PS()
            for k in range(2):
                S.mm(pb[:, k * C:(k + 1) * C], spl[0:C, (h0 + k) * 128:(h0 + k + 1) * 128], M(pas, M_UI), True, True)
            pb3 = pb[:, 0:H2].re("p (h c) -> p h c", c=C)
            S.act(eb3, pb3, AF.Exp, scale=-1.0 / 16.0)
            S.act(enb3, pb3, AF.Exp, scale=1.0 / 16.0)
            yield
            S.stt(qd3, qT_[:, h0:h0 + 2, cs], float(128 ** -0.5), eb3, ALU.mult, ALU.mult)
            S.tt(kd3, kT_[:, h0:h0 + 2, cs], enb3, ALU.mult)
            yield
            pa = PS()
            for k in range(2):
                S.mm(pa[0:C, k * C:(k + 1) * C], kd3[:, k, :], qd3[:, k, :], True, True)
            S.tt(Am3, pa[0:C, 0:H2].re("p (h c) -> p h c", c=C), M(pas, M_UI).un(1).bc([C, 2, C]), ALU.mult)
            yield
            po = PS(pin=True)
            SGh = B["SG"]
            for k in range(2):
                ks = slice(k * 128, (k + 1) * 128)
                S.mm(po[0:C, ks], Am3[:, k, :], vtm[c][0:C, (h0 + k) * 128:(h0 + k + 1) * 128], True, False)
                S.mm(po[0:C, ks], qd3[:, k, :], SGh[:, k, :], False, True)
            o_in = po[0:C, 0:256]
            c1 = B["C1"][0:C, :]
            c2 = B["C2"][0:C, :]
            st1 = B["SM"][0:C, 0:2]
            st2 = B["SM"][0:C, 2:4]
            gate = grt[c][0:C, hv]
            S.act(c1, o_in, AF.Square)
            S.red(st1, c1.re("p (g w) -> p g w", w=128))
            S.act(st2, st1, AF.Ln, bias=EPS, scale=1.0 / 128)
            S.act(st1, st2, AF.Exp, scale=-0.5)
            yield
            S.tt(c1.re("p (g w) -> p g w", w=128), o_in.re("p (g w) -> p g w", w=128), st1.un(2).bc([C, 2, 128]), ALU.mult)
            unpin(po)
            S.tt(c1.re("p (g w) -> p g w", w=128), c1.re("p (g w) -> p g w", w=128),
                 rv[l][0:C, RV_GLAN:RV_GLAN + 128].un(1).bc([C, 2, 128]), ALU.mult)
            S.act(c2, gate, AF.Tanh, scale=0.5)
            S.stt(c2, c2, 1.0, gate, ALU.add, ALU.mult)
            yc = B["YC"][0:C, :]
            S.stt(yc, c2, 0.5, c1, ALU.mult, ALU.mult)
            yield
            pt_ = PS()
            for k in range(2):
                S.mm(pt_[:, k * C:(k + 1) * C], yc[:, k * 128:(k + 1) * 128], msk[0:C, M_ID, 0:C], True, True)
            S.cp(yT_h[hp][:, :, c * C:(c + 1) * C], pt_[:, 0:H2].re("p (j c) -> p j c", c=C), "act")
            yield
            pu = PS()
            for k in range(2):
                ks = slice(k * 128, (k + 1) * 128)
                S.mm(pu[:, ks], kd2[0:C, (h0 + k) * 128:(h0 + k + 1) * 128], vtm[c][0:C, (h0 + k) * 128:(h0 + k + 1) * 128], True, True)
            for k in range(2):
                ks = slice(k * 128, (k + 1) * 128)
                S.stt(SGh[:, k, :], SGh[:, k, :], eb3[:, k, C - 1:C], pu[:, ks], ALU.mult, ALU.add)

        for c in range(NCH):
            cs = slice(c * C, (c + 1) * C)
            plg = PS()
            S.mm(plg[0:C, :], glrA[0:17, cs], wga[:, :], True, True)
            S.act(pl[0][0:C, :], plg[0:C, :], AF.Exp, scale=-1.0)
            S.act(spl[0:C, :], pl[0][0:C, :], AF.Ln, bias=1.0)
            if not pas.samp:
                pr = PS()
                S.mm(pr[0:C, :], M(pas, M_SL), spl[0:C, :], True, True)
                S.act(pl[4][0:C, :], pr[0:C, :], AF.Exp, scale=-1.0 / 16.0)
                S.tt(kd2[0:C, :], ktm[c][0:C, :], pl[4][0:C, :], ALU.mult)
                gens = [gla_chain(hp, c, cs) for hp in range(2)]
                while gens:
                    for gq_ in list(gens):
                        try:
                            next(gq_)
                        except StopIteration:
                            gens.remove(gq_)
                continue
            pb = PS()
            for h in range(4):
                S.mm(pb[:, h * C:(h + 1) * C], spl[0:C, h * 128:(h + 1) * 128], M(pas, M_UI), True, True)
            pb3 = pb[:, 0:4 * C].re("p (h c) -> p h c", c=C)
            eb3 = ebx[:, 0:4 * C].re("p (h c) -> p h c", c=C)
            enb3 = enb[:, 0:4 * C].re("p (h c) -> p h c", c=C)
            qd3 = qdT[:, 0:4 * C].re("p (h c) -> p h c", c=C)
            kd3 = kdT[:, 0:4 * C].re("p (h c) -> p h c", c=C)
            S.act(eb3, pb3, AF.Exp, scale=-1.0 / 16.0)
            S.act(enb3, pb3, AF.Exp, scale=1.0 / 16.0)
            S.stt(qd3, qT_[:, :, cs], float(128 ** -0.5), eb3, ALU.mult, ALU.mult)
            S.tt(kd3, kT_[:, :, cs], enb3, ALU.mult)
            pa = PS()
            for h in range(4):
                S.mm(pa[0:C, h * C:(h + 1) * C], kd3[:, h, :], qd3[:, h, :], True, True)
            Am3 = AmT[0:C, 0:4 * C].re("p (h c) -> p h c", c=C)
            S.tt(Am3, pa[0:C, 0:4 * C].re("p (h c) -> p h c", c=C), M(pas, M_UI).un(1).bc([C, 4, C]), ALU.mult)
            po = PS()
            for h in range(4):
                hs = slice(h * 128, (h + 1) * 128)
                S.mm(po[0:C, hs], Am3[:, h, :], vtm[c][0:C, hs], True, False)
                if not pas.samp:
                    S.mm(po[0:C, hs], qd3[:, h, :], Sg[:, h, :], False, True)
                else:
                    pad = build_pad(s3(qd3[:, h, :]))
                    for i in range(NSEQ):
                        S.mm(po[0:C, hs], pad[:, i, :], sts[i][:, hs], False, i == NSEQ - 1)
            gate_norm_out(pas, po[0:C, :], grt[c][0:C, :], rv[l][0:C, RV_GLAN:RV_GLAN + 128], 4, 128, ycur[0:C, :])
            y_to_yT(pas, c, ycur)
            pr = PS()
            S.mm(pr[0:C, :], M(pas, M_SL), spl[0:C, :], True, True)
            S.act(pl[0][0:C, :], pr[0:C, :], AF.Exp, scale=-1.0 / 16.0)
            S.tt(kd2[0:C, :], ktm[c][0:C, :], pl[0][0:C, :], ALU.mult)
            if not pas.samp:
                pu = PS()
                for h in range(4):
                    hs = slice(h * 128, (h + 1) * 128)
                    S.mm(pu[:, hs], kd2[0:C, hs], vtm[c][0:C, hs], True, True)
                for h in range(4):
                    hs = slice(h * 128, (h + 1) * 128)
                    S.stt(Sg[:, h, :], Sg[:, h, :], eb3[:, h, C - 1:C], pu[:, hs], ALU.mult, ALU.add)
            else:
                def gla_tail(c=c, eb3=eb3):
                    for i in range(NSEQ):
                        vzi = scr[1]
                        S.ts(vzi[0:C, :], vtm[c][0:C, :], msk[0:C, M_IND, i:i + 1], ALU.mult)
                        pu = PS()
                        for h in range(4):
                            hs = slice(h * 128, (h + 1) * 128)
                            S.mm(pu[:, hs], kd2[0:C, hs], vzi[0:C, hs], True, True)
                        sn = big[1].v().re("p a b -> p (a b)")[:, (i % 2) * 512:(i % 2) * 512 + 512]
                        S.tt(sn.re("p (h v) -> p h v", v=128), sts[i].re("p (h v) -> p h v", v=128),
                             eb3[:, :, LS * i + LS - 1:LS * i + LS].bc([128, 4, 128]), ALU.mult)
                        S.tt(sn, sn, pu[:, :], ALU.add)
                        S.dma_out("sp", o_sg[l, i].rearrange("h k v -> k h v"), sn.re("p (h v) -> p h v", v=128))
                        yield
                tails.append(gla_tail())
        if pas.last and not pas.samp:
            S.dma_out("sp", o_pg[l].rearrange("h k v -> k h v"), Sg.v())

    def ssd(pas, l):
        T, C, NCH = pas.T, pas.C, pas.NCH
        bp = Bump()
        cvo = bp.fm("s_cv", 8, pas)
        cvo_t = [S.sub(cvo.buf, cvo.ap[:, t, :], "s_cv_t%d" % t) for t in range(8)]
        szt = bp.tm("s_z", NCH)
        xtm = bp.tm("s_x", NCH)
        Btm = bp.tm("s_B", NCH)
        W = w_in[l]
        hs_ = hist_s[l]
        if pas.samp:
            load_hist_sample(bp, cv_ssd[l], 8, "s_cvin")

        def conv_sink(t, ps):
            raw = pl[3 + t % 2][:, 0:T]
            g_ = conv_fm(pas, l, ps, raw, cvo_t[t][:, 0:T], lambda j: wsc[:, l, t, j:j + 1], bsc[:, l, t:t + 1],
                         hist_in[:, t, :, :] if pas.samp else hs_[:, t, :], pl[t % 3][:, 0:T])
            next(g_)
            yield
            next(g_)
            if not pas.samp:
                S.cp(hs_[:, t, :], raw[:, T - 3:T])
                if pas.last:
                    S.dma_out("sp", o_pcs[l][:, t * 128:(t + 1) * 128].rearrange("j p -> p j"), hs_[:, t, :],
                              allow_slow_non_contiguous=True)
            yield
            for _ in g_:
                pass

        proj_conv(W[:, O_SXBC:O_SXBC + 1024], 1024, pas, conv_sink, o_scs[l])
        proj_tm_wide(W[:, O_SZ:O_SZ + 512], 512, pas, szt, False)
        dts = smallT

        def dt_sink(c, ps):
            evac(dts[0:C, c * 8:(c + 1) * 8], ps)
        proj_tm(W[:, O_SDT:O_SDT + 8], 8, pas, dt_sink)
        ST = S_ssd[l]
        Lh, E, MT = big[0], big[1], big[2]
        ycur = scr[0]
        ex = sm
        nat_slots = bp.tm("s_nat", 3) if pas.samp else None
        if pas.samp:
            CTz = [None, None]
            ctzb = [bp.raw("s_ctz%d" % g, 2) for g in range(2)]
            for g in range(2):
                S.memset(ctzb[g].v(), 0.0)
        if not pas.samp:
            def gs2(key, parent, ap):
                kk = (key, id(parent))
                if kk not in gsub:
                    gsub[kk] = S.sub(parent, ap, key)
                return gsub[kk]
            sb_ = {}
            for g in range(2):
                sb_[g] = dict(
                    LH=gs2("sLH%d" % g, big[0], big[0].base[:, 4 * g:4 * g + 4, :]),
                    E=gs2("sE%d" % g, big[1], big[1].base[:, 4 * g:4 * g + 4, :]),
                    MT=gs2("sMT%d" % g, big[2], big[2].base[:, 4 * g:4 * g + 4, :]),
                    CB=gs2("sCB%d" % g, pl[1], pl[1].base[:, g * 256:(g + 1) * 256]),
                    T1=gs2("sT1%d" % g, pl[2], pl[2].base[:, g * 256:(g + 1) * 256]),
                    T2=gs2("sT2%d" % g, pl[3], pl[3].base[:, g * 256:(g + 1) * 256]),
                    C1=gs2("sC1%d" % g, pl[0], pl[0].base[:, g * 256:(g + 1) * 256]),
                    XW=gs2("sXW%d" % g, scr[3], scr[3].base[:, g * 256:(g + 1) * 256]),
                    YC=gs2("sYC%d" % g, scr[0], scr[0].base[:, g * 256:(g + 1) * 256]),
                    ST=gs2("sST%d" % g, ST, ST.base[:, 4 * g:4 * g + 4, :]),
                    SM=gs2("sSM%d" % g, sm2, sm2.base[:, 8 + g * 4:8 + g * 4 + 4]),
                )

        def ssd_chain(g, c, cs, dt, dtA, wts, ecum, elast):
            B = sb_[g]
            h4 = slice(4 * g, 4 * g + 4)
            gv = slice(g * 256, (g + 1) * 256)
            Lh3 = B["LH"][0:C, :, 0:C]
            E3 = B["E"][0:C, :, 0:C]
            MT3 = B["MT"][0:C, :, 0:C]
            S.tt(Lh3, M(pas, M_UI).un(1).bc([C, 4, C]), dtA[:, h4].un(2).bc([C, 4, C]), ALU.mult)
            yield
            pseg = PS()
            S.mm(pseg[0:C, 0:4 * C].re("p (h c) -> p h c", c=C), M(pas, M_SL), Lh3, True, True)
            S.act(E3, pseg[0:C, 0:4 * C].re("p (h c) -> p h c", c=C), AF.Exp)
            pcb = PS()
            S.mm(pcb[0:C, 0:C], cvo[:, 4 + g, cs], cvo[:, 6 + g, cs], True, True)
            CBm = B["CB"][0:C, 0:C]
            S.tt(CBm, pcb[0:C, 0:C], M(pas, M_UI), ALU.mult)
            yield
            S.tt(E3, E3, dt[:, h4].un(2).bc([C, 4, C]), ALU.mult)
            S.tt(MT3, E3, CBm.un(1).bc([C, 4, C]), ALU.mult)
            yield
            py = PS()
            for hh in range(4):
                h = 4 * g + hh
                S.mm(py[0:C, hh * 64:(hh + 1) * 64], MT3[:, hh, :], xtm[c][0:C, h * 64:(h + 1) * 64], True, True)
            pz = PS()
            STg = B["ST"]
            S.mm(pz[0:C, 0:256], cvo[:, 6 + g, cs], STg[:, :, :].re("p h q -> p (h q)"), True, True)
            t1 = B["T1"][0:C, :]
            S.tt(t1.re("p (h q) -> p h q", q=64), pz[0:C, 0:256].re("p (h q) -> p h q", q=64), ecum[:, h4].un(2).bc([C, 4, 64]), ALU.mult)
            S.tt(t1, t1, py[0:C, 0:256], ALU.add)
            yield
            t2 = B["T2"][0:C, :]
            S.tt(t2.re("p (h q) -> p h q", q=64), xtm[c][0:C, gv].re("p (h q) -> p h q", q=64),
                 rv[l][0:C, RV_SD + 4 * g:RV_SD + 4 * g + 4].un(2).bc([C, 4, 64]), ALU.mult)
            S.tt(t1, t1, t2, ALU.add)
            S.tt(t1, t1, szt[c][0:C, gv], ALU.mult)
            yield
            c1 = B["C1"][0:C, :]
            st1 = B["SM"][0:C, 0:1]
            st2 = B["SM"][0:C, 2:3]
            S.act(c1, t1, AF.Square)
            S.red(st1, c1.re("p (g w) -> p g w", w=256))
            S.act(st2, st1, AF.Ln, bias=EPS, scale=1.0 / 256)
            S.act(st1, st2, AF.Exp, scale=-0.5)
            yield
            yc = B["YC"][0:C, :]
            S.stt(yc, t1, st1, rv[l][0:C, RV_SNORM + g * 256:RV_SNORM + (g + 1) * 256], ALU.mult, ALU.mult)
            pt_ = PS()
            for k in range(2):
                S.mm(pt_[:, k * C:(k + 1) * C], yc[:, k * 128:(k + 1) * 128], msk[0:C, M_ID, 0:C], True, True)
            S.cp(yT_h[g][:, :, c * C:(c + 1) * C], pt_[:, 0:2 * C].re("p (j c) -> p j c", c=C), "act")
            yield
            xw = B["XW"][0:C, :]
            S.tt(xw.re("p (h q) -> p h q", q=64), xtm[c][0:C, gv].re("p (h q) -> p h q", q=64),
                 wts[:, h4].un(2).bc([C, 4, 64]), ALU.mult)
            pu = PS()
            S.mm(pu[:, 0:256], Btm[c][0:C, g * 128:(g + 1) * 128], xw, True, True)
            t3 = B["T2"][:, :]
            S.tt(t3.re("p (h q) -> p h q", q=64), STg.v(), elast[:, h4].un(2).bc([128, 4, 64]), ALU.mult)
            S.tt(STg.v().re("p h q -> p (h q)"), t3, pu[:, 0:256], ALU.add)

        def ssd_pro(c, par, hold):
            cs = slice(c * C, (c + 1) * C)
            smx, smrx = (sm, sm_b)[par], (smr, smr_b)[par]
            px = transposes_fm(cvo, 0, 4, cs, C)
            S.cp(xtm[c][0:C, :].re("p (j w) -> p j w", w=128), px, "act")
            yield
            pB = transposes_fm(cvo, 4, 2, cs, C)
            S.cp(Btm[c][0:C, 0:256].re("p (j w) -> p j w", w=128), pB, "act")
            yield
            silu_inplace(szt[c][0:C, :], pl[4][0:C, :])
            yield
            dt = smx[0:C, 24:32]
            dtA = smrx[0:C, 0:8]
            wts = smx[0:C, 40:48]
            S.tt(dt, dts[0:C, c * 8:(c + 1) * 8], rv[l][0:C, RV_SDTB:RV_SDTB + 8], ALU.add)
            S.act(dt, dt, AF.Exp)
            S.act(dt, dt, AF.Ln, bias=1.0)
            S.tt(dtA, dt, negA[0:C, l, 0:8], ALU.mult)
            yield
            cum3(pas, dtA, 8, smx)
            ecum, erev, elast = smx[0:C, 0:8], smx[0:C, 8:16], smx[0:C, 16:24]
            S.tt(wts, erev, dt, ALU.mult)
            hold["res"] = (cs, dt, dtA, wts, ecum, erev, elast, smx)

        if not pas.samp:
            hold = {}
            run_gens([ssd_pro(0, 0, hold)])
            for c in range(NCH):
                nhold = {}
                cs, dt, dtA, wts, ecum, erev, elast, ex = hold["res"]
                gens = [ssd_chain(g, c, cs, dt, dtA, wts, ecum, elast) for g in range(2)]
                if c + 1 < NCH:
                    gens.append(ssd_pro(c + 1, (c + 1) % 2, nhold))
                run_gens(gens)
                hold = nhold
        for c in (range(NCH) if pas.samp else []):
            hold = {}
            run_gens([ssd_pro(c, 0, hold)])
            cs, dt, dtA, wts, ecum, erev, elast, ex = hold["res"]
            if not pas.samp:
                continue
            Lh3 = Lh[0:C, :, 0:C]
            S.tt(Lh3, M(pas, M_SL).un(1).bc([C, 8, C]), dtA.un(2).bc([C, 8, C]), ALU.mult)
            for half in range(2):
                pseg = PS()
                for hh in range(4):
                    h = half * 4 + hh
                    S.mm(pseg[0:C, hh * C:(hh + 1) * C], Lh3[:, h, :], M(pas, M_UI), True, True)
                S.act(E[0:C, half * 4:(half + 1) * 4, 0:C], pseg[0:C, 0:4 * C].re("p (h c) -> p h c", c=C), AF.Exp)
            pcb = PS()
            for g in range(2):
                S.mm(pcb[0:C, g * C:(g + 1) * C], cvo[:, 4 + g, cs], cvo[:, 6 + g, cs], True, True)
            CBm = pl[1][0:C, 0:2 * C].re("p (g c) -> p g c", c=C)
            S.tt(CBm, pcb[0:C, 0:2 * C].re("p (g c) -> p g c", c=C), M(pas, M_UI).un(1).bc([C, 2, C]), ALU.mult)
            E3 = E[0:C, :, 0:C]
            MT3 = MT[0:C, :, 0:C]
            S.tt(E3, E3, dt.un(2).bc([C, 8, C]), ALU.mult)
            for g in range(2):
                S.tt(MT3[:, g * 4:(g + 1) * 4, :], E3[:, g * 4:(g + 1) * 4, :], CBm[:, g:g + 1, :].bc([C, 4, C]), ALU.mult)
            py = PS(pin=True)
            for h in range(8):
                S.mm(py[0:C, h * 64:(h + 1) * 64], MT3[:, h, :], xtm[c][0:C, h * 64:(h + 1) * 64], True, True)
            pz = PS(pin=True)
            if not pas.samp:
                for g in range(2):
                    S.mm(pz[0:C, g * 256:(g + 1) * 256], cvo[:, 6 + g, cs], ST[:, g * 4:(g + 1) * 4, :].re("p h q -> p (h q)"), True, True)
            else:
                for g in range(2):
                    CTz[g] = build_pad_into(ctzb[g], ctzb[g].base, s3(cvo[:, 6 + g, 0:T]))
                S.mm(pz[0:C, :], zt[:, 0:C], msk[:, 0:4, :].re("p a b -> p (a b)"), True, True)
                xw = scr[3]
                S.tt(xw[0:C, :].re("p (h q) -> p h q", q=64), xtm[c][0:C, :].re("p (h q) -> p h q", q=64),
                     wts.un(2).bc([C, 8, 64]), ALU.mult)
                dtrep = scr[2][0:C, :].re("p (a m) -> p a m", m=128)
                for a in range(4):
                    S.cp(dtrep[:, a, :].re("p (b q) -> p b q", q=64), dtA[:, 2 * a:2 * a + 2].un(2).bc([C, 2, 64]))
                pe_ = PS()
                for a in range(4):
                    S.mm(pe_[:, a * NSEQ:(a + 1) * NSEQ], dtrep[:, a, :], msk[0:C, M_IND, 0:NSEQ], True, True)
                S.act(ecol.v().re("p a i -> p (a i)"), pe_[:, 0:4 * NSEQ], AF.Exp)
                for i in range(NSEQ):
                    nat = nat_slots[i % 3].re("p (a n) -> p a n", n=128)
                    S.dma_in("pool", nat, st_ssd[l, i].rearrange("(a b) q n -> (b q) a n", b=2))
                    pt = PS()
                    for a in range(4):
                        S.mm(pt[:, a * 128:(a + 1) * 128], nat[:, a, :], msk[:, M_ID, :], True, True)
                    sti = scr[(5, 6)[i % 2]]
                    evac(sti[:, :], pt[:, :])
                    for g in range(2):
                        S.mm(pz[0:C, g * 256:(g + 1) * 256], CTz[g][:, i, :], sti[:, g * 256:(g + 1) * 256], False, i == NSEQ - 1)
                    Bz = scr[1]
                    S.ts(Bz[0:C, 0:256], Btm[c][0:C, 0:256], msk[0:C, M_IND, i:i + 1], ALU.mult)
                    pu = PS()
                    for a in range(4):
                        g = a // 2
                        S.mm(pu[:, a * 128:(a + 1) * 128], xw[0:C, a * 128:(a + 1) * 128], Bz[0:C, g * 128:(g + 1) * 128], True, True)
                    sn = big[1].v().re("p a b -> p (a b)")[:, (i % 2) * 512:(i % 2) * 512 + 512].re("p (a n) -> p a n", n=128)
                    for a in range(4):
                        S.stt(sn[:, a, :], nat[:, a, :], ecol[:, a, i:i + 1], pu[:, a * 128:(a + 1) * 128], ALU.mult, ALU.add)
                    S.dma_out("sp", o_ss[l, i].rearrange("(a b) q n -> (b q) a n", b=2), sn)
            t1 = pl[2][0:C, :]
            S.tt(t1.re("p (h q) -> p h q", q=64), pz[0:C, :].re("p (h q) -> p h q", q=64), ecum.un(2).bc([C, 8, 64]), ALU.mult)
            S.tt(t1, t1, py[0:C, :], ALU.add)
            unpin(py)
            unpin(pz)
            t2 = pl[3][0:C, :]
            S.tt(t2.re("p (h q) -> p h q", q=64), xtm[c][0:C, :].re("p (h q) -> p h q", q=64),
                 rv[l][0:C, RV_SD:RV_SD + 8].un(2).bc([C, 8, 64]), ALU.mult)
            S.tt(t1, t1, t2, ALU.add)
            S.tt(t1, t1, szt[c][0:C, :], ALU.mult)
            c1 = pl[0][0:C, :]
            st1 = sm[0:C, 64:66]
            st2 = sm[0:C, 96:98]
            S.act(c1, t1, AF.Square)
            S.red(st1, c1.re("p (g w) -> p g w", w=256))
            S.act(st2, st1, AF.Ln, bias=EPS, scale=1.0 / 256)
            S.act(st1, st2, AF.Exp, scale=-0.5)
            S.tt(t1.re("p (g w) -> p g w", w=256), t1.re("p (g w) -> p g w", w=256), st1.un(2).bc([C, 2, 256]), ALU.mult)
            S.tt(ycur[0:C, :], t1, rv[l][0:C, RV_SNORM:RV_SNORM + 512], ALU.mult)
            y_to_yT(pas, c, ycur)
            if not pas.samp:
                xw = scr[3]
                S.tt(xw[0:C, :].re("p (h q) -> p h q", q=64), xtm[c][0:C, :].re("p (h q) -> p h q", q=64),
                     wts.un(2).bc([C, 8, 64]), ALU.mult)
                pu = PS()
                for g in range(2):
                    S.mm(pu[:, g * 256:(g + 1) * 256], Btm[c][0:C, g * 128:(g + 1) * 128], xw[0:C, g * 256:(g + 1) * 256], True, True)
                t3 = pl[3]
                S.tt(t3.v().re("p (h q) -> p h q", q=64), ST.v(), elast.un(2).bc([128, 8, 64]), ALU.mult)
                S.tt(ST.v().re("p h q -> p (h q)"), t3.v(), pu[:, :], ALU.add)
        if pas.last and not pas.samp:
            pt = PS()
            for a in range(4):
                S.mm(pt[:, a * 128:(a + 1) * 128], ST[:, 2 * a:2 * a + 2, :].re("p h q -> p (h q)"), msk[:, M_ID, :], True, True)
            sn = big[1].v().re("p a b -> p (a b)")[:, 0:512]
            evac(sn, pt[:, :])
            S.dma_out("sp", o_ps[l].rearrange("(a b) q n -> (b q) a n", b=2), sn.re("p (a n) -> p a n", n=128))

    def transposes_fm(fmv, t0, nt, cs, C):
        ps = PS()
        for j in range(nt):
            S.mm(ps[0:C, j * 128:(j + 1) * 128], fmv[:, t0 + j, cs], msk[:, M_ID, :], True, True)
        return ps[0:C, 0:nt * 128].re("p (j w) -> p j w", w=128)

    def gdn(pas, l):
        T, C, NCH = pas.T, pas.C, pas.NCH
        bp = Bump()
        cvo = bp.fm("d_cv", 12, pas)
        cvo_t = [S.sub(cvo.buf, cvo.ap[:, t, :], "d_cv_t%d" % t) for t in range(12)]
        dgt = bp.tm("d_g", NCH)
        extra = bp.tm("d_x", 4)
        vtm_, kbtm, kbT_, ycur = extra
        W = w_in[l]
        hd_ = hist_d[l]
        if pas.samp:
            load_hist_sample(bp, cv_gdn[l], 12, "d_cvin")
        norm_todo = []

        def conv_sink(t, ps):
            raw = pl[3 + t % 2][:, 0:T]
            g_ = conv_fm(pas, l, ps, raw, cvo_t[t][:, 0:T], lambda j: wdc[:, l, t, j:j + 1], None,
                         hist_in[:, t, :, :] if pas.samp else hd_[:, t, :], pl[t % 3][:, 0:T])
            next(g_)
            yield
            next(g_)
            if not pas.samp:
                S.cp(hd_[:, t, :], raw[:, T - 3:T])
                if pas.last:
                    S.dma_out("sp", o_pcd[l][:, t * 128:(t + 1) * 128].rearrange("j p -> p j"), hd_[:, t, :],
                              allow_slow_non_contiguous=True)
            yield
            for _ in g_:
                pass
            if t < 8:
                norm_todo.append(t)

        def l2norm_tiles():
            for t in norm_todo:
                sq = scr[t % 2][:, 0:T]
                S.act(sq, cvo_t[t][:, 0:T], AF.Square)
                pn = PS()
                S.mm(pn[:, 0:T], msk[:, M_BLK, :], sq, True, True)
                S.act(sq, pn[:, 0:T], AF.Ln, bias=EPS)
                S.act(sq, sq, AF.Exp, scale=-0.5, bias=(math.log(128 ** -0.5) if t < 4 else 0.0))
                S.tt(cvo_t[t][:, 0:T], cvo_t[t][:, 0:T], sq, ALU.mult)

        proj_conv(W[:, O_DQKV:O_DQKV + 1536], 1536, pas, conv_sink, o_scd[l])
        l2norm_tiles()
        proj_tm_wide(W[:, O_DG:O_DG + 512], 512, pas, dgt, False)
        dab = smallT

        def ab_sink(c, ps):
            evac(dab[0:C, c * 8:(c + 1) * 8], ps)
        proj_tm(W[:, O_DA:O_DA + 8], 8, pas, ab_sink)
        Sd = S_gdn[l]
        sts = None
        if pas.samp:
            sts = bp.tm("d_S", NSEQ)
            for i in range(NSEQ):
                S.dma_in("pool", sts[i].re("p (h v) -> p h v", v=128), st_gdn[l, i].rearrange("h k v -> k h v"))
        nlev = 1 if pas.samp else 5
        GM = {M_UI: M_GUI, M_SL: M_GSL, M_SU: M_GSU, M_BLK: M_GBLK}

        def mk(idx):
            return M(pas, idx) if pas.samp else msk[0:C, GM[idx], 0:C]
        blocks = [(0, 64)] if pas.samp else [(0, 64), (64, 128)]
        if not pas.samp:
            NHD = 2
            H2 = NHD * C

            def gs(key, parent, ap):
                kk = (key, id(parent))
                if kk not in gsub:
                    gsub[kk] = S.sub(parent, ap, key)
                return gsub[kk]
            cb = {}
            for hp in range(4 // NHD):
                q0, q1 = hp * NHD, (hp + 1) * NHD
                tg = "n%d_%d" % (NHD, hp)
                cb[hp] = dict(
                    LA=gs("LA" + tg, big[0], big[0].base[:, q0:q1, :]),
                    LB=gs("LB" + tg, big[0], big[0].base[:, 4 + q0:4 + q1, :]),
                    EA=gs("EA" + tg, big[1], big[1].base[:, q0:q1, :]),
                    EB=gs("EB" + tg, big[1], big[1].base[:, 4 + q0:4 + q1, :]),
                    AQ=gs("AQ" + tg, scr[2], scr[2].base[:, hp * H2:(hp + 1) * H2]),
                    P=gs("P" + tg, pl[1], pl[1].base[:, hp * H2:(hp + 1) * H2]),
                    X0=gs("X0" + tg, pl[2], pl[2].base[:, hp * H2:(hp + 1) * H2]),
                    XT0=gs("XT0" + tg, pl[3], pl[3].base[:, hp * H2:(hp + 1) * H2]),
                    X1=gs("X1" + tg, pl[4], pl[4].base[:, hp * H2:(hp + 1) * H2]),
                    XT1=gs("XT1" + tg, pl[0], pl[0].base[:, hp * H2:(hp + 1) * H2]),
                    WT=gs("WT" + tg, scr[3], scr[3].base[:, hp * H2:(hp + 1) * H2]),
                    VN=gs("VN" + tg, scr[4], scr[4].base[:, q0 * 128:q1 * 128]),
                    SD=gs("SD" + tg, Sd, Sd.base[:, q0:q1, :]),
                    ST=gs("ST" + tg, sm2, sm2.base[:, q0 * 2:q1 * 2]),
                    YC=gs("YC" + tg, ycur.buf, ycur.ap[:, q0 * 128:q1 * 128]),
                    YT=gs("YT" + tg, yT, yT.base[:, q0:q1, :]),
                )

        def gdn_chain(hp, c, cs, beta, gd, bec, ecum, kn, kbT3, kw, el2, vt):
            B = cb[hp]
            h0 = NHD * hp
            HW = NHD * 128
            hv = slice(h0 * 128, (h0 + NHD) * 128)

            def v3(buf):
                return buf[0:C, :].re("p (h c) -> p h c", c=C) if len(buf.base.shape) == 2 else buf[0:C, :, 0:C]
            LA, LB, EA, EB = v3(B["LA"]), v3(B["LB"]), v3(B["EA"]), v3(B["EB"])
            AqT, P = v3(B["AQ"]), v3(B["P"])
            gdh = gd[:, h0:h0 + NHD]
            S.tt(LA, mk(M_SL).un(1).bc([C, NHD, C]), gdh.un(2).bc([C, NHD, C]), ALU.mult)
            S.tt(LB, mk(M_UI).un(1).bc([C, NHD, C]), gdh.un(2).bc([C, NHD, C]), ALU.mult)
            yield
            psa = PS()
            S.mm(psa[0:C, 0:H2].re("p (h c) -> p h c", c=C), mk(M_SL), LB, True, True)
            S.act(EA, psa[0:C, 0:H2].re("p (h c) -> p h c", c=C), AF.Exp)
            psb = PS()
            S.mm(psb[0:C, 0:H2].re("p (h c) -> p h c", c=C), mk(M_UI), LA, True, True)
            S.act(EB, psb[0:C, 0:H2].re("p (h c) -> p h c", c=C), AF.Exp)
            yield
            pgm = PS()
            for k in range(NHD):
                S.mm(pgm[0:C, k * C:(k + 1) * C], kbT3[:, h0 + k, :], kbT3[:, h0 + k, :], True, True)
            pg3 = pgm[0:C, 0:H2].re("p (h c) -> p h c", c=C)
            paq = PS()
            for k in range(NHD):
                S.mm(paq[0:C, k * C:(k + 1) * C], cvo[:, 4 + h0 + k, cs], cvo[:, h0 + k, cs], True, True)
            S.tt(EA, EA, mk(M_UI).un(1).bc([C, NHD, C]), ALU.mult)
            S.tt(AqT, paq[0:C, 0:H2].re("p (h c) -> p h c", c=C), EA, ALU.mult)
            X, XT = EA, EB
            S.tt(X, pg3, EA, ALU.mult)
            S.tt(XT, pg3, EB, ALU.mult)
            yield
            S.stt(X, X, -1.0, msk[0:C, M_GSU, 0:C].un(1).bc([C, NHD, C]), ALU.mult, ALU.mult)
            S.stt(XT, XT, -1.0, mk(M_SL).un(1).bc([C, NHD, C]), ALU.mult, ALU.mult)
            S.tt(P, X, msk[0:C, M_ID, 0:C].un(1).bc([C, NHD, C]), ALU.add)
            yield
            Xc, XTc = X, XT
            for lev in range(nlev):
                lastlev = (lev == nlev - 1)
                Xn = v3(B["X0"] if lev % 2 == 0 else B["X1"])
                XTn = v3(B["XT0"] if lev % 2 == 0 else B["XT1"])
                pxt = PS()
                for k in range(NHD):
                    S.mm32(pxt[0:C, k * C:(k + 1) * C], Xc[:, k, :], XTc[:, k, :], True, True)
                S.cp(XTn, pxt[0:C, 0:H2].re("p (h c) -> p h c", c=C), "act")
                if not lastlev:
                    px_ = PS()
                    for k in range(NHD):
                        S.mm32(px_[0:C, k * C:(k + 1) * C], XTc[:, k, :], Xc[:, k, :], True, True)
                    S.cp(Xn, px_[0:C, 0:H2].re("p (h c) -> p h c", c=C), "act")
                yield
                pp = PS()
                for k in range(NHD):
                    S.mm32(pp[0:C, k * C:(k + 1) * C], XTn[:, k, :], P[:, k, :], True, True)
                S.tt(P, P, pp[0:C, 0:H2].re("p (h c) -> p h c", c=C), ALU.add)
                Xc, XTc = Xn, XTn
                yield
            vb = B["LA"][0:C, :, :]
            kbd = B["LB"][0:C, :, :]
            S.tt(vb, vt[0:C, hv].re("p (h w) -> p h w", w=128), beta[:, h0:h0 + NHD].un(2).bc([C, NHD, 128]), ALU.mult)
            S.tt(kbd, kn[0:C, hv].re("p (h w) -> p h w", w=128), bec[:, h0:h0 + NHD].un(2).bc([C, NHD, 128]), ALU.mult)
            yield
            pw = PS()
            for k in range(NHD):
                S.mm32(pw[:, k * C:(k + 1) * C], kbd[:, k, :], P[:, k, :], True, True)
            WT = v3(B["WT"])
            S.ts(WT, pw[:, 0:H2].re("p (h c) -> p h c", c=C), -1.0, ALU.mult)
            yield
            vnew = B["VN"]
            SDh = B["SD"]
            o_sb = B["X0"][0:C, :]
            for b, (r0, r1) in enumerate(blocks):
                rs = slice(r0, r1)
                pvn = PS()
                pqs = PS()
                for k in range(NHD):
                    ks = slice(k * 128, (k + 1) * 128)
                    S.mm32(pvn[0:C, ks], P[:, k, :], vb[:, k, :], True, False)
                    S.mm32(pvn[0:C, ks], WT[:, k, :], SDh[:, k, :], False, True)
                evac(vnew[rs, :], pvn[rs, 0:HW])
                for k in range(NHD):
                    ks = slice(k * 128, (k + 1) * 128)
                    S.mm(pqs[0:C, ks], cvo[:, h0 + k, cs], SDh[:, k, :], True, True)
                S.tt(o_sb[rs, :].re("p (h w) -> p h w", w=128), pqs[rs, 0:HW].re("p (h w) -> p h w", w=128),
                     ecum[rs, h0:h0 + NHD].un(2).bc([r1 - r0, NHD, 128]), ALU.mult)
                yield
                pu = PS()
                for k in range(NHD):
                    ks = slice(k * 128, (k + 1) * 128)
                    S.mm(pu[:, ks], kw[rs, (h0 + k) * 128:(h0 + k + 1) * 128], vnew[rs, ks], True, True)
                t3 = B["XT1"][:, :]
                S.tt(t3.re("p (h w) -> p h w", w=128), SDh.v(), el2[:, b, h0:h0 + NHD].un(2).bc([128, NHD, 128]), ALU.mult)
                S.tt(SDh.v(), t3.re("p (h w) -> p h w", w=128), pu[:, 0:HW].re("p (h w) -> p h w", w=128), ALU.add)
                yield
            pav = PS()
            for k in range(NHD):
                ks = slice(k * 128, (k + 1) * 128)
                S.mm(pav[0:C, ks], AqT[:, k, :], vnew[0:C, ks], True, True)
            S.tt(o_sb, o_sb, pav[0:C, 0:HW], ALU.add)
            yield
            c1 = B["XT0"][0:C, :]
            c2 = B["X1"][0:C, :]
            st1 = B["ST"][0:C, 0:NHD]
            st2 = B["ST"][0:C, NHD:2 * NHD]
            gate = dgt[c][0:C, hv]
            S.act(c1, o_sb, AF.Square)
            S.red(st1, c1.re("p (g w) -> p g w", w=128))
            S.act(st2, st1, AF.Ln, bias=EPS, scale=1.0 / 128)
            S.act(st1, st2, AF.Exp, scale=-0.5)
            yield
            S.tt(c1.re("p (g w) -> p g w", w=128), o_sb.re("p (g w) -> p g w", w=128), st1.un(2).bc([C, NHD, 128]), ALU.mult)
            S.tt(c1.re("p (g w) -> p g w", w=128), c1.re("p (g w) -> p g w", w=128),
                 rv[l][0:C, RV_DNORM:RV_DNORM + 128].un(1).bc([C, NHD, 128]), ALU.mult)
            S.act(c2, gate, AF.Tanh, scale=0.5)
            S.stt(c2, c2, 1.0, gate, ALU.add, ALU.mult)
            yc = B["YC"][0:C, :]
            S.stt(yc, c2, 0.5, c1, ALU.mult, ALU.mult)
            yield
            pt_ = PS()
            for k in range(NHD):
                S.mm(pt_[:, k * C:(k + 1) * C], yc[:, k * 128:(k + 1) * 128], msk[0:C, M_ID, 0:C], True, True)
            evac(B["YT"][:, :, c * C:(c + 1) * C], pt_[:, 0:H2].re("p (j c) -> p j c", c=C))

        alt_ = bp.tm("d_alt", 3) if not pas.samp else None

        def gdn_pro(c, par, hold):
            cs = slice(c * C, (c + 1) * C)
            smx, smrx = (sm, sm_b)[par], (smr, smr_b)[par]
            vt = vtm_ if par == 0 else alt_[0]
            kbt = kbtm if par == 0 else alt_[1]
            kbTx = kbT_ if par == 0 else alt_[2]
            kn = scr[6] if par == 0 else scr[0]
            kw = scr[5] if par == 0 else scr[1]
            beta = smx[0:C, 24:28]
            gd = smrx[0:C, 0:4]
            bec = smx[0:C, 32:36]
            tmp4 = smx[0:C, 36:40]
            sigmoid_from(beta, dab[0:C, c * 8 + 4:c * 8 + 8], tmp4)
            S.tt(gd, dab[0:C, c * 8:c * 8 + 4], rv[l][0:C, RV_DDTB:RV_DDTB + 4], ALU.add)
            S.act(gd, gd, AF.Exp)
            S.act(gd, gd, AF.Ln, bias=1.0)
            S.tt(gd, gd, negA[0:C, l, 8:12], ALU.mult)
            yield
            pc = PS()
            S.mm(pc[0:C, 0:4], mk(M_UI), gd, True, True)
            S.mm(pc[0:C, 4:8], mk(M_SL), gd, True, True)
            S.mm(pc[0:C, 8:12], mk(M_BLK), gd, True, True)
            S.act(smx[0:C, 0:12], pc[0:C, 0:12], AF.Exp)
            ecum, erev, elast = smx[0:C, 0:4], smx[0:C, 4:8], smx[0:C, 8:12]
            S.tt(bec, beta, ecum, ALU.mult)
            yield
            pv = transposes_fm(cvo, 8, 4, cs, C)
            evac(vt[0:C, :].re("p (j w) -> p j w", w=128), pv)
            yield
            pk = transposes_fm(cvo, 4, 4, cs, C)
            evac(kn[0:C, :].re("p (j w) -> p j w", w=128), pk)
            yield
            S.tt(kbt[0:C, :].re("p (h w) -> p h w", w=128), kn[0:C, :].re("p (h w) -> p h w", w=128),
                 beta.un(2).bc([C, 4, 128]), ALU.mult)
            yield
            pkb = transposes(kbt, C, 4)
            kbT3 = kbTx[:, 0:4 * C].re("p (h c) -> p h c", c=C)
            evac(kbT3, pkb)
            yield
            el2 = None
            if not pas.samp:
                S.tt(kw[0:C, :].re("p (h w) -> p h w", w=128), kn[0:C, :].re("p (h w) -> p h w", w=128),
                     erev.un(2).bc([C, 4, 128]), ALU.mult)
                gz2 = smrx[:, 8:16].re("p (b h) -> p b h", h=4)
                for b, (r0, r1) in enumerate(blocks):
                    S.ts(gz2[:, b, :], gd, msk[:, M_GBLK, r0:r0 + 1], ALU.mult)
                pe2 = PS()
                S.mm(pe2[:, 0:8], msk[:, M_BLK, :], smrx[:, 8:16], True, True)
                el2 = smx[:, 48:56].re("p (b h) -> p b h", h=4)
                S.act(smx[:, 48:56], pe2[:, 0:8], AF.Exp)
            hold["res"] = (cs, beta, gd, bec, ecum, erev, elast, kn, kbT3, kw, el2, vt, smx)

        if not pas.samp:
            hold = {}
            run_gens([gdn_pro(0, 0, hold)])
            for c in range(NCH):
                nhold = {}
                cs, beta, gd, bec, ecum, erev, elast, kn, kbT3, kw, el2, vt, ex = hold["res"]
                gens = [gdn_chain(hp, c, cs, beta, gd, bec, ecum, kn, kbT3, kw, el2, vt) for hp in range(4 // NHD)]
                if c + 1 < NCH:
                    gens.append(gdn_pro(c + 1, (c + 1) % 2, nhold))
                run_gens(gens)
                hold = nhold
        for c in (range(NCH) if pas.samp else []):
            hold = {}
            run_gens([gdn_pro(c, 0, hold)])
            cs, beta, gd, bec, ecum, erev, elast, kn, kbT3, kw, el2, vt, ex = hold["res"]
            Lh = big[0]
            S.tt(Lh[0:C, 0:4, 0:C], mk(M_SL).un(1).bc([C, 4, C]), gd.un(2).bc([C, 4, C]), ALU.mult)
            S.tt(Lh[0:C, 4:8, 0:C], mk(M_UI).un(1).bc([C, 4, C]), gd.un(2).bc([C, 4, C]), ALU.mult)
            EA = big[1][0:C, 0:4, 0:C]
            EB = big[1][0:C, 4:8, 0:C]
            psa = PS()
            for h in range(4):
                S.mm(psa[0:C, h * C:(h + 1) * C], Lh[0:C, h, 0:C], mk(M_UI), True, True)
            S.act(EA, psa[0:C, 0:4 * C].re("p (h c) -> p h c", c=C), AF.Exp)
            psb = PS()
            for h in range(4):
                S.mm(psb[0:C, h * C:(h + 1) * C], Lh[0:C, 4 + h, 0:C], mk(M_SL), True, True)
            S.act(EB, psb[0:C, 0:4 * C].re("p (h c) -> p h c", c=C), AF.Exp)
            pgm = PS()
            for h in range(4):
                S.mm(pgm[0:C, h * C:(h + 1) * C], kbT3[:, h, :], kbT3[:, h, :], True, True)
            pg3 = pgm[0:C, 0:4 * C].re("p (h c) -> p h c", c=C)
            S.tt(EA, EA, mk(M_UI).un(1).bc([C, 4, C]), ALU.mult)
            paq = PS()
            for h in range(4):
                S.mm(paq[0:C, h * C:(h + 1) * C], cvo[:, 4 + h, cs], cvo[:, h, cs], True, True)
            AqT = scr[2][0:C, 0:4 * C].re("p (h c) -> p h c", c=C)
            S.tt(AqT, paq[0:C, 0:4 * C].re("p (h c) -> p h c", c=C), EA, ALU.mult)
            X, XT = EA, EB
            S.tt(X, pg3, EA, ALU.mult)
            S.tt(X, X, mk(M_SU).un(1).bc([C, 4, C]), ALU.mult)
            S.ts(X, X, -1.0, ALU.mult)
            S.tt(XT, pg3, EB, ALU.mult)
            S.tt(XT, XT, mk(M_SL).un(1).bc([C, 4, C]), ALU.mult)
            S.ts(XT, XT, -1.0, ALU.mult)
            P = pl[1][0:C, 0:4 * C].re("p (h c) -> p h c", c=C)
            S.tt(P, X, msk[0:C, M_ID, 0:C].un(1).bc([C, 4, C]), ALU.add)
            Xc, XTc = X, XT
            for lev in range(nlev):
                lastlev = (lev == nlev - 1)
                Xn = (pl[2] if lev % 2 == 0 else pl[4])[0:C, 0:4 * C].re("p (h c) -> p h c", c=C)
                XTn = (pl[3] if lev % 2 == 0 else pl[0])[0:C, 0:4 * C].re("p (h c) -> p h c", c=C)
                pxt = PS()
                for h in range(4):
                    S.mm32(pxt[0:C, h * C:(h + 1) * C], Xc[:, h, :], XTc[:, h, :], True, True)
                S.cp(XTn, pxt[0:C, 0:4 * C].re("p (h c) -> p h c", c=C), "act")
                if not lastlev:
                    px_ = PS()
                    for h in range(4):
                        S.mm32(px_[0:C, h * C:(h + 1) * C], XTc[:, h, :], Xc[:, h, :], True, True)
                    S.cp(Xn, px_[0:C, 0:4 * C].re("p (h c) -> p h c", c=C), "dve")
                pp = PS()
                for h in range(4):
                    S.mm32(pp[0:C, h * C:(h + 1) * C], msk[0:C, M_ID, 0:C], P[:, h, :], True, False)
                    S.mm32(pp[0:C, h * C:(h + 1) * C], XTn[:, h, :], P[:, h, :], False, True)
                S.cp(P, pp[0:C, 0:4 * C].re("p (h c) -> p h c", c=C), "dve")
                Xc, XTc = Xn, XTn
            if c == 0 and l == 0 and pas.idx == 0 and not pas.samp:
                dump("d_kn", kn[0:C, :]); dump("d_beta", beta); dump("d_gd", gd); dump("d_vtm", vtm_[0:C, :])
                dump("d_P", P); dump("d_X", X); dump("d_AqT", AqT); dump("d_ex", ex[0:C, 0:12])
            vb = big[0][0:C, 0:4, :]
            kbd = big[0][0:C, 4:8, :]
            S.tt(vb, vtm_[0:C, :].re("p (h w) -> p h w", w=128), beta.un(2).bc([C, 4, 128]), ALU.mult)
            S.tt(kbd, kn[0:C, :].re("p (h w) -> p h w", w=128), bec.un(2).bc([C, 4, 128]), ALU.mult)
            pw = PS()
            for h in range(4):
                S.mm32(pw[:, h * C:(h + 1) * C], kbd[:, h, :], P[:, h, :], True, True)
            WT = scr[3][:, 0:4 * C].re("p (h c) -> p h c", c=C)
            S.ts(WT, pw[:, 0:4 * C].re("p (h c) -> p h c", c=C), -1.0, ALU.mult)
            vnew = scr[4]
            kw = scr[5]
            S.tt(kw[0:C, :].re("p (h w) -> p h w", w=128), kn[0:C, :].re("p (h w) -> p h w", w=128),
                 erev.un(2).bc([C, 4, 128]), ALU.mult)
            o_sb = pl[2][0:C, :]
            if not pas.samp:
                gz2 = smr[:, 8:16].re("p (b h) -> p b h", h=4)
                for b, (r0, r1) in enumerate(blocks):
                    S.ts(gz2[:, b, :], gd, msk[:, M_GBLK, r0:r0 + 1], ALU.mult)
                pe2 = PS()
                S.mm(pe2[:, 0:8], msk[:, M_BLK, :], smr[:, 8:16], True, True)
                el2 = sm[:, 48:56].re("p (b h) -> p b h", h=4)
                S.act(sm[:, 48:56], pe2[:, 0:8], AF.Exp)
            for b, (r0, r1) in enumerate(blocks):
                rs = slice(r0, r1)
                pvn = PS()
                pqs = PS()
                if pas.samp:
                    S.mm(pqs[0:C, :], zt[:, 0:C], msk[:, 0:4, :].re("p a b -> p (a b)"), True, True)
                for h in range(4):
                    hs = slice(h * 128, (h + 1) * 128)
                    S.mm32(pvn[0:C, hs], P[:, h, :], vb[:, h, :], True, False)
                    if not pas.samp:
                        S.mm32(pvn[0:C, hs], WT[:, h, :], Sd[:, h, :], False, True)
                    else:
                        pad = build_pad(s3(WT[:, h, :]))
                        for i in range(NSEQ):
                            S.mm32(pvn[0:C, hs], pad[:, i, :], sts[i][:, hs], False, i == NSEQ - 1)
                evac(vnew[rs, :], pvn[rs, :])
                for h in range(4):
                    hs = slice(h * 128, (h + 1) * 128)
                    if not pas.samp:
                        S.mm(pqs[0:C, hs], cvo[:, h, cs], Sd[:, h, :], True, True)
                    else:
                        pad = build_pad(s3(cvo[:, h, 0:T]))
                        for i in range(NSEQ):
                            S.mm(pqs[0:C, hs], pad[:, i, :], sts[i][:, hs], False, i == NSEQ - 1)
                S.tt(o_sb[rs, :].re("p (h w) -> p h w", w=128), pqs[rs, :].re("p (h w) -> p h w", w=128),
                     ecum[rs, :].un(2).bc([r1 - r0, 4, 128]), ALU.mult)
                if not pas.samp:
                    pu = PS()
                    for h in range(4):
                        hs = slice(h * 128, (h + 1) * 128)
                        S.mm(pu[:, hs], kw[rs, hs], vnew[rs, hs], True, True)
                    t3 = pl[4]
                    S.tt(t3.v().re("p (h w) -> p h w", w=128), Sd.v(), el2[:, b, :].un(2).bc([128, 4, 128]), ALU.mult)
                    S.tt(Sd.v().re("p h w -> p (h w)"), t3.v(), pu[:, :], ALU.add)
            pav = PS()
            for h in range(4):
                hs = slice(h * 128, (h + 1) * 128)
                S.mm(pav[0:C, hs], AqT[:, h, :], vnew[0:C, hs], True, True)
            S.tt(o_sb, o_sb, pav[0:C, :], ALU.add)
            if c == 0 and l == 0 and pas.idx == 0 and not pas.samp:
                dump("d_vnew", vnew[0:C, :]); dump("d_o", o_sb)
            gate_norm_out(pas, o_sb, dgt[c][0:C, :], rv[l][0:C, RV_DNORM:RV_DNORM + 128], 4, 128, ycur[0:C, :])
            y_to_yT(pas, c, ycur)
            if pas.samp:
                seq_totals(gd, 4)

                def gdn_tail(vnew=vnew, kw=kw):
                    for i in range(NSEQ):
                        vzi = scr[1]
                        S.ts(vzi[0:C, :], vnew[0:C, :], msk[0:C, M_IND, i:i + 1], ALU.mult)
                        pu = PS()
                        for h in range(4):
                            hs = slice(h * 128, (h + 1) * 128)
                            S.mm(pu[:, hs], kw[0:C, hs], vzi[0:C, hs], True, True)
                        sn = big[1].v().re("p a b -> p (a b)")[:, (i % 2) * 512:(i % 2) * 512 + 512]
                        S.tt(sn.re("p (h v) -> p h v", v=128), sts[i].re("p (h v) -> p h v", v=128),
                             elast_all[:, i, 0:4].un(2).bc([128, 4, 128]), ALU.mult)
                        S.tt(sn, sn, pu[:, :], ALU.add)
                        S.dma_out("sp", o_sd[l, i].rearrange("h k v -> k h v"), sn.re("p (h v) -> p h v", v=128))
                        yield
                tails.append(gdn_tail())
        if pas.last and not pas.samp:
            S.dma_out("sp", o_pd[l].rearrange("h k v -> k h v"), Sd.v())

    def out_and_ffn(pas, l):
        T = pas.T
        mix = KSplit(S, carve("mix", 0, 8))
        for eb2 in range(4):
            proj_fm(w_out[l][:, eb2 * 256:(eb2 + 1) * 256], 8, 256, lambda kc: mergedT[:, kc, 0:T], T,
                    lambda j, ps, m, eb2=eb2: evac(mix[:, eb2 * 2 + j, 0:T], ps))
        postnorm_residual(l, 0, pas, mix)
        if "ffn" not in stages:
            return
        prenorm(l, 3, pas)
        fT = KSplit(S, carve("fT", 0, NSLOT))
        for f in range(22):
            wa = wload(w3(w_ffn_in[l][:, f * 128:(f + 1) * 128]), 8, 128)
            wb_ = wload(w3(w_ffn_in[l][:, FH + f * 128:FH + (f + 1) * 128]), 8, 128)
            pa = PS()
            for kc in range(8):
                S.mm(pa[:, 0:T], wa[:, kc, :], hT[:, kc, 0:T], kc == 0, kc == 7)
            pb = PS()
            for kc in range(8):
                S.mm(pb[:, 0:T], wb_[:, kc, :], hT[:, kc, 0:T], kc == 0, kc == 7)
            sg = pl[2 + f % 3][:, 0:T]
            S.act(sg, pa[:, 0:T], AF.Tanh, scale=0.5)
            S.stt(sg, sg, 1.0, pa[:, 0:T], ALU.add, ALU.mult)
            S.stt(fT[:, f, 0:T], sg, 0.5, pb[:, 0:T], ALU.mult, ALU.mult)
        for ep in range(4):
            pso = [PS(), PS()]
            for kg in range(3):
                nf = 8 if kg < 2 else 6
                wo = wload(w_ffn_out[l][kg * 1024:kg * 1024 + nf * 128, ep * 256:(ep + 1) * 256].rearrange("(kc p) n -> p kc n", p=128), nf, 256)
                for jj in range(2):
                    for fc in range(nf):
                        f = kg * 8 + fc
                        S.mm(pso[jj][:, 0:T], wo[:, fc, jj * 128:(jj + 1) * 128], fT[:, f, 0:T], f == 0, f == 21)
            for jj in range(2):
                evac(mergedT[:, ep * 2 + jj, 0:T], pso[jj][:, 0:T])
        postnorm_residual(l, 3, pas, mergedT)

    passes = [Pass(False, i, npass) for i in range(npass)]
    if do_sample:
        passes.append(Pass(True, 0, npass))
    for pas in passes:
        T = pas.T
        for kc in range(8):
            if pas.samp:
                S.dma_in("sp", xT[:, kc, 0:T], xTs[kc * 128:(kc + 1) * 128, :])
            else:
                S.dma_in("sp", xT[:, kc, 0:T], xTp[kc * 128:(kc + 1) * 128, pas.idx * 512:(pas.idx + 1) * 512])
        for l in range(nlayers):
            prenorm(l, 0, pas)
            if dbg and pas.idx == 0 and l == 0 and not pas.samp:
                dump("hT0", hT.v())
            first = True
            for n, (name, fn) in enumerate((("gla", gla), ("ssd", ssd), ("gdn", gdn))):
                if name in stages:
                    del tails[:]
                    fn(pas, l)
                    if dbg and pas.idx == 0 and l == 0 and not pas.samp:
                        dump("yT_" + name, yT.v())
                    if dbg and pas.samp and l == 0:
                        dump("yTs_" + name, yT[:, :, 0:64])
                    run_gens([branch_merge(l, n, pas, first)] + list(tails))
                    del tails[:]
                    first = False
            out_and_ffn(pas, l)
        for kc in range(8):
            if pas.samp:
                S.dma_out("sp", yTs[kc * 128:(kc + 1) * 128, :], xT[:, kc, 0:T])
            else:
                S.dma_out("sp", yTp[kc * 128:(kc + 1) * 128, pas.idx * 512:(pas.idx + 1) * 512], xT[:, kc, 0:T])
    S.emit()
    es.close()
    return nc


def _c(a):
    return np.ascontiguousarray(a, dtype=np.float32)


def make_in_maps(inp):
    masks = _masks()
    rowv = np.concatenate([inp["g_gla_norm"], inp["ssd_dt_bias"], inp["ssd_a_log"], inp["ssd_d"], inp["g_ssd_norm"],
                           inp["gdn_dt_bias"], inp["gdn_a_log"], inp["g_gdn_norm"]], axis=1)[:, None, :]
    shared = {
        "w_ada": _c(inp["w_ada"]),
        "b_adaT": _c(inp["b_ada"].reshape(2, 48, 128).transpose(0, 2, 1)),
        "gainsT": _c(np.stack([inp["g_pre_mix"], inp["g_post_mix"], inp["g_pre_ffn"], inp["g_post_ffn"]], axis=1)
                     .reshape(2, 4, 8, 128).transpose(0, 3, 1, 2)),
        "w_in": _c(inp["w_in"]),
        "w_gate": _c(inp["w_gla_gate"]),
        "b_gate": _c(inp["b_gla_gate"][:, None, :]),
        "rowvec": _c(rowv),
        "w_sconvT": _c(inp["w_ssd_conv"].reshape(2, 4, 8, 128).transpose(0, 3, 2, 1)),
        "b_sconvT": _c(inp["b_ssd_conv"].reshape(2, 8, 128).transpose(0, 2, 1)),
        "w_dconvT": _c(inp["w_gdn_conv"].reshape(2, 4, 12, 128).transpose(0, 3, 2, 1)),
        "w_branch": _c(inp["w_branch"]),
        "w_out": _c(inp["w_out"]),
        "w_ffn_in": _c(inp["w_ffn_in"]),
        "w_ffn_out": _c(inp["w_ffn_out"]),
        "cst": masks,
    }
    maps = []
    for i in range(NCORE):
        sl = slice(NSEQ * i, NSEQ * (i + 1))
        m = dict(shared)
        m["xTp"] = _c(inp["x_prompt"][i].T)
        m["xTs"] = _c(inp["x_sample"][sl].reshape(NSEQ * LS, D).T)
        m["cT"] = _c(np.concatenate([inp["c_prompt"][i:i + 1], inp["c_sample"][sl]], axis=0).T)
        m["st_gla"] = _c(inp["state_gla"][:, sl])
        m["st_ssd"] = _c(inp["state_ssd"][:, sl])
        m["cv_ssd"] = _c(inp["cache_ssd_conv"][:, sl])
        m["st_gdn"] = _c(inp["state_gdn"][:, sl])
        m["cv_gdn"] = _c(inp["cache_gdn_conv"][:, sl])
        maps.append(m)
    return maps


def gather(results):
    n = len(results)
    y_p = np.stack([r["yTp"].T for r in results])
    y_s = np.concatenate([r["yTs"].T.reshape(NSEQ, LS, D) for r in results])

    def cat1(k):
        return np.concatenate([r[k][:, None] for r in results], axis=1)

    def cats(k):
        return np.concatenate([r[k] for r in results], axis=1)

    outs = (y_p, y_s, cat1("o_pg"), cat1("o_ps"), cat1("o_pcs"), cat1("o_pd"), cat1("o_pcd"),
            cats("o_sg"), cats("o_ss"), cats("o_scs"), cats("o_sd"), cats("o_scd"))
    return tuple(np.ascontiguousarray(o, dtype=np.float32) for o in outs)


def kernel(**inputs):
    inp = {k: np.asarray(v) for k, v in inputs.items()}
    nc = build()
    maps = make_in_maps(inp)
    res = run_bass_kernel_spmd(nc, maps, core_ids=list(range(NCORE)))
    return gather(res.results)
```

```python
import math
from contextlib import ExitStack
import numpy as np
import concourse.bass as bass
import concourse.mybir as mybir
from concourse.bass_utils import run_bass_kernel_spmd

F32 = mybir.dt.float32
F32R = mybir.dt.float32r
AF = mybir.ActivationFunctionType
ALU = mybir.AluOpType
AX = mybir.AxisListType

ENGS = ("pe", "act", "dve", "pool", "sp")
EPS = 1e-6
NCORE = 8
SEQ = 2048
NSEQ = 16
LS = 4
D = 1024
NIN = 8736
O_GQ, O_GK, O_GV, O_GR, O_GLR = 0, 512, 1024, 1536, 2048
O_SZ, O_SXBC, O_SDT = 2064, 2576, 3600
O_DQKV, O_DA, O_DB, O_DG, O_MG = 3608, 5144, 5148, 5152, 5664
FH = 2816
RV_GLAN, RV_SDTB, RV_SALOG, RV_SD, RV_SNORM, RV_DDTB, RV_DALOG, RV_DNORM, RV_N = 0, 128, 136, 144, 152, 664, 668, 672, 800
M_UI, M_SL, M_SU, M_BLK, M_GUI, M_GSL, M_ID = 0, 1, 2, 3, 4, 5, 6
M_GSU, M_GBLK = 11, 12
M_SOFF = 7
M_IND = 13
NMASK = 14


class V:
    __slots__ = ("buf", "ap")

    def __init__(self, buf, ap):
        self.buf = buf
        self.ap = ap

    def __getitem__(self, k):
        return V(self.buf, self.ap[k])

    def f32(self):
        return V(self.buf, self.ap.bitcast(F32))

    def r(self):
        return V(self.buf, self.ap.bitcast(F32R))

    def bc(self, shape):
        return V(self.buf, self.ap.broadcast_to(list(shape)))

    def un(self, axis):
        return V(self.buf, self.ap.unsqueeze(axis))

    def re(self, pat, **kw):
        return V(self.buf, self.ap.rearrange(pat, **kw))

    @property
    def shape(self):
        return self.ap.shape


class Buf:
    __slots__ = ("base", "name", "w", "r", "dsem", "dcnt", "rr", "fam", "dkey")

    def __init__(self, base_ap, name, rr=False):
        self.base = base_ap
        self.name = name
        self.rr = rr
        self.fam = []
        self.dkey = None
        self.w = None
        self.r = {}
        self.dsem = None
        self.dcnt = 0

    def __getitem__(self, k):
        return V(self, self.base[k])

    def v(self):
        return V(self, self.base)


class KSplit:
    def __init__(self, sched, buf):
        self.buf = buf
        self.ch = [sched.sub(buf, buf.base[:, k, :], "%s_k%d" % (buf.name, k)) for k in range(buf.base.shape[1])]

    def __getitem__(self, key):
        if isinstance(key, tuple) and len(key) == 3 and isinstance(key[1], int):
            return self.ch[key[1]][key[0], key[2]]
        return self.buf[key]

    def v(self):
        return self.buf.v()


class Ins:
    __slots__ = ("eng", "fn", "deps", "need", "val", "dma", "dbuf", "dval", "pos", "dk")

    def __init__(self, eng, fn):
        self.eng = eng
        self.fn = fn
        self.deps = []
        self.need = False
        self.val = 0
        self.dma = False
        self.dbuf = None
        self.dval = 0
        self.pos = 0


class Sched:
    def __init__(self, nc, es):
        self.nc = nc
        self.es = es
        self.q = {e: [] for e in ENGS}
        self.order = []
        self.nalias = 0
        self.dcnt = {}
        self.zsrc = self.sb("zsrc", [128, 1])
        self.memset(self.zsrc.v(), 0.0)

    def sb(self, name, shape, rr=False):
        t = self.es.enter_context(self.nc.sbuf_tensor(name, list(shape), F32))
        return Buf(t[:], name, rr)

    def psum(self, name, shape, dt=F32):
        t = self.es.enter_context(self.nc.psum_tensor(name, list(shape), dt))
        return Buf(t[:], name)

    def sub(self, parent, ap, name):
        b = Buf(ap, name, parent.rr)
        b.fam = [parent]
        parent.fam.append(b)
        return b

    def region(self, ap, name, parents=(), rr=False):
        b = Buf(ap, name, rr)
        for p0 in parents:
            for p in [p0] + p0.fam:
                if p.w is not None:
                    self.nalias += 1
                    b.r[("a", self.nalias)] = p.w
                for x in p.r.values():
                    self.nalias += 1
                    b.r[("a", self.nalias)] = x
        return b

    def _add(self, eng, fn, reads, writes, dma=False, dbuf=None):
        ins = Ins(eng, fn)
        ins.dma = dma
        deps = []
        for b0 in reads:
            for b in [b0] + b0.fam:
                if b.w is not None:
                    deps.append(b.w)
        for b0 in writes:
            for b in [b0] + b0.fam:
                if b.w is not None:
                    deps.append(b.w)
                deps.extend(b.r.values())
        seen = set()
        for d in deps:
            if id(d) not in seen and d is not ins:
                seen.add(id(d))
                ins.deps.append(d)
                d.need = True
        if dma:
            ins.dbuf = dbuf
            ins.dk = dbuf.dkey if dbuf.dkey is not None else id(dbuf)
            self.dcnt[ins.dk] = self.dcnt.get(ins.dk, 0) + 16
            ins.dval = self.dcnt[ins.dk]
        for b in reads:
            b.r[("d", id(ins)) if dma else eng] = ins
        for b in writes:
            b.w = ins
            b.r = {}
        ins.pos = len(self.order)
        self.q[eng].append(ins)
        self.order.append(ins)
        return ins

    @staticmethod
    def _bufs(vs):
        out = []
        for v in vs:
            if isinstance(v, V) and v.buf not in out:
                out.append(v.buf)
        return out

    def op(self, eng, fn, outs, ins):
        return self._add(eng, fn, self._bufs(ins), self._bufs(outs))

    @staticmethod
    def _o(v):
        return v.ap.bitcast(F32R) if v.buf.rr else v.ap

    def mm(self, out, lhsT, rhs, start=True, stop=True, tp=None):
        o, a, b = out.ap, lhsT.ap.bitcast(F32R), rhs.ap.bitcast(F32R)
        assert lhsT.buf.rr and rhs.buf.rr, (lhsT.buf.name, rhs.buf.name)
        if tp is None:
            return self.op("pe", lambda e: e.matmul(o, a, b, start=start, stop=stop), [out], [lhsT, rhs])
        return self.op("pe", lambda e: e.matmul(o, a, b, start=start, stop=stop, tile_position=tp), [out], [lhsT, rhs])

    def mm32(self, out, lhsT, rhs, start=True, stop=True, tp=None):
        o, a, b = out.ap, lhsT.ap, rhs.ap
        if tp is None:
            return self.op("pe", lambda e: e.matmul(o, a, b, start=start, stop=stop), [out], [lhsT, rhs])
        return self.op("pe", lambda e: e.matmul(o, a, b, start=start, stop=stop, tile_position=tp), [out], [lhsT, rhs])

    def act(self, out, in_, func, bias=0.0, scale=1.0, accum=None):
        o, i = self._o(out), in_.ap
        b = bias.ap if isinstance(bias, V) else bias
        s = scale.ap if isinstance(scale, V) else scale
        ac = accum.ap if accum is not None else None
        outs = [out] + ([accum] if accum is not None else [])
        return self.op("act", lambda e: e.activation(o, i, func, bias=b, scale=s, accum_out=ac),
                       outs, [in_, bias, scale])

    def tt(self, out, a, b, op, eng="dve"):
        o, x, y = self._o(out), a.ap, b.ap
        return self.op(eng, lambda e: e.tensor_tensor(o, x, y, op), [out], [a, b])

    def ts(self, out, a, s1, op0, s2=None, op1=None, eng="dve"):
        o, x = self._o(out), a.ap
        p1 = s1.ap if isinstance(s1, V) else s1
        p2 = s2.ap if isinstance(s2, V) else s2
        if op1 is None:
            return self.op(eng, lambda e: e.tensor_scalar(o, x, p1, None, op0), [out], [a, s1])
        return self.op(eng, lambda e: e.tensor_scalar(o, x, p1, p2, op0, op1), [out], [a, s1, s2])

    def stt(self, out, a, sc, b, op0, op1):
        o, x, y = self._o(out), a.ap, b.ap
        p = sc.ap if isinstance(sc, V) else sc
        return self.op("dve", lambda e: e.scalar_tensor_tensor(o, x, p, y, op0, op1), [out], [a, sc, b])

    def red(self, out, in_, op=None):
        o, i = self._o(out), in_.ap
        op = op or ALU.add
        return self.op("dve", lambda e: e.tensor_reduce(o, i, AX.X, op), [out], [in_])

    def cp(self, out, in_, eng="dve"):
        o, i = self._o(out), in_.ap
        if eng == "act":
            return self.op("act", lambda e: e.copy(o, i), [out], [in_])
        return self.op(eng, lambda e: e.tensor_copy(o, i), [out], [in_])

    def recip(self, out, in_):
        o, i = self._o(out), in_.ap
        return self.op("dve", lambda e: e.reciprocal(o, i), [out], [in_])

    def memset(self, out, val, eng="dve"):
        if out.buf.rr:
            z = self.zsrc
            shp = list(out.ap.shape)
            zin = bass.AP(z.base.tensor, z.base.offset, [[z.base.ap[0][0], shp[0]]] + [[0, n] for n in shp[1:]])
            return self.ts(out, V(z, zin), float(val), ALU.add)
        o = out.ap
        return self.op(eng, lambda e: e.memset(o, val), [out], [])

    def dma_in(self, eng, out, src_ap, **kw):
        o = self._o(out)
        if out.buf.rr:
            eng = "pool"
        return self._add(eng, lambda e: e.dma_start(out=o, in_=src_ap, **kw), [], [out.buf], dma=True, dbuf=out.buf)

    def dma_out(self, eng, dst_ap, in_, **kw):
        i = in_.ap
        return self._add(eng, lambda e: e.dma_start(out=dst_ap, in_=i, **kw), [in_.buf], [], dma=True, dbuf=in_.buf)

    def emit(self):
        nc, es = self.nc, self.es
        sems = {e: es.enter_context(nc.semaphore("s_" + e)) for e in ENGS}
        dsems = {}
        for ins in self.order:
            if ins.dma and ins.dk not in dsems:
                dsems[ins.dk] = es.enter_context(nc.semaphore("d%d" % len(dsems)))
        for e in ENGS:
            c = 0
            for ins in self.q[e]:
                if not ins.dma and ins.need:
                    c += 1
                    ins.val = c
        hist = {}
        for ins in self.order:
            if ins.dma:
                hist.setdefault(ins.dk, []).append((ins.pos, ins.dval))
        engobj = {"pe": nc.tensor, "act": nc.scalar, "dve": nc.vector, "pool": nc.gpsimd, "sp": nc.sync}
        block = es.enter_context(nc.Block())

        def run(e):
            eo = engobj[e]
            waited = {}
            for ins in self.q[e]:
                need = {}
                for d in ins.deps:
                    if d.dma:
                        hl = hist[d.dk]
                        lo, hi = 0, len(hl)
                        while lo < hi:
                            mid = (lo + hi) // 2
                            if hl[mid][0] < ins.pos:
                                lo = mid + 1
                            else:
                                hi = mid
                        v = hl[lo - 1][1] if lo > 0 else 0
                        key = ("d", d.dk)
                        sem = dsems[d.dk]
                    else:
                        if d.eng == e and e == "pe":
                            continue
                        v = d.val
                        key = d.eng
                        sem = sems[d.eng]
                    if waited.get(key, 0) >= v:
                        continue
                    if key not in need or need[key][1] < v:
                        need[key] = (sem, v)
                for key, (sem, v) in need.items():
                    eo.wait_ge(sem, v)
                    waited[key] = v
                bi = ins.fn(eo)
                if ins.dma:
                    bi.then_inc(dsems[ins.dk], 16)
                elif ins.need:
                    bi.then_inc(sems[e], 1)
            if e == "sp":
                for k, sem in dsems.items():
                    eo.wait_ge(sem, self.dcnt[k])

        @block.tensor
        def _(x):
            run("pe")

        @block.scalar
        def _(x):
            run("act")

        @block.vector
        def _(x):
            run("dve")

        @block.gpsimd
        def _(x):
            run("pool")

        @block.sync
        def _(x):
            run("sp")


def _masks():
    r = np.arange(128)[:, None]
    c = np.arange(128)[None, :]
    m = np.zeros((NMASK, 128, 128), np.float32)
    m[M_UI] = (r <= c)
    m[M_SL] = (r > c)
    m[M_SU] = (r < c)
    m[M_BLK] = 1.0
    blk64 = ((r // 64) == (c // 64))
    m[M_GUI] = m[M_UI] * blk64
    m[M_GSL] = m[M_SL] * blk64
    m[M_GSU] = m[M_SU] * blk64
    m[M_GBLK] = blk64
    m[M_ID] = (r == c)
    same = ((r // LS) == (c // LS)) & (r < 64) & (c < 64)
    for k in (M_UI, M_SL, M_SU, M_BLK):
        m[M_SOFF + k] = m[k] * same
    m[M_IND] = ((r // LS) == c) & (c < NSEQ) & (r < 64)
    return m


class Pass:
    def __init__(self, samp, idx, npass):
        self.samp = samp
        self.idx = idx
        self.T = 64 if samp else 512
        self.C = 64 if samp else 128
        self.NCH = 1 if samp else 4
        self.moff = M_SOFF if samp else 0
        self.first = (idx == 0)
        self.last = samp or idx == npass - 1


def build(nlayers=2, npass=4, do_sample=True, stages=("gla", "ssd", "gdn", "ffn"), dbg=None):
    nc = bass.Bass("TRN2", target_bir_lowering=False)

    def din(name, shape):
        return nc.dram_tensor(name, list(shape), F32, kind="ExternalInput").ap()

    def dout(name, shape):
        return nc.dram_tensor(name, list(shape), F32, kind="ExternalOutput").ap()

    xTp = din("xTp", [D, SEQ])
    xTs = din("xTs", [D, 64])
    cT = din("cT", [D, 17])
    st_gla = din("st_gla", [2, NSEQ, 4, 128, 128])
    st_ssd = din("st_ssd", [2, NSEQ, 8, 64, 128])
    cv_ssd = din("cv_ssd", [2, NSEQ, 3, 1024])
    st_gdn = din("st_gdn", [2, NSEQ, 4, 128, 128])
    cv_gdn = din("cv_gdn", [2, NSEQ, 3, 1536])
    w_ada = din("w_ada", [2, D, 6144])
    b_adaT = din("b_adaT", [2, 128, 48])
    gainsT = din("gainsT", [2, 128, 4, 8])
    w_in = din("w_in", [2, D, NIN])
    w_gate = din("w_gate", [2, 16, 512])
    b_gate = din("b_gate", [2, 1, 512])
    rowvec = din("rowvec", [2, 1, RV_N])
    w_sconvT = din("w_sconvT", [2, 128, 8, 4])
    b_sconvT = din("b_sconvT", [2, 128, 8])
    w_dconvT = din("w_dconvT", [2, 128, 12, 4])
    w_branch = din("w_branch", [2, 3, 512, D])
    w_out = din("w_out", [2, D, D])
    w_ffn_in = din("w_ffn_in", [2, D, 2 * FH])
    w_ffn_out = din("w_ffn_out", [2, FH, D])
    cst = din("cst", [NMASK, 128, 128])

    yTp = dout("yTp", [D, SEQ])
    yTs = dout("yTs", [D, 64])
    o_pg = dout("o_pg", [2, 4, 128, 128])
    o_ps = dout("o_ps", [2, 8, 64, 128])
    o_pcs = dout("o_pcs", [2, 3, 1024])
    o_pd = dout("o_pd", [2, 4, 128, 128])
    o_pcd = dout("o_pcd", [2, 3, 1536])
    o_sg = dout("o_sg", [2, NSEQ, 4, 128, 128])
    o_ss = dout("o_ss", [2, NSEQ, 8, 64, 128])
    o_scs = dout("o_scs", [2, NSEQ, 3, 1024])
    o_sd = dout("o_sd", [2, NSEQ, 4, 128, 128])
    o_scd = dout("o_scd", [2, NSEQ, 3, 1536])
    dbg_out = {}
    if dbg:
        for name, shape in dbg.items():
            dbg_out[name] = dout("dbg_" + name, shape)

    es = ExitStack()
    S = Sched(nc, es)
    es.enter_context(nc.allow_low_precision(reason="fp32r-rounded PE operands"))

    def dump(name, v):
        if name in dbg_out:
            S.dma_out("sp", dbg_out[name], v)

    msk = S.sb("msk", [128, NMASK, 128], True)
    S.dma_in("pool", msk.v(), cst.rearrange("m p c -> p m c"))
    xT = KSplit(S, S.sb("xT", [128, 8, 512]))
    hT = KSplit(S, S.sb("hT", [128, 8, 512], True))
    NSLOT = 23
    arena_t = es.enter_context(nc.sbuf_tensor("arena", [128, NSLOT, 512], F32))
    wbufs = [S.sb("wb%d" % i, [128, 8, 256], True) for i in range(3)]
    whalf = []
    for i in range(3):
        for k in range(2):
            whalf.append(S.sub(wbufs[i], wbufs[i].base[:, :, k * 128:(k + 1) * 128], "wh%d_%d" % (i, k)))
    banks = [S.psum("pb%d" % i, [128, 512], F32) for i in range(8)]
    dmod = [S.sb("dmod%d" % l, [128, 6, 8, 17]) for l in range(2)]
    gains = S.sb("gains", [128, 2, 4, 8])
    badd = S.sb("badd", [128, 2, 48])
    rv = [S.sb("rv%d" % l, [128, RV_N]) for l in range(2)]
    negA = S.sb("negA", [128, 2, 12])
    wsc = S.sb("wsc", [128, 2, 8, 4])
    bsc = S.sb("bsc", [128, 2, 8])
    wdc = S.sb("wdc", [128, 2, 12, 4])
    wga = S.sb("wga", [17, 512], True)
    glrA = S.sb("glrA", [17, 512], True)
    sT = S.sb("sT", [128, 8, 18], True)
    S_gla = [S.sb("S_gla%d" % l, [128, 4, 128], True) for l in range(2)]
    S_ssd = [S.sb("S_ssd%d" % l, [128, 8, 64], True) for l in range(2)]
    S_gdn = [S.sb("S_gdn%d" % l, [128, 4, 128], True) for l in range(2)]
    hist_s = [S.sb("hist_s%d" % l, [128, 8, 3]) for l in range(2)]
    hist_d = [S.sb("hist_d%d" % l, [128, 12, 3]) for l in range(2)]
    mergedT = KSplit(S, S.sb("mergedT", [128, 8, 512], True))
    yT = S.sb("yT", [128, 4, 512], True)
    scr = [S.sb("scr%d" % i, [128, 512], True) for i in range(7)]
    pl = [S.sb("pl%d" % i, [128, 512]) for i in range(5)]
    big = [S.sb("big%d" % i, [128, 8, 128], i != 1) for i in range(3)]
    hist_in = V(big[1], big[1].base.rearrange("p a b -> p (a b)")[:, 0:576].rearrange("p (t b j) -> p t b j", b=NSEQ, j=3))
    modt = V(big[1], big[1].base.rearrange("p a b -> p (a b)")[:, 0:816].rearrange("p (j n) -> p j n", n=17))
    sm = S.sb("sm", [128, 128])
    padz = S.sb("padz", [128, NSEQ, 64], True)
    zt = S.sb("zt", [128, 64], True)
    elast_all = S.sb("elast_all", [128, NSEQ, 8])
    gz = S.sb("gz", [64, NSEQ, 8], True)
    ecol = S.sb("ecol", [128, 4, NSEQ])
    smallT = S.sb("smallT", [128, 32])
    smr = S.sb("smr", [128, 16], True)
    sm2 = S.sb("sm2", [128, 16])
    sm_b = S.sb("sm_b", [128, 128])
    smr_b = S.sb("smr_b", [128, 16], True)
    yT_h = [S.sub(yT, yT.base[:, 2 * k:2 * k + 2, :], "yT_h%d" % k) for k in range(2)]

    gsub = {}
    tails = []
    state = {"arsem": 0, "ps": 0, "wb": 0, "arena": [], "ev": 0, "pad": 0, "pinned": set()}

    def PS(pin=False):
        while (state["ps"] % 8) in state["pinned"]:
            state["ps"] += 1
        k = state["ps"] % 8
        state["ps"] += 1
        if pin:
            state["pinned"].add(k)
        return banks[k]

    def unpin(b):
        state["pinned"].discard(banks.index(b))

    def carve(name, s0, n):
        assert s0 + n <= NSLOT, (name, s0, n)
        parents = [b for (b, a0, a1) in state["arena"] if a0 < s0 + n and s0 < a1]
        state["arena"] = [(b, a0, a1) for (b, a0, a1) in state["arena"] if not (a0 < s0 + n and s0 < a1)]
        b = S.region(arena_t[:, s0:s0 + n, :], name, parents, True)
        b.dkey = ("ar", state["arsem"] % 24)
        state["arsem"] += 1
        state["arena"].append((b, s0, s0 + n))
        return b

    class Bump:
        def __init__(self):
            self.n = 0

        def fm(self, name, ntile, pas):
            if pas.samp:
                ns = (ntile * 64 + 511) // 512
                b = carve(name, self.n, ns)
                self.n += ns
                v = V(b, b.base.rearrange("p s c -> p (s c)")[:, 0:ntile * 64].rearrange("p (n t) -> p n t", t=64))
                return v
            b = carve(name, self.n, ntile)
            self.n += ntile
            return b.v()

        def raw(self, name, n):
            b = carve(name, self.n, n)
            self.n += n
            return b

        def tm(self, name, n=1):
            out = []
            for i in range(n):
                b = carve("%s%d" % (name, i), self.n, 1)
                self.n += 1
                out.append(b[:, 0, :])
            return out

    def M(pas, idx, rows=None, cols=None):
        C = pas.C
        k = idx if idx in (M_ID, M_IND) else pas.moff + idx
        return msk[0:(rows or C), k, 0:(cols or C)]

    def wload(src3, nk, ncols):
        if ncols <= 128:
            b = whalf[state["wb"] % 6]
            state["wb"] += 1
        else:
            if state["wb"] % 2:
                state["wb"] += 1
            b = wbufs[(state["wb"] % 6) // 2]
            state["wb"] += 2
        S.dma_in("pool", b[:, 0:nk, 0:ncols], src3)
        return b

    def w3(w2d):
        return w2d.rearrange("(kc p) n -> p kc n", p=128)

    def run_gens(gens):
        gens = list(gens)
        while gens:
            for g_ in list(gens):
                try:
                    next(g_)
                except StopIteration:
                    gens.remove(g_)

    def evac(out, in_):
        state["ev"] += 1
        S.cp(out, in_, "act" if state["ev"] % 2 else "dve")

    def proj_fm(w2d, nk, ncols, rhs_fn, T, sink):
        wb = wload(w3(w2d), nk, ncols)
        for j in range((ncols + 127) // 128):
            m = min(128, ncols - j * 128)
            ps = PS()
            for kc in range(nk):
                S.mm(ps[0:m, 0:T], wb[:, kc, j * 128:j * 128 + m], rhs_fn(kc), kc == 0, kc == nk - 1)
            sink(j, ps[0:m, 0:T], m)

    def proj_fm_wide(w2d, ncols, T, sink):
        for b0 in range(0, ncols, 256):
            nb = min(256, ncols - b0)
            proj_fm(w2d[:, b0:b0 + nb], 8, nb, lambda kc: hT[:, kc, 0:T], T,
                    lambda j, ps, m, b0=b0: sink(b0 // 128 + j, ps))

    def proj_tm(w2d, ncols, pas, sink):
        wb = wload(w3(w2d), 8, ncols)
        for c in range(pas.NCH):
            ps = PS()
            for kc in range(8):
                S.mm(ps[0:pas.C, 0:ncols], hT[:, kc, c * pas.C:(c + 1) * pas.C], wb[:, kc, 0:ncols], kc == 0, kc == 7)
            sink(c, ps[0:pas.C, 0:ncols])

    def proj_tm_wide(w2d, ncols, pas, dst, rnd):
        for b0 in range(0, ncols, 256):
            nb = min(256, ncols - b0)

            def sink(c, ps, b0=b0, nb=nb):
                o = dst[c][0:pas.C, b0:b0 + nb]
                evac(o if rnd else o, ps)
            proj_tm(w2d[:, b0:b0 + nb], nb, pas, sink)

    S.dma_in("sp", gains.v(), gainsT.rearrange("l p w k -> p l w k"))
    S.dma_in("sp", badd.v(), b_adaT.rearrange("l p j -> p l j"))
    S.dma_in("sp", wsc.v(), w_sconvT.rearrange("l p t j -> p l t j"))
    S.dma_in("sp", bsc.v(), b_sconvT.rearrange("l p t -> p l t"))
    S.dma_in("sp", wdc.v(), w_dconvT.rearrange("l p t j -> p l t j"))
    S.memset(glrA.v(), 1.0)
    S.memset(padz.v(), 0.0)
    S.memset(zt.v(), 0.0)
    for l in range(2):
        S.dma_in("sp", rv[l].v(), rowvec[l].partition_broadcast(128).rearrange("p o n -> p (o n)"))
        S.act(negA[:, l, 0:8], rv[l][:, RV_SALOG:RV_SALOG + 8], AF.Exp)
        S.act(negA[:, l, 8:12], rv[l][:, RV_DALOG:RV_DALOG + 4], AF.Exp)
        S.memset(hist_s[l].v(), 0.0)
        S.memset(hist_d[l].v(), 0.0)
        S.memset(S_gla[l].v(), 0.0)
        S.memset(S_ssd[l].v(), 0.0)
        S.memset(S_gdn[l].v(), 0.0)
    S.ts(negA.v(), negA.v(), -1.0, ALU.mult)
    S.ts(wsc.v(), wsc.v(), 0.5, ALU.mult)
    S.ts(bsc.v(), bsc.v(), 0.5, ALU.mult)
    S.ts(wdc.v(), wdc.v(), 0.5, ALU.mult)
    cin = pl[4][:, 0:136]
    ce = pl[1][:, 0:136]
    S.dma_in("sp", cin.re("p (k n) -> p k n", n=17), cT.rearrange("(k p) n -> p k n", p=128))
    S.act(ce, cin, AF.Exp, scale=-1.0)
    S.ts(ce, ce, 1.0, ALU.add)
    S.recip(ce, ce)
    S.memset(sT.v(), 0.0)
    S.tt(sT[:, :, 0:17], cin.re("p (k n) -> p k n", n=17), ce.re("p (k n) -> p k n", n=17), ALU.mult)
    for l in range(nlayers):
        ps = None
        for jb in range(24):
            wb = wload(w3(w_ada[l][:, jb * 256:(jb + 1) * 256]), 8, 256)
            for jj in range(2):
                j = jb * 2 + jj
                if j % 24 == 0:
                    ps = PS()
                for kc in range(8):
                    S.mm(ps[:, (j % 24) * 18:(j % 24) * 18 + 18], wb[:, kc, jj * 128:(jj + 1) * 128], sT[:, kc, :], kc == 0, kc == 7)
                if j % 24 == 23:
                    j0 = j - 23
                    S.tt(modt[:, j0:j0 + 24, :], ps[:, 0:432].re("p (j n) -> p j n", n=18)[:, :, 0:17],
                         badd[:, l, j0:j0 + 24].un(2).bc([128, 24, 17]), ALU.add)
        dm = dmod[l]
        for (which, sc0, sh0, gt0, gpre, gpost) in ((0, 8, 0, 16, 0, 1), (3, 32, 24, 40, 2, 3)):
            S.ts(dm[:, which, :, :], modt[:, sc0:sc0 + 8, :], 1.0, ALU.add)
            S.tt(dm[:, which, :, :], dm[:, which, :, :], gains[:, l, gpre, :].un(2).bc([128, 8, 17]), ALU.mult)
            S.cp(dm[:, which + 1, :, :], modt[:, sh0:sh0 + 8, :])
            S.tt(dm[:, which + 2, :, :], modt[:, gt0:gt0 + 8, :], gains[:, l, gpost, :].un(2).bc([128, 8, 17]), ALU.mult)
    dump("dmod0", dmod[0].v())

    def rms_fm(src_fn, T):
        ps = PS()
        for kc in range(8):
            sq = scr[kc % 2]
            S.act(sq[:, 0:T], src_fn(kc), AF.Square)
            S.mm(ps[:, 0:T], msk[:, M_BLK, :], sq[:, 0:T], kc == 0, kc == 7)
        S.act(pl[1][:, 0:T], ps[:, 0:T], AF.Ln, bias=EPS, scale=1.0 / D)
        S.act(pl[0][:, 0:T], pl[1][:, 0:T], AF.Exp, scale=-0.5)
        return pl[0][:, 0:T]

    def s3(v):
        return v.re("p (b j) -> p b j", j=LS)

    def modcol(l, which, kc, pas):
        if not pas.samp:
            return dmod[l][:, which, kc, 0:1]
        return dmod[l][:, which, kc, 1:17].un(2).bc([128, NSEQ, LS])

    def prenorm(l, which, pas):
        T = pas.T
        rstd = rms_fm(lambda kc: xT[:, kc, 0:T], T)
        for kc in range(8):
            t = pl[2 + kc % 3][:, 0:T]
            S.tt(t, xT[:, kc, 0:T], rstd, ALU.mult)
            if not pas.samp:
                S.act(hT[:, kc, 0:T], t, AF.Identity, bias=modcol(l, which + 1, kc, pas), scale=modcol(l, which, kc, pas))
            else:
                S.tt(s3(t), s3(t), modcol(l, which, kc, pas), ALU.mult)
                S.tt(s3(hT[:, kc, 0:T]), s3(t), modcol(l, which + 1, kc, pas), ALU.add)

    def postnorm_residual(l, which, pas, src):
        T = pas.T
        rstd = rms_fm(lambda kc: src[:, kc, 0:T], T)
        for kc in range(8):
            t = pl[2 + kc % 3][:, 0:T]
            S.tt(t, src[:, kc, 0:T], rstd, ALU.mult)
            if not pas.samp:
                S.stt(xT[:, kc, 0:T], t, modcol(l, which + 2, kc, pas), xT[:, kc, 0:T], ALU.mult, ALU.add)
            else:
                S.tt(s3(t), s3(t), modcol(l, which + 2, kc, pas), ALU.mult)
                S.tt(xT[:, kc, 0:T], xT[:, kc, 0:T], t, ALU.add)

    def sigmoid_from(out, src, scratch):
        S.act(scratch, src, AF.Tanh, scale=0.5)
        S.ts(out, scratch, 0.5, ALU.mult, 0.5, ALU.add)

    def silu_inplace(x, scratch, rnd=False):
        S.act(scratch, x, AF.Tanh, scale=0.5)
        S.stt(scratch, scratch, 1.0, x, ALU.add, ALU.mult)
        S.ts(x, scratch, 0.5, ALU.mult)

    def transposes(src, C, nblk, width=128):
        ps = PS()
        for j in range(nblk):
            S.mm(ps[0:width, j * C:(j + 1) * C], src[0:C, j * width:(j + 1) * width], msk[0:C, M_ID, 0:C], True, True)
        return ps[0:width, 0:nblk * C].re("p (j c) -> p j c", c=C)

    def branch_merge(l, n, pas, first):
        T = pas.T
        for e in range(8):
            wbg = wload(w3(w_in[l][:, O_MG + n * 1024 + e * 128:O_MG + n * 1024 + (e + 1) * 128]), 8, 128)
            wbb = wload(w3(w_branch[l, n][:, e * 128:(e + 1) * 128]), 4, 128)
            pg = PS()
            for kc in range(8):
                S.mm(pg[:, 0:T], wbg[:, kc, :], hT[:, kc, 0:T], kc == 0, kc == 7)
            py = PS()
            for wc in range(4):
                S.mm(py[:, 0:T], wbb[:, wc, :], yT[:, wc, 0:T], wc == 0, wc == 3)
            sg = pl[2 + e % 3][:, 0:T]
            S.act(sg, pg[:, 0:T], AF.Tanh, scale=0.5)
            S.stt(sg, sg, 1.0, py[:, 0:T], ALU.add, ALU.mult)
            if first:
                S.ts(mergedT[:, e, 0:T], sg, 0.5, ALU.mult)
            else:
                S.stt(mergedT[:, e, 0:T], sg, 0.5, mergedT[:, e, 0:T], ALU.mult, ALU.add)
            yield

    def gate_norm_out(pas, o_in, gate_tm, gn_bc, ngrp, gw, ycur):
        C = pas.C
        c1 = pl[0][0:C, :]
        c2 = pl[3][0:C, :]
        st1 = sm[0:C, 64:64 + ngrp]
        st2 = sm[0:C, 96:96 + ngrp]
        S.act(c1, o_in, AF.Square)
        S.red(st1, c1.re("p (g w) -> p g w", w=gw))
        S.act(st2, st1, AF.Ln, bias=EPS, scale=1.0 / gw)
        S.act(st1, st2, AF.Exp, scale=-0.5)
        S.tt(c1.re("p (g w) -> p g w", w=gw), o_in.re("p (g w) -> p g w", w=gw), st1.un(2).bc([C, ngrp, gw]), ALU.mult)
        S.tt(c1.re("p (g w) -> p g w", w=gn_bc.shape[-1]), c1.re("p (g w) -> p g w", w=gn_bc.shape[-1]),
             gn_bc.un(1).bc([C, 512 // gn_bc.shape[-1], gn_bc.shape[-1]]), ALU.mult)
        S.act(c2, gate_tm, AF.Tanh, scale=0.5)
        S.stt(c2, c2, 1.0, gate_tm, ALU.add, ALU.mult)
        S.stt(ycur, c2, 0.5, c1, ALU.mult, ALU.mult)

    def y_to_yT(pas, c, ycur):
        C = pas.C
        ps = transposes(ycur, C, 4)
        evac(yT[:, 0:4, c * C:(c + 1) * C], ps)

    def build_pad(src3):
        return build_pad_into(padz, padz.base, src3)

    def build_pad_into(buf, base, src3):
        dst = bass.AP(base.tensor, base.offset, [list(base.ap[0]), [64 + LS, NSEQ], [1, LS]])
        S.cp(V(buf, dst), src3)
        return V(buf, bass.AP(base.tensor, base.offset, [list(base.ap[0]), [64, NSEQ], [1, 64]]))

    def cum3(pas, G, n, ex):
        C = pas.C
        pc = PS()
        S.mm(pc[0:C, 0:n], M(pas, M_UI), G, True, True)
        S.mm(pc[0:C, n:2 * n], M(pas, M_SL), G, True, True)
        S.mm(pc[0:C, 2 * n:3 * n], M(pas, M_BLK), G, True, True)
        S.act(ex[0:C, 0:3 * n], pc[0:C, 0:3 * n], AF.Exp)

    def seq_totals(G, n):
        S.tt(gz[:, :, 0:n], msk[0:64, M_IND, 0:NSEQ].un(2).bc([64, NSEQ, n]), G.un(1).bc([64, NSEQ, n]), ALU.mult)
        pe_ = PS()
        S.mm(pe_[:, 0:NSEQ * n].re("p (i n) -> p i n", n=n), msk[0:64, M_BLK, :], gz[:, :, 0:n], True, True)
        S.act(elast_all[:, :, 0:n], pe_[:, 0:NSEQ * n].re("p (i n) -> p i n", n=n), AF.Exp)

    def conv_fm(pas, l, ps, raw, out, wcol, bcol, hist_v, scratch):
        T = pas.T
        S.cp(raw, ps, "act")
        S.act(out, ps, AF.Identity, bias=(bcol if bcol is not None else 0.0), scale=wcol(3))
        yield
        if not pas.samp:
            for s_ in (1, 2, 3):
                S.stt(out[:, s_:T], raw[:, 0:T - s_], wcol(3 - s_), out[:, s_:T], ALU.mult, ALU.add)
                S.stt(out[:, 0:s_], hist_v[:, 3 - s_:3], wcol(3 - s_), out[:, 0:s_], ALU.mult, ALU.add)
        else:
            r3, o3 = s3(raw), s3(out)
            for s_ in (1, 2, 3):
                S.stt(o3[:, :, s_:LS], r3[:, :, 0:LS - s_], wcol(3 - s_), o3[:, :, s_:LS], ALU.mult, ALU.add)
                S.stt(o3[:, :, 0:s_], hist_v[:, :, 3 - s_:3], wcol(3 - s_), o3[:, :, 0:s_], ALU.mult, ALU.add)
        yield
        S.act(scratch, out, AF.Tanh)
        S.stt(out, scratch, 1.0, out, ALU.add, ALU.mult)

    def load_hist_sample(bp, cv_l, ntile, name):
        n0 = bp.n
        nsl = (ntile * 128 + 511) // 512
        cvb_ = bp.raw(name, nsl)
        bp.n = n0
        cv2 = V(cvb_, cvb_.base.rearrange("p s c -> p (s c)"))
        S.dma_in("pool", cv2[0:48, 0:ntile * 128], cv_l.rearrange("b j c -> (b j) c"))
        for t0 in range(0, ntile, 4):
            ps = PS()
            for k in range(4):
                t = t0 + k
                S.mm(ps[:, k * 48:(k + 1) * 48], cv2[0:48, t * 128:(t + 1) * 128], msk[0:48, M_ID, 0:48], True, True)
            evac(hist_in[:, t0:t0 + 4, :, :].re("p t b j -> p t (b j)"), ps[:, 0:192].re("p (t n) -> p t n", n=48))

    def proj_conv(w2d, ncols, pas, sink, cache_out):
        T = pas.T
        win = []
        for bi, b0 in enumerate(range(0, ncols, 256)):
            wb = wload(w3(w2d[:, b0:b0 + 256]), 8, 256)
            for j in range(2):
                ps = PS()
                for kc in range(8):
                    S.mm(ps[:, 0:T], wb[:, kc, j * 128:(j + 1) * 128], hT[:, kc, 0:T], kc == 0, kc == 7)
                win.insert(0, sink(b0 // 128 + j, ps[:, 0:T]))
                for g_ in list(win):
                    try:
                        next(g_)
                    except StopIteration:
                        win.remove(g_)
            if pas.samp:
                ps2 = PS()
                for kc in range(8):
                    S.mm(ps2[0:64, 0:256], hT[:, kc, 0:64], wb[:, kc, 0:256], kc == 0, kc == 7)
                stg = scr[2 + bi % 2]
                evac(stg[0:64, 0:256], ps2[0:64, 0:256])
                for j in range(1, LS):
                    S.dma_out("sp", cache_out[:, j - 1, b0:b0 + 256], stg[j:64:LS, 0:256])
        run_gens(win)

    def gla(pas, l):
        T, C, NCH = pas.T, pas.C, pas.NCH
        bp = Bump()
        qT_ = bp.fm("g_qT", 4, pas)
        kT_ = bp.fm("g_kT", 4, pas)
        ktm = bp.tm("g_k", NCH)
        vtm = bp.tm("g_v", NCH)
        grt = bp.tm("g_gr", NCH)
        W = w_in[l]
        proj_fm_wide(W[:, O_GQ:O_GQ + 512], 512, T, lambda t, ps: evac(qT_[:, t, 0:T], ps))
        proj_fm_wide(W[:, O_GK:O_GK + 512], 512, T, lambda t, ps: evac(kT_[:, t, 0:T], ps))
        proj_tm_wide(W[:, O_GK:O_GK + 512], 512, pas, ktm, False)
        proj_tm_wide(W[:, O_GV:O_GV + 512], 512, pas, vtm, True)
        proj_tm_wide(W[:, O_GR:O_GR + 512], 512, pas, grt, False)
        proj_fm(W[:, O_GLR:O_GLR + 16], 8, 16, lambda kc: hT[:, kc, 0:T], T,
                lambda j, ps, m: evac(glrA[0:16, 0:T], ps))
        S.dma_in("pool", wga[0:16, :], w_gate[l])
        S.dma_in("pool", wga[16:17, :], b_gate[l])
        Sg = S_gla[l]
        sts = None
        if pas.samp:
            sts = bp.tm("g_S", NSEQ)
            for i in range(NSEQ):
                S.dma_in("pool", sts[i].re("p (h v) -> p h v", v=128), st_gla[l, i].rearrange("h k v -> k h v"))
        spl, ebx, enb, qdT, kdT, AmT, kd2, ycur = scr[2], pl[1], pl[2], scr[3], scr[4], scr[5], scr[6], scr[0]
        if not pas.samp:
            def gs3(key, parent, ap):
                kk = (key, id(parent))
                if kk not in gsub:
                    gsub[kk] = S.sub(parent, ap, key)
                return gsub[kk]
            gb_ = {}
            for hp in range(2):
                hc = slice(hp * 2 * C, (hp + 1) * 2 * C)
                gb_[hp] = dict(
                    EB=gs3("gEB%d" % hp, ebx, ebx.base[:, hc]),
                    ENB=gs3("gENB%d" % hp, enb, enb.base[:, hc]),
                    QD=gs3("gQD%d" % hp, qdT, qdT.base[:, hc]),
                    KD=gs3("gKD%d" % hp, kdT, kdT.base[:, hc]),
                    AM=gs3("gAM%d" % hp, AmT, AmT.base[:, hc]),
                    YC=gs3("gYC%d" % hp, ycur, ycur.base[:, hp * 256:(hp + 1) * 256]),
                    C1=gs3("gC1%d" % hp, pl[0], pl[0].base[:, hp * 256:(hp + 1) * 256]),
                    C2=gs3("gC2%d" % hp, pl[3], pl[3].base[:, hp * 256:(hp + 1) * 256]),
                    SG=gs3("gSG%d" % hp, Sg, Sg.base[:, 2 * hp:2 * hp + 2, :]),
                    SM=gs3("gSM%d" % hp, sm2, sm2.base[:, hp * 4:hp * 4 + 4]),
                )

        def gla_chain(hp, c, cs):
            B = gb_[hp]
            h0 = 2 * hp
            hv = slice(h0 * 128, (h0 + 2) * 128)
            H2 = 2 * C

            def v3(b):
                return b[:, :].re("p (h c) -> p h c", c=C)
            eb3, enb3, qd3, kd3 = v3(B["EB"]), v3(B["ENB"]), v3(B["QD"]), v3(B["KD"])
            Am3 = B["AM"][0:C, :].re("p (h c) -> p h c", c=C)
            pb = PS()
            for k in range(2):
                S.mm(pb[:, k * C:(k + 1) * C], spl[0:C, (h0 + k) * 128:(h0 + k + 1) * 128], M(pas, M_UI), True, True)
            pb3 = pb[:, 0:H2].re("p (h c) -> p h c", c=C)
            S.act(eb3, pb3, AF.Exp, scale=-1.0 / 16.0)
            S.act(enb3, pb3, AF.Exp, scale=1.0 / 16.0)
            yield
            S.stt(qd3, qT_[:, h0:h0 + 2, cs], float(128 ** -0.5), eb3, ALU.mult, ALU.mult)
            S.tt(kd3, kT_[:, h0:h0 + 2, cs], enb3, ALU.mult)
            yield
            pa = PS()
            for k in range(2):
                S.mm(pa[0:C, k * C:(k + 1) * C], kd3[:, k, :], qd3[:, k, :], True, True)
            S.tt(Am3, pa[0:C, 0:H2].re("p (h c) -> p h c", c=C), M(pas, M_UI).un(1).bc([C, 2, C]), ALU.mult)
            yield
            po = PS(pin=True)
            SGh = B["SG"]
            for k in range(2):
                ks = slice(k * 128, (k + 1) * 128)
                S.mm(po[0:C, ks], Am3[:, k, :], vtm[c][0:C, (h0 + k) * 128:(h0 + k + 1) * 128], True, False)
                S.mm(po[0:C, ks], qd3[:, k, :], SGh[:, k, :], False, True)
            o_in = po[0:C, 0:256]
            c1 = B["C1"][0:C, :]
            c2 = B["C2"][0:C, :]
            st1 = B["SM"][0:C, 0:2]
            st2 = B["SM"][0:C, 2:4]
            gate = grt[c][0:C, hv]
            S.act(c1, o_in, AF.Square)
            S.red(st1, c1.re("p (g w) -> p g w", w=128))
            S.act(st2, st1, AF.Ln, bias=EPS, scale=1.0 / 128)
            S.act(st1, st2, AF.Exp, scale=-0.5)
            yield
            S.tt(c1.re("p (g w) -> p g w", w=128), o_in.re("p (g w) -> p g w", w=128), st1.un(2).bc([C, 2, 128]), ALU.mult)
            unpin(po)
            S.tt(c1.re("p (g w) -> p g w", w=128), c1.re("p (g w) -> p g w", w=128),
                 rv[l][0:C, RV_GLAN:RV_GLAN + 128].un(1).bc([C, 2, 128]), ALU.mult)
            S.act(c2, gate, AF.Tanh, scale=0.5)
            S.stt(c2, c2, 1.0, gate, ALU.add, ALU.mult)
            yc = B["YC"][0:C, :]
            S.stt(yc, c2, 0.5, c1, ALU.mult, ALU.mult)
            yield
            pt_ = PS()
            for k in range(2):
                S.mm(pt_[:, k * C:(k + 1) * C], yc[:, k * 128:(k + 1) * 128], msk[0:C, M_ID, 0:C], True, True)
            evac(yT_h[hp][:, :, c * C:(c + 1) * C], pt_[:, 0:H2].re("p (j c) -> p j c", c=C))
            yield
            pu = PS()
            for k in range(2):
                ks = slice(k * 128, (k + 1) * 128)
                S.mm(pu[:, ks], kd2[0:C, (h0 + k) * 128:(h0 + k + 1) * 128], vtm[c][0:C, (h0 + k) * 128:(h0 + k + 1) * 128], True, True)
            for k in range(2):
                ks = slice(k * 128, (k + 1) * 128)
                S.stt(SGh[:, k, :], SGh[:, k, :], eb3[:, k, C - 1:C], pu[:, ks], ALU.mult, ALU.add)

        for c in range(NCH):
            cs = slice(c * C, (c + 1) * C)
            plg = PS()
            S.mm(plg[0:C, :], glrA[0:17, cs], wga[:, :], True, True)
            S.act(pl[0][0:C, :], plg[0:C, :], AF.Exp, scale=-1.0)
            S.act(spl[0:C, :], pl[0][0:C, :], AF.Ln, bias=1.0)
            if not pas.samp:
                pr = PS()
                S.mm(pr[0:C, :], M(pas, M_SL), spl[0:C, :], True, True)
                S.act(pl[4][0:C, :], pr[0:C, :], AF.Exp, scale=-1.0 / 16.0)
                S.tt(kd2[0:C, :], ktm[c][0:C, :], pl[4][0:C, :], ALU.mult)
                gens = [gla_chain(hp, c, cs) for hp in range(2)]
                while gens:
                    for gq_ in list(gens):
                        try:
                            next(gq_)
                        except StopIteration:
                            gens.remove(gq_)
                continue
            pb = PS()
            for h in range(4):
                S.mm(pb[:, h * C:(h + 1) * C], spl[0:C, h * 128:(h + 1) * 128], M(pas, M_UI), True, True)
            pb3 = pb[:, 0:4 * C].re("p (h c) -> p h c", c=C)
            eb3 = ebx[:, 0:4 * C].re("p (h c) -> p h c", c=C)
            enb3 = enb[:, 0:4 * C].re("p (h c) -> p h c", c=C)
            qd3 = qdT[:, 0:4 * C].re("p (h c) -> p h c", c=C)
            kd3 = kdT[:, 0:4 * C].re("p (h c) -> p h c", c=C)
            S.act(eb3, pb3, AF.Exp, scale=-1.0 / 16.0)
            S.act(enb3, pb3, AF.Exp, scale=1.0 / 16.0)
            S.stt(qd3, qT_[:, :, cs], float(128 ** -0.5), eb3, ALU.mult, ALU.mult)
            S.tt(kd3, kT_[:, :, cs], enb3, ALU.mult)
            pa = PS()
            for h in range(4):
                S.mm(pa[0:C, h * C:(h + 1) * C], kd3[:, h, :], qd3[:, h, :], True, True)
            Am3 = AmT[0:C, 0:4 * C].re("p (h c) -> p h c", c=C)
            S.tt(Am3, pa[0:C, 0:4 * C].re("p (h c) -> p h c", c=C), M(pas, M_UI).un(1).bc([C, 4, C]), ALU.mult)
            po = PS()
            for h in range(4):
                hs = slice(h * 128, (h + 1) * 128)
                S.mm(po[0:C, hs], Am3[:, h, :], vtm[c][0:C, hs], True, False)
                if not pas.samp:
                    S.mm(po[0:C, hs], qd3[:, h, :], Sg[:, h, :], False, True)
                else:
                    pad = build_pad(s3(qd3[:, h, :]))
                    for i in range(NSEQ):
                        S.mm(po[0:C, hs], pad[:, i, :], sts[i][:, hs], False, i == NSEQ - 1)
            gate_norm_out(pas, po[0:C, :], grt[c][0:C, :], rv[l][0:C, RV_GLAN:RV_GLAN + 128], 4, 128, ycur[0:C, :])
            y_to_yT(pas, c, ycur)
            pr = PS()
            S.mm(pr[0:C, :], M(pas, M_SL), spl[0:C, :], True, True)
            S.act(pl[0][0:C, :], pr[0:C, :], AF.Exp, scale=-1.0 / 16.0)
            S.tt(kd2[0:C, :], ktm[c][0:C, :], pl[0][0:C, :], ALU.mult)
            if not pas.samp:
                pu = PS()
                for h in range(4):
                    hs = slice(h * 128, (h + 1) * 128)
                    S.mm(pu[:, hs], kd2[0:C, hs], vtm[c][0:C, hs], True, True)
                for h in range(4):
                    hs = slice(h * 128, (h + 1) * 128)
                    S.stt(Sg[:, h, :], Sg[:, h, :], eb3[:, h, C - 1:C], pu[:, hs], ALU.mult, ALU.add)
            else:
                def gla_tail(c=c, eb3=eb3):
                    for i in range(NSEQ):
                        vzi = scr[1]
                        S.ts(vzi[0:C, :], vtm[c][0:C, :], msk[0:C, M_IND, i:i + 1], ALU.mult)
                        pu = PS()
                        for h in range(4):
                            hs = slice(h * 128, (h + 1) * 128)
                            S.mm(pu[:, hs], kd2[0:C, hs], vzi[0:C, hs], True, True)
                        sn = big[1].v().re("p a b -> p (a b)")[:, (i % 2) * 512:(i % 2) * 512 + 512]
                        S.tt(sn.re("p (h v) -> p h v", v=128), sts[i].re("p (h v) -> p h v", v=128),
                             eb3[:, :, LS * i + LS - 1:LS * i + LS].bc([128, 4, 128]), ALU.mult)
                        S.tt(sn, sn, pu[:, :], ALU.add)
                        S.dma_out("sp", o_sg[l, i].rearrange("h k v -> k h v"), sn.re("p (h v) -> p h v", v=128))
                        yield
                tails.append(gla_tail())
        if pas.last and not pas.samp:
            S.dma_out("sp", o_pg[l].rearrange("h k v -> k h v"), Sg.v())

    def ssd(pas, l):
        T, C, NCH = pas.T, pas.C, pas.NCH
        bp = Bump()
        cvo = bp.fm("s_cv", 8, pas)
        cvo_t = [S.sub(cvo.buf, cvo.ap[:, t, :], "s_cv_t%d" % t) for t in range(8)]
        szt = bp.tm("s_z", NCH)
        xtm = bp.tm("s_x", NCH)
        Btm = bp.tm("s_B", NCH)
        W = w_in[l]
        hs_ = hist_s[l]
        if pas.samp:
            load_hist_sample(bp, cv_ssd[l], 8, "s_cvin")

        def conv_sink(t, ps):
            raw = pl[3 + t % 2][:, 0:T]
            g_ = conv_fm(pas, l, ps, raw, cvo_t[t][:, 0:T], lambda j: wsc[:, l, t, j:j + 1], bsc[:, l, t:t + 1],
                         hist_in[:, t, :, :] if pas.samp else hs_[:, t, :], pl[t % 3][:, 0:T])
            next(g_)
            yield
            next(g_)
            if not pas.samp:
                S.cp(hs_[:, t, :], raw[:, T - 3:T])
                if pas.last:
                    S.dma_out("sp", o_pcs[l][:, t * 128:(t + 1) * 128].rearrange("j p -> p j"), hs_[:, t, :],
                              allow_slow_non_contiguous=True)
            yield
            for _ in g_:
                pass

        proj_conv(W[:, O_SXBC:O_SXBC + 1024], 1024, pas, conv_sink, o_scs[l])
        proj_tm_wide(W[:, O_SZ:O_SZ + 512], 512, pas, szt, False)
        for c_ in range(NCH):
            silu_inplace(szt[c_][0:C, :], pl[c_ % 3][0:C, :])
        dts = smallT

        def dt_sink(c, ps):
            evac(dts[0:C, c * 8:(c + 1) * 8], ps)
        proj_tm(W[:, O_SDT:O_SDT + 8], 8, pas, dt_sink)
        ST = S_ssd[l]
        Lh, E, MT = big[0], big[1], big[2]
        ycur = scr[0]
        ex = sm
        nat_slots = bp.tm("s_nat", 3) if pas.samp else None
        if pas.samp:
            CTz = [None, None]
            ctzb = [bp.raw("s_ctz%d" % g, 2) for g in range(2)]
            for g in range(2):
                S.memset(ctzb[g].v(), 0.0)
        if not pas.samp:
            def gs2(key, parent, ap):
                kk = (key, id(parent))
                if kk not in gsub:
                    gsub[kk] = S.sub(parent, ap, key)
                return gsub[kk]
            sb_ = {}
            for g in range(2):
                sb_[g] = dict(
                    LH=gs2("sLH%d" % g, big[0], big[0].base[:, 4 * g:4 * g + 4, :]),
                    E=gs2("sE%d" % g, big[1], big[1].base[:, 4 * g:4 * g + 4, :]),
                    MT=gs2("sMT%d" % g, big[2], big[2].base[:, 4 * g:4 * g + 4, :]),
                    CB=gs2("sCB%d" % g, pl[1], pl[1].base[:, g * 256:(g + 1) * 256]),
                    T1=gs2("sT1%d" % g, pl[2], pl[2].base[:, g * 256:(g + 1) * 256]),
                    T2=gs2("sT2%d" % g, pl[3], pl[3].base[:, g * 256:(g + 1) * 256]),
                    C1=gs2("sC1%d" % g, pl[0], pl[0].base[:, g * 256:(g + 1) * 256]),
                    XW=gs2("sXW%d" % g, scr[3], scr[3].base[:, g * 256:(g + 1) * 256]),
                    YC=gs2("sYC%d" % g, scr[0], scr[0].base[:, g * 256:(g + 1) * 256]),
                    ST=gs2("sST%d" % g, ST, ST.base[:, 4 * g:4 * g + 4, :]),
                    SM=gs2("sSM%d" % g, sm2, sm2.base[:, 8 + g * 4:8 + g * 4 + 4]),
                )

        def ssd_chain(g, c, cs, dt, dtA, wts, ecum, elast):
            B = sb_[g]
            h4 = slice(4 * g, 4 * g + 4)
            gv = slice(g * 256, (g + 1) * 256)
            Lh3 = B["LH"][0:C, :, 0:C]
            E3 = B["E"][0:C, :, 0:C]
            MT3 = B["MT"][0:C, :, 0:C]
            S.tt(Lh3, M(pas, M_UI).un(1).bc([C, 4, C]), dtA[:, h4].un(2).bc([C, 4, C]), ALU.mult)
            yield
            pseg = PS()
            S.mm(pseg[0:C, 0:4 * C].re("p (h c) -> p h c", c=C), M(pas, M_SL), Lh3, True, True)
            S.act(E3, pseg[0:C, 0:4 * C].re("p (h c) -> p h c", c=C), AF.Exp)
            pcb = PS()
            S.mm(pcb[0:C, 0:C], cvo[:, 4 + g, cs], cvo[:, 6 + g, cs], True, True)
            CBm = B["CB"][0:C, 0:C]
            S.tt(CBm, pcb[0:C, 0:C], M(pas, M_UI), ALU.mult)
            yield
            S.tt(E3, E3, dt[:, h4].un(2).bc([C, 4, C]), ALU.mult)
            S.tt(MT3, E3, CBm.un(1).bc([C, 4, C]), ALU.mult)
            yield
            py = PS()
            for hh in range(4):
                h = 4 * g + hh
                S.mm(py[0:C, hh * 64:(hh + 1) * 64], MT3[:, hh, :], xtm[c][0:C, h * 64:(h + 1) * 64], True, True)
            pz = PS()
            STg = B["ST"]
            S.mm(pz[0:C, 0:256], cvo[:, 6 + g, cs], STg[:, :, :].re("p h q -> p (h q)"), True, True)
            t1 = B["T1"][0:C, :]
            S.tt(t1.re("p (h q) -> p h q", q=64), pz[0:C, 0:256].re("p (h q) -> p h q", q=64), ecum[:, h4].un(2).bc([C, 4, 64]), ALU.mult)
            S.tt(t1, t1, py[0:C, 0:256], ALU.add)
            yield
            t2 = B["T2"][0:C, :]
            S.tt(t2.re("p (h q) -> p h q", q=64), xtm[c][0:C, gv].re("p (h q) -> p h q", q=64),
                 rv[l][0:C, RV_SD + 4 * g:RV_SD + 4 * g + 4].un(2).bc([C, 4, 64]), ALU.mult)
            S.tt(t1, t1, t2, ALU.add)
            S.tt(t1, t1, szt[c][0:C, gv], ALU.mult)
            yield
            c1 = B["C1"][0:C, :]
            st1 = B["SM"][0:C, 0:1]
            st2 = B["SM"][0:C, 2:3]
            S.act(c1, t1, AF.Square)
            S.red(st1, c1.re("p (g w) -> p g w", w=256))
            S.act(st2, st1, AF.Ln, bias=EPS, scale=1.0 / 256)
            S.act(st1, st2, AF.Exp, scale=-0.5)
            yield
            yc = B["YC"][0:C, :]
            S.stt(yc, t1, st1, rv[l][0:C, RV_SNORM + g * 256:RV_SNORM + (g + 1) * 256], ALU.mult, ALU.mult)
            pt_ = PS()
            for k in range(2):
                S.mm(pt_[:, k * C:(k + 1) * C], yc[:, k * 128:(k + 1) * 128], msk[0:C, M_ID, 0:C], True, True)
            evac(yT_h[g][:, :, c * C:(c + 1) * C], pt_[:, 0:2 * C].re("p (j c) -> p j c", c=C))
            yield
            xw = B["XW"][0:C, :]
            S.tt(xw.re("p (h q) -> p h q", q=64), xtm[c][0:C, gv].re("p (h q) -> p h q", q=64),
                 wts[:, h4].un(2).bc([C, 4, 64]), ALU.mult)
            pu = PS()
            S.mm(pu[:, 0:256], Btm[c][0:C, g * 128:(g + 1) * 128], xw, True, True)
            t3 = B["T2"][:, :]
            S.tt(t3.re("p (h q) -> p h q", q=64), STg.v(), elast[:, h4].un(2).bc([128, 4, 64]), ALU.mult)
            S.tt(STg.v().re("p h q -> p (h q)"), t3, pu[:, 0:256], ALU.add)

        def ssd_pro(c, par, hold):
            cs = slice(c * C, (c + 1) * C)
            smx, smrx = (sm, sm_b)[par], (smr, smr_b)[par]
            px = transposes_fm(cvo, 0, 4, cs, C)
            evac(xtm[c][0:C, :].re("p (j w) -> p j w", w=128), px)
            yield
            pB = transposes_fm(cvo, 4, 2, cs, C)
            evac(Btm[c][0:C, 0:256].re("p (j w) -> p j w", w=128), pB)
            yield
            dt = smx[0:C, 24:32]
            dtA = smrx[0:C, 0:8]
            wts = smx[0:C, 40:48]
            S.tt(dt, dts[0:C, c * 8:(c + 1) * 8], rv[l][0:C, RV_SDTB:RV_SDTB + 8], ALU.add)
            S.act(dt, dt, AF.Exp)
            S.act(dt, dt, AF.Ln, bias=1.0)
            S.tt(dtA, dt, negA[0:C, l, 0:8], ALU.mult)
            yield
            cum3(pas, dtA, 8, smx)
            ecum, erev, elast = smx[0:C, 0:8], smx[0:C, 8:16], smx[0:C, 16:24]
            S.tt(wts, erev, dt, ALU.mult)
            hold["res"] = (cs, dt, dtA, wts, ecum, erev, elast, smx)

        if not pas.samp:
            hold = {}
            run_gens([ssd_pro(0, 0, hold)])
            for c in range(NCH):
                nhold = {}
                cs, dt, dtA, wts, ecum, erev, elast, ex = hold["res"]
                gens = [ssd_chain(g, c, cs, dt, dtA, wts, ecum, elast) for g in range(2)]
                if c + 1 < NCH:
                    gens.append(ssd_pro(c + 1, (c + 1) % 2, nhold))
                run_gens(gens)
                hold = nhold
        for c in (range(NCH) if pas.samp else []):
            hold = {}
            run_gens([ssd_pro(c, 0, hold)])
            cs, dt, dtA, wts, ecum, erev, elast, ex = hold["res"]
            if not pas.samp:
                continue
            Lh3 = Lh[0:C, :, 0:C]
            S.tt(Lh3, M(pas, M_SL).un(1).bc([C, 8, C]), dtA.un(2).bc([C, 8, C]), ALU.mult)
            for half in range(2):
                pseg = PS()
                for hh in range(4):
                    h = half * 4 + hh
                    S.mm(pseg[0:C, hh * C:(hh + 1) * C], Lh3[:, h, :], M(pas, M_UI), True, True)
                S.act(E[0:C, half * 4:(half + 1) * 4, 0:C], pseg[0:C, 0:4 * C].re("p (h c) -> p h c", c=C), AF.Exp)
            pcb = PS()
            for g in range(2):
                S.mm(pcb[0:C, g * C:(g + 1) * C], cvo[:, 4 + g, cs], cvo[:, 6 + g, cs], True, True)
            CBm = pl[1][0:C, 0:2 * C].re("p (g c) -> p g c", c=C)
            S.tt(CBm, pcb[0:C, 0:2 * C].re("p (g c) -> p g c", c=C), M(pas, M_UI).un(1).bc([C, 2, C]), ALU.mult)
            E3 = E[0:C, :, 0:C]
            MT3 = MT[0:C, :, 0:C]
            S.tt(E3, E3, dt.un(2).bc([C, 8, C]), ALU.mult)
            for g in range(2):
                S.tt(MT3[:, g * 4:(g + 1) * 4, :], E3[:, g * 4:(g + 1) * 4, :], CBm[:, g:g + 1, :].bc([C, 4, C]), ALU.mult)
            py = PS(pin=True)
            for h in range(8):
                S.mm(py[0:C, h * 64:(h + 1) * 64], MT3[:, h, :], xtm[c][0:C, h * 64:(h + 1) * 64], True, True)
            pz = PS(pin=True)
            if not pas.samp:
                for g in range(2):
                    S.mm(pz[0:C, g * 256:(g + 1) * 256], cvo[:, 6 + g, cs], ST[:, g * 4:(g + 1) * 4, :].re("p h q -> p (h q)"), True, True)
            else:
                for g in range(2):
                    CTz[g] = build_pad_into(ctzb[g], ctzb[g].base, s3(cvo[:, 6 + g, 0:T]))
                S.mm(pz[0:C, :], zt[:, 0:C], msk[:, 0:4, :].re("p a b -> p (a b)"), True, True)
                xw = scr[3]
                S.tt(xw[0:C, :].re("p (h q) -> p h q", q=64), xtm[c][0:C, :].re("p (h q) -> p h q", q=64),
                     wts.un(2).bc([C, 8, 64]), ALU.mult)
                dtrep = scr[2][0:C, :].re("p (a m) -> p a m", m=128)
                for a in range(4):
                    S.cp(dtrep[:, a, :].re("p (b q) -> p b q", q=64), dtA[:, 2 * a:2 * a + 2].un(2).bc([C, 2, 64]))
                pe_ = PS()
                for a in range(4):
                    S.mm(pe_[:, a * NSEQ:(a + 1) * NSEQ], dtrep[:, a, :], msk[0:C, M_IND, 0:NSEQ], True, True)
                S.act(ecol.v().re("p a i -> p (a i)"), pe_[:, 0:4 * NSEQ], AF.Exp)
                for i in range(NSEQ):
                    nat = nat_slots[i % 3].re("p (a n) -> p a n", n=128)
                    S.dma_in("pool", nat, st_ssd[l, i].rearrange("(a b) q n -> (b q) a n", b=2))
                    pt = PS()
                    for a in range(4):
                        S.mm(pt[:, a * 128:(a + 1) * 128], nat[:, a, :], msk[:, M_ID, :], True, True)
                    sti = scr[(5, 6)[i % 2]]
                    evac(sti[:, :], pt[:, :])
                    for g in range(2):
                        S.mm(pz[0:C, g * 256:(g + 1) * 256], CTz[g][:, i, :], sti[:, g * 256:(g + 1) * 256], False, i == NSEQ - 1)
                    Bz = scr[1]
                    S.ts(Bz[0:C, 0:256], Btm[c][0:C, 0:256], msk[0:C, M_IND, i:i + 1], ALU.mult)
                    pu = PS()
                    for a in range(4):
                        g = a // 2
                        S.mm(pu[:, a * 128:(a + 1) * 128], xw[0:C, a * 128:(a + 1) * 128], Bz[0:C, g * 128:(g + 1) * 128], True, True)
                    sn = big[1].v().re("p a b -> p (a b)")[:, (i % 2) * 512:(i % 2) * 512 + 512].re("p (a n) -> p a n", n=128)
                    for a in range(4):
                        S.stt(sn[:, a, :], nat[:, a, :], ecol[:, a, i:i + 1], pu[:, a * 128:(a + 1) * 128], ALU.mult, ALU.add)
                    S.dma_out("sp", o_ss[l, i].rearrange("(a b) q n -> (b q) a n", b=2), sn)
            t1 = pl[2][0:C, :]
            S.tt(t1.re("p (h q) -> p h q", q=64), pz[0:C, :].re("p (h q) -> p h q", q=64), ecum.un(2).bc([C, 8, 64]), ALU.mult)
            S.tt(t1, t1, py[0:C, :], ALU.add)
            unpin(py)
            unpin(pz)
            t2 = pl[3][0:C, :]
            S.tt(t2.re("p (h q) -> p h q", q=64), xtm[c][0:C, :].re("p (h q) -> p h q", q=64),
                 rv[l][0:C, RV_SD:RV_SD + 8].un(2).bc([C, 8, 64]), ALU.mult)
            S.tt(t1, t1, t2, ALU.add)
            S.tt(t1, t1, szt[c][0:C, :], ALU.mult)
            c1 = pl[0][0:C, :]
            st1 = sm[0:C, 64:66]
            st2 = sm[0:C, 96:98]
            S.act(c1, t1, AF.Square)
            S.red(st1, c1.re("p (g w) -> p g w", w=256))
            S.act(st2, st1, AF.Ln, bias=EPS, scale=1.0 / 256)
            S.act(st1, st2, AF.Exp, scale=-0.5)
            S.tt(t1.re("p (g w) -> p g w", w=256), t1.re("p (g w) -> p g w", w=256), st1.un(2).bc([C, 2, 256]), ALU.mult)
            S.tt(ycur[0:C, :], t1, rv[l][0:C, RV_SNORM:RV_SNORM + 512], ALU.mult)
            y_to_yT(pas, c, ycur)
            if not pas.samp:
                xw = scr[3]
                S.tt(xw[0:C, :].re("p (h q) -> p h q", q=64), xtm[c][0:C, :].re("p (h q) -> p h q", q=64),
                     wts.un(2).bc([C, 8, 64]), ALU.mult)
                pu = PS()
                for g in range(2):
                    S.mm(pu[:, g * 256:(g + 1) * 256], Btm[c][0:C, g * 128:(g + 1) * 128], xw[0:C, g * 256:(g + 1) * 256], True, True)
                t3 = pl[3]
                S.tt(t3.v().re("p (h q) -> p h q", q=64), ST.v(), elast.un(2).bc([128, 8, 64]), ALU.mult)
                S.tt(ST.v().re("p h q -> p (h q)"), t3.v(), pu[:, :], ALU.add)
        if pas.last and not pas.samp:
            pt = PS()
            for a in range(4):
                S.mm(pt[:, a * 128:(a + 1) * 128], ST[:, 2 * a:2 * a + 2, :].re("p h q -> p (h q)"), msk[:, M_ID, :], True, True)
            sn = big[1].v().re("p a b -> p (a b)")[:, 0:512]
            evac(sn, pt[:, :])
            S.dma_out("sp", o_ps[l].rearrange("(a b) q n -> (b q) a n", b=2), sn.re("p (a n) -> p a n", n=128))

    def transposes_fm(fmv, t0, nt, cs, C):
        ps = PS()
        for j in range(nt):
            S.mm(ps[0:C, j * 128:(j + 1) * 128], fmv[:, t0 + j, cs], msk[:, M_ID, :], True, True)
        return ps[0:C, 0:nt * 128].re("p (j w) -> p j w", w=128)

    def gdn(pas, l):
        T, C, NCH = pas.T, pas.C, pas.NCH
        bp = Bump()
        cvo = bp.fm("d_cv", 12, pas)
        cvo_t = [S.sub(cvo.buf, cvo.ap[:, t, :], "d_cv_t%d" % t) for t in range(12)]
        dgt = bp.tm("d_g", NCH)
        extra = bp.tm("d_x", 4)
        vtm_, kbtm, kbT_, ycur = extra
        W = w_in[l]
        hd_ = hist_d[l]
        if pas.samp:
            load_hist_sample(bp, cv_gdn[l], 12, "d_cvin")
        norm_todo = []

        def conv_sink(t, ps):
            raw = pl[3 + t % 2][:, 0:T]
            g_ = conv_fm(pas, l, ps, raw, cvo_t[t][:, 0:T], lambda j: wdc[:, l, t, j:j + 1], None,
                         hist_in[:, t, :, :] if pas.samp else hd_[:, t, :], pl[t % 3][:, 0:T])
            next(g_)
            yield
            next(g_)
            if not pas.samp:
                S.cp(hd_[:, t, :], raw[:, T - 3:T])
                if pas.last:
                    S.dma_out("sp", o_pcd[l][:, t * 128:(t + 1) * 128].rearrange("j p -> p j"), hd_[:, t, :],
                              allow_slow_non_contiguous=True)
            yield
            for _ in g_:
                pass
            if t < 8:
                norm_todo.append(t)

        def l2norm_tiles():
            for t in norm_todo:
                sq = scr[t % 2][:, 0:T]
                S.act(sq, cvo_t[t][:, 0:T], AF.Square)
                pn = PS()
                S.mm(pn[:, 0:T], msk[:, M_BLK, :], sq, True, True)
                S.act(sq, pn[:, 0:T], AF.Ln, bias=EPS)
                S.act(sq, sq, AF.Exp, scale=-0.5, bias=(math.log(128 ** -0.5) if t < 4 else 0.0))
                S.tt(cvo_t[t][:, 0:T], cvo_t[t][:, 0:T], sq, ALU.mult)

        proj_conv(W[:, O_DQKV:O_DQKV + 1536], 1536, pas, conv_sink, o_scd[l])
        l2norm_tiles()
        proj_tm_wide(W[:, O_DG:O_DG + 512], 512, pas, dgt, False)
        dab = smallT

        def ab_sink(c, ps):
            evac(dab[0:C, c * 8:(c + 1) * 8], ps)
        proj_tm(W[:, O_DA:O_DA + 8], 8, pas, ab_sink)
        Sd = S_gdn[l]
        sts = None
        if pas.samp:
            sts = bp.tm("d_S", NSEQ)
            for i in range(NSEQ):
                S.dma_in("pool", sts[i].re("p (h v) -> p h v", v=128), st_gdn[l, i].rearrange("h k v -> k h v"))
        nlev = 1 if pas.samp else 5
        GM = {M_UI: M_GUI, M_SL: M_GSL, M_SU: M_GSU, M_BLK: M_GBLK}

        def mk(idx):
            return M(pas, idx) if pas.samp else msk[0:C, GM[idx], 0:C]
        blocks = [(0, 64)] if pas.samp else [(0, 64), (64, 128)]
        if not pas.samp:
            NHD = 2
            H2 = NHD * C

            def gs(key, parent, ap):
                kk = (key, id(parent))
                if kk not in gsub:
                    gsub[kk] = S.sub(parent, ap, key)
                return gsub[kk]
            cb = {}
            for hp in range(4 // NHD):
                q0, q1 = hp * NHD, (hp + 1) * NHD
                tg = "n%d_%d" % (NHD, hp)
                cb[hp] = dict(
                    LA=gs("LA" + tg, big[0], big[0].base[:, q0:q1, :]),
                    LB=gs("LB" + tg, big[0], big[0].base[:, 4 + q0:4 + q1, :]),
                    EA=gs("EA" + tg, big[1], big[1].base[:, q0:q1, :]),
                    EB=gs("EB" + tg, big[1], big[1].base[:, 4 + q0:4 + q1, :]),
                    AQ=gs("AQ" + tg, scr[2], scr[2].base[:, hp * H2:(hp + 1) * H2]),
                    P=gs("P" + tg, pl[1], pl[1].base[:, hp * H2:(hp + 1) * H2]),
                    X0=gs("X0" + tg, pl[2], pl[2].base[:, hp * H2:(hp + 1) * H2]),
                    XT0=gs("XT0" + tg, pl[3], pl[3].base[:, hp * H2:(hp + 1) * H2]),
                    X1=gs("X1" + tg, pl[4], pl[4].base[:, hp * H2:(hp + 1) * H2]),
                    XT1=gs("XT1" + tg, pl[0], pl[0].base[:, hp * H2:(hp + 1) * H2]),
                    WT=gs("WT" + tg, scr[3], scr[3].base[:, hp * H2:(hp + 1) * H2]),
                    VN=gs("VN" + tg, scr[4], scr[4].base[:, q0 * 128:q1 * 128]),
                    SD=gs("SD" + tg, Sd, Sd.base[:, q0:q1, :]),
                    ST=gs("ST" + tg, sm2, sm2.base[:, q0 * 2:q1 * 2]),
                    YC=gs("YC" + tg, ycur.buf, ycur.ap[:, q0 * 128:q1 * 128]),
                    YT=gs("YT" + tg, yT, yT.base[:, q0:q1, :]),
                )

        def gdn_chain(hp, c, cs, beta, gd, bec, ecum, kn, kbT3, kw, el2, vt):
            B = cb[hp]
            h0 = NHD * hp
            HW = NHD * 128
            hv = slice(h0 * 128, (h0 + NHD) * 128)

            def v3(buf):
                return buf[0:C, :].re("p (h c) -> p h c", c=C) if len(buf.base.shape) == 2 else buf[0:C, :, 0:C]
            LA, LB, EA, EB = v3(B["LA"]), v3(B["LB"]), v3(B["EA"]), v3(B["EB"])
            AqT, P = v3(B["AQ"]), v3(B["P"])
            gdh = gd[:, h0:h0 + NHD]
            S.tt(LA, mk(M_SL).un(1).bc([C, NHD, C]), gdh.un(2).bc([C, NHD, C]), ALU.mult)
            S.tt(LB, mk(M_UI).un(1).bc([C, NHD, C]), gdh.un(2).bc([C, NHD, C]), ALU.mult)
            yield
            psa = PS()
            S.mm(psa[0:C, 0:H2].re("p (h c) -> p h c", c=C), mk(M_SL), LB, True, True)
            S.act(EA, psa[0:C, 0:H2].re("p (h c) -> p h c", c=C), AF.Exp)
            psb = PS()
            S.mm(psb[0:C, 0:H2].re("p (h c) -> p h c", c=C), mk(M_UI), LA, True, True)
            S.act(EB, psb[0:C, 0:H2].re("p (h c) -> p h c", c=C), AF.Exp)
            yield
            pgm = PS()
            for k in range(NHD):
                S.mm(pgm[0:C, k * C:(k + 1) * C], kbT3[:, h0 + k, :], kbT3[:, h0 + k, :], True, True)
            pg3 = pgm[0:C, 0:H2].re("p (h c) -> p h c", c=C)
            paq = PS()
            for k in range(NHD):
                S.mm(paq[0:C, k * C:(k + 1) * C], cvo[:, 4 + h0 + k, cs], cvo[:, h0 + k, cs], True, True)
            S.tt(EA, EA, mk(M_UI).un(1).bc([C, NHD, C]), ALU.mult)
            S.tt(AqT, paq[0:C, 0:H2].re("p (h c) -> p h c", c=C), EA, ALU.mult)
            X, XT = EA, EB
            S.tt(X, pg3, EA, ALU.mult)
            S.tt(XT, pg3, EB, ALU.mult)
            yield
            S.stt(X, X, -1.0, msk[0:C, M_GSU, 0:C].un(1).bc([C, NHD, C]), ALU.mult, ALU.mult)
            S.stt(XT, XT, -1.0, mk(M_SL).un(1).bc([C, NHD, C]), ALU.mult, ALU.mult)
            S.tt(P, X, msk[0:C, M_ID, 0:C].un(1).bc([C, NHD, C]), ALU.add)
            yield
            Xc, XTc = X, XT
            for lev in range(nlev):
                lastlev = (lev == nlev - 1)
                Xn = v3(B["X0"] if lev % 2 == 0 else B["X1"])
                XTn = v3(B["XT0"] if lev % 2 == 0 else B["XT1"])
                pxt = PS()
                for k in range(NHD):
                    S.mm32(pxt[0:C, k * C:(k + 1) * C], Xc[:, k, :], XTc[:, k, :], True, True)
                S.cp(XTn, pxt[0:C, 0:H2].re("p (h c) -> p h c", c=C), "act")
                if not lastlev:
                    px_ = PS()
                    for k in range(NHD):
                        S.mm32(px_[0:C, k * C:(k + 1) * C], XTc[:, k, :], Xc[:, k, :], True, True)
                    S.cp(Xn, px_[0:C, 0:H2].re("p (h c) -> p h c", c=C), "act")
                yield
                pp = PS()
                for k in range(NHD):
                    S.mm32(pp[0:C, k * C:(k + 1) * C], XTn[:, k, :], P[:, k, :], True, True)
                S.tt(P, P, pp[0:C, 0:H2].re("p (h c) -> p h c", c=C), ALU.add)
                Xc, XTc = Xn, XTn
                yield
            vb = B["LA"][0:C, :, :]
            kbd = B["LB"][0:C, :, :]
            S.tt(vb, vt[0:C, hv].re("p (h w) -> p h w", w=128), beta[:, h0:h0 + NHD].un(2).bc([C, NHD, 128]), ALU.mult)
            S.tt(kbd, kn[0:C, hv].re("p (h w) -> p h w", w=128), bec[:, h0:h0 + NHD].un(2).bc([C, NHD, 128]), ALU.mult)
            yield
            pw = PS()
            for k in range(NHD):
                S.mm32(pw[:, k * C:(k + 1) * C], kbd[:, k, :], P[:, k, :], True, True)
            WT = v3(B["WT"])
            S.ts(WT, pw[:, 0:H2].re("p (h c) -> p h c", c=C), -1.0, ALU.mult)
            yield
            vnew = B["VN"]
            SDh = B["SD"]
            o_sb = B["X0"][0:C, :]
            for b, (r0, r1) in enumerate(blocks):
                rs = slice(r0, r1)
                pvn = PS()
                pqs = PS()
                for k in range(NHD):
                    ks = slice(k * 128, (k + 1) * 128)
                    S.mm32(pvn[0:C, ks], P[:, k, :], vb[:, k, :], True, False)
                    S.mm32(pvn[0:C, ks], WT[:, k, :], SDh[:, k, :], False, True)
                evac(vnew[rs, :], pvn[rs, 0:HW])
                for k in range(NHD):
                    ks = slice(k * 128, (k + 1) * 128)
                    S.mm(pqs[0:C, ks], cvo[:, h0 + k, cs], SDh[:, k, :], True, True)
                S.tt(o_sb[rs, :].re("p (h w) -> p h w", w=128), pqs[rs, 0:HW].re("p (h w) -> p h w", w=128),
                     ecum[rs, h0:h0 + NHD].un(2).bc([r1 - r0, NHD, 128]), ALU.mult)
                yield
                pu = PS()
                for k in range(NHD):
                    ks = slice(k * 128, (k + 1) * 128)
                    S.mm(pu[:, ks], kw[rs, (h0 + k) * 128:(h0 + k + 1) * 128], vnew[rs, ks], True, True)
                t3 = B["XT1"][:, :]
                S.tt(t3.re("p (h w) -> p h w", w=128), SDh.v(), el2[:, b, h0:h0 + NHD].un(2).bc([128, NHD, 128]), ALU.mult)
                S.tt(SDh.v(), t3.re("p (h w) -> p h w", w=128), pu[:, 0:HW].re("p (h w) -> p h w", w=128), ALU.add)
                yield
            pav = PS()
            for k in range(NHD):
                ks = slice(k * 128, (k + 1) * 128)
                S.mm(pav[0:C, ks], AqT[:, k, :], vnew[0:C, ks], True, True)
            S.tt(o_sb, o_sb, pav[0:C, 0:HW], ALU.add)
            yield
            c1 = B["XT0"][0:C, :]
            c2 = B["X1"][0:C, :]
            st1 = B["ST"][0:C, 0:NHD]
            st2 = B["ST"][0:C, NHD:2 * NHD]
            gate = dgt[c][0:C, hv]
            S.act(c1, o_sb, AF.Square)
            S.red(st1, c1.re("p (g w) -> p g w", w=128))
            S.act(st2, st1, AF.Ln, bias=EPS, scale=1.0 / 128)
            S.act(st1, st2, AF.Exp, scale=-0.5)
            yield
            S.tt(c1.re("p (g w) -> p g w", w=128), o_sb.re("p (g w) -> p g w", w=128), st1.un(2).bc([C, NHD, 128]), ALU.mult)
            S.tt(c1.re("p (g w) -> p g w", w=128), c1.re("p (g w) -> p g w", w=128),
                 rv[l][0:C, RV_DNORM:RV_DNORM + 128].un(1).bc([C, NHD, 128]), ALU.mult)
            S.act(c2, gate, AF.Tanh, scale=0.5)
            S.stt(c2, c2, 1.0, gate, ALU.add, ALU.mult)
            yc = B["YC"][0:C, :]
            S.stt(yc, c2, 0.5, c1, ALU.mult, ALU.mult)
            yield
            pt_ = PS()
            for k in range(NHD):
                S.mm(pt_[:, k * C:(k + 1) * C], yc[:, k * 128:(k + 1) * 128], msk[0:C, M_ID, 0:C], True, True)
            evac(B["YT"][:, :, c * C:(c + 1) * C], pt_[:, 0:H2].re("p (j c) -> p j c", c=C))

        alt_ = bp.tm("d_alt", 3) if not pas.samp else None

        def gdn_pro(c, par, hold):
            cs = slice(c * C, (c + 1) * C)
            smx, smrx = (sm, sm_b)[par], (smr, smr_b)[par]
            vt = vtm_ if par == 0 else alt_[0]
            kbt = kbtm if par == 0 else alt_[1]
            kbTx = kbT_ if par == 0 else alt_[2]
            kn = scr[6] if par == 0 else scr[0]
            kw = scr[5] if par == 0 else scr[1]
            beta = smx[0:C, 24:28]
            gd = smrx[0:C, 0:4]
            bec = smx[0:C, 32:36]
            tmp4 = smx[0:C, 36:40]
            sigmoid_from(beta, dab[0:C, c * 8 + 4:c * 8 + 8], tmp4)
            S.tt(gd, dab[0:C, c * 8:c * 8 + 4], rv[l][0:C, RV_DDTB:RV_DDTB + 4], ALU.add)
            S.act(gd, gd, AF.Exp)
            S.act(gd, gd, AF.Ln, bias=1.0)
            S.tt(gd, gd, negA[0:C, l, 8:12], ALU.mult)
            yield
            pc = PS()
            S.mm(pc[0:C, 0:4], mk(M_UI), gd, True, True)
            S.mm(pc[0:C, 4:8], mk(M_SL), gd, True, True)
            S.mm(pc[0:C, 8:12], mk(M_BLK), gd, True, True)
            S.act(smx[0:C, 0:12], pc[0:C, 0:12], AF.Exp)
            ecum, erev, elast = smx[0:C, 0:4], smx[0:C, 4:8], smx[0:C, 8:12]
            S.tt(bec, beta, ecum, ALU.mult)
            yield
            pv = transposes_fm(cvo, 8, 4, cs, C)
            evac(vt[0:C, :].re("p (j w) -> p j w", w=128), pv)
            yield
            pk = transposes_fm(cvo, 4, 4, cs, C)
            evac(kn[0:C, :].re("p (j w) -> p j w", w=128), pk)
            yield
            S.tt(kbt[0:C, :].re("p (h w) -> p h w", w=128), kn[0:C, :].re("p (h w) -> p h w", w=128),
                 beta.un(2).bc([C, 4, 128]), ALU.mult)
            yield
            pkb = transposes(kbt, C, 4)
            kbT3 = kbTx[:, 0:4 * C].re("p (h c) -> p h c", c=C)
            evac(kbT3, pkb)
            yield
            el2 = None
            if not pas.samp:
                S.tt(kw[0:C, :].re("p (h w) -> p h w", w=128), kn[0:C, :].re("p (h w) -> p h w", w=128),
                     erev.un(2).bc([C, 4, 128]), ALU.mult)
                gz2 = smrx[:, 8:16].re("p (b h) -> p b h", h=4)
                for b, (r0, r1) in enumerate(blocks):
                    S.ts(gz2[:, b, :], gd, msk[:, M_GBLK, r0:r0 + 1], ALU.mult)
                pe2 = PS()
                S.mm(pe2[:, 0:8], msk[:, M_BLK, :], smrx[:, 8:16], True, True)
                el2 = smx[:, 48:56].re("p (b h) -> p b h", h=4)
                S.act(smx[:, 48:56], pe2[:, 0:8], AF.Exp)
            hold["res"] = (cs, beta, gd, bec, ecum, erev, elast, kn, kbT3, kw, el2, vt, smx)

        if not pas.samp:
            hold = {}
            run_gens([gdn_pro(0, 0, hold)])
            for c in range(NCH):
                nhold = {}
                cs, beta, gd, bec, ecum, erev, elast, kn, kbT3, kw, el2, vt, ex = hold["res"]
                gens = [gdn_chain(hp, c, cs, beta, gd, bec, ecum, kn, kbT3, kw, el2, vt) for hp in range(4 // NHD)]
                if c + 1 < NCH:
                    gens.append(gdn_pro(c + 1, (c + 1) % 2, nhold))
                run_gens(gens)
                hold = nhold
        for c in (range(NCH) if pas.samp else []):
            hold = {}
            run_gens([gdn_pro(c, 0, hold)])
            cs, beta, gd, bec, ecum, erev, elast, kn, kbT3, kw, el2, vt, ex = hold["res"]
            Lh = big[0]
            S.tt(Lh[0:C, 0:4, 0:C], mk(M_SL).un(1).bc([C, 4, C]), gd.un(2).bc([C, 4, C]), ALU.mult)
            S.tt(Lh[0:C, 4:8, 0:C], mk(M_UI).un(1).bc([C, 4, C]), gd.un(2).bc([C, 4, C]), ALU.mult)
            EA = big[1][0:C, 0:4, 0:C]
            EB = big[1][0:C, 4:8, 0:C]
            psa = PS()
            for h in range(4):
                S.mm(psa[0:C, h * C:(h + 1) * C], Lh[0:C, h, 0:C], mk(M_UI), True, True)
            S.act(EA, psa[0:C, 0:4 * C].re("p (h c) -> p h c", c=C), AF.Exp)
            psb = PS()
            for h in range(4):
                S.mm(psb[0:C, h * C:(h + 1) * C], Lh[0:C, 4 + h, 0:C], mk(M_SL), True, True)
            S.act(EB, psb[0:C, 0:4 * C].re("p (h c) -> p h c", c=C), AF.Exp)
            pgm = PS()
            for h in range(4):
                S.mm(pgm[0:C, h * C:(h + 1) * C], kbT3[:, h, :], kbT3[:, h, :], True, True)
            pg3 = pgm[0:C, 0:4 * C].re("p (h c) -> p h c", c=C)
            S.tt(EA, EA, mk(M_UI).un(1).bc([C, 4, C]), ALU.mult)
            paq = PS()
            for h in range(4):
                S.mm(paq[0:C, h * C:(h + 1) * C], cvo[:, 4 + h, cs], cvo[:, h, cs], True, True)
            AqT = scr[2][0:C, 0:4 * C].re("p (h c) -> p h c", c=C)
            S.tt(AqT, paq[0:C, 0:4 * C].re("p (h c) -> p h c", c=C), EA, ALU.mult)
            X, XT = EA, EB
            S.tt(X, pg3, EA, ALU.mult)
            S.tt(X, X, mk(M_SU).un(1).bc([C, 4, C]), ALU.mult)
            S.ts(X, X, -1.0, ALU.mult)
            S.tt(XT, pg3, EB, ALU.mult)
            S.tt(XT, XT, mk(M_SL).un(1).bc([C, 4, C]), ALU.mult)
            S.ts(XT, XT, -1.0, ALU.mult)
            P = pl[1][0:C, 0:4 * C].re("p (h c) -> p h c", c=C)
            S.tt(P, X, msk[0:C, M_ID, 0:C].un(1).bc([C, 4, C]), ALU.add)
            Xc, XTc = X, XT
            for lev in range(nlev):
                lastlev = (lev == nlev - 1)
                Xn = (pl[2] if lev % 2 == 0 else pl[4])[0:C, 0:4 * C].re("p (h c) -> p h c", c=C)
                XTn = (pl[3] if lev % 2 == 0 else pl[0])[0:C, 0:4 * C].re("p (h c) -> p h c", c=C)
                pxt = PS()
                for h in range(4):
                    S.mm32(pxt[0:C, h * C:(h + 1) * C], Xc[:, h, :], XTc[:, h, :], True, True)
                S.cp(XTn, pxt[0:C, 0:4 * C].re("p (h c) -> p h c", c=C), "act")
                if not lastlev:
                    px_ = PS()
                    for h in range(4):
                        S.mm32(px_[0:C, h * C:(h + 1) * C], XTc[:, h, :], Xc[:, h, :], True, True)
                    S.cp(Xn, px_[0:C, 0:4 * C].re("p (h c) -> p h c", c=C), "dve")
                pp = PS()
                for h in range(4):
                    S.mm32(pp[0:C, h * C:(h + 1) * C], msk[0:C, M_ID, 0:C], P[:, h, :], True, False)
                    S.mm32(pp[0:C, h * C:(h + 1) * C], XTn[:, h, :], P[:, h, :], False, True)
                S.cp(P, pp[0:C, 0:4 * C].re("p (h c) -> p h c", c=C), "dve")
                Xc, XTc = Xn, XTn
            if c == 0 and l == 0 and pas.idx == 0 and not pas.samp:
                dump("d_kn", kn[0:C, :]); dump("d_beta", beta); dump("d_gd", gd); dump("d_vtm", vtm_[0:C, :])
                dump("d_P", P); dump("d_X", X); dump("d_AqT", AqT); dump("d_ex", ex[0:C, 0:12])
            vb = big[0][0:C, 0:4, :]
            kbd = big[0][0:C, 4:8, :]
            S.tt(vb, vtm_[0:C, :].re("p (h w) -> p h w", w=128), beta.un(2).bc([C, 4, 128]), ALU.mult)
            S.tt(kbd, kn[0:C, :].re("p (h w) -> p h w", w=128), bec.un(2).bc([C, 4, 128]), ALU.mult)
            pw = PS()
            for h in range(4):
                S.mm32(pw[:, h * C:(h + 1) * C], kbd[:, h, :], P[:, h, :], True, True)
            WT = scr[3][:, 0:4 * C].re("p (h c) -> p h c", c=C)
            S.ts(WT, pw[:, 0:4 * C].re("p (h c) -> p h c", c=C), -1.0, ALU.mult)
            vnew = scr[4]
            kw = scr[5]
            S.tt(kw[0:C, :].re("p (h w) -> p h w", w=128), kn[0:C, :].re("p (h w) -> p h w", w=128),
                 erev.un(2).bc([C, 4, 128]), ALU.mult)
            o_sb = pl[2][0:C, :]
            if not pas.samp:
                gz2 = smr[:, 8:16].re("p (b h) -> p b h", h=4)
                for b, (r0, r1) in enumerate(blocks):
                    S.ts(gz2[:, b, :], gd, msk[:, M_GBLK, r0:r0 + 1], ALU.mult)
                pe2 = PS()
                S.mm(pe2[:, 0:8], msk[:, M_BLK, :], smr[:, 8:16], True, True)
                el2 = sm[:, 48:56].re("p (b h) -> p b h", h=4)
                S.act(sm[:, 48:56], pe2[:, 0:8], AF.Exp)
            for b, (r0, r1) in enumerate(blocks):
                rs = slice(r0, r1)
                pvn = PS()
                pqs = PS()
                if pas.samp:
                    S.mm(pqs[0:C, :], zt[:, 0:C], msk[:, 0:4, :].re("p a b -> p (a b)"), True, True)
                for h in range(4):
                    hs = slice(h * 128, (h + 1) * 128)
                    S.mm32(pvn[0:C, hs], P[:, h, :], vb[:, h, :], True, False)
                    if not pas.samp:
                        S.mm32(pvn[0:C, hs], WT[:, h, :], Sd[:, h, :], False, True)
                    else:
                        pad = build_pad(s3(WT[:, h, :]))
                        for i in range(NSEQ):
                            S.mm32(pvn[0:C, hs], pad[:, i, :], sts[i][:, hs], False, i == NSEQ - 1)
                evac(vnew[rs, :], pvn[rs, :])
                for h in range(4):
                    hs = slice(h * 128, (h + 1) * 128)
                    if not pas.samp:
                        S.mm(pqs[0:C, hs], cvo[:, h, cs], Sd[:, h, :], True, True)
                    else:
                        pad = build_pad(s3(cvo[:, h, 0:T]))
                        for i in range(NSEQ):
                            S.mm(pqs[0:C, hs], pad[:, i, :], sts[i][:, hs], False, i == NSEQ - 1)
                S.tt(o_sb[rs, :].re("p (h w) -> p h w", w=128), pqs[rs, :].re("p (h w) -> p h w", w=128),
                     ecum[rs, :].un(2).bc([r1 - r0, 4, 128]), ALU.mult)
                if not pas.samp:
                    pu = PS()
                    for h in range(4):
                        hs = slice(h * 128, (h + 1) * 128)
                        S.mm(pu[:, hs], kw[rs, hs], vnew[rs, hs], True, True)
                    t3 = pl[4]
                    S.tt(t3.v().re("p (h w) -> p h w", w=128), Sd.v(), el2[:, b, :].un(2).bc([128, 4, 128]), ALU.mult)
                    S.tt(Sd.v().re("p h w -> p (h w)"), t3.v(), pu[:, :], ALU.add)
            pav = PS()
            for h in range(4):
                hs = slice(h * 128, (h + 1) * 128)
                S.mm(pav[0:C, hs], AqT[:, h, :], vnew[0:C, hs], True, True)
            S.tt(o_sb, o_sb, pav[0:C, :], ALU.add)
            if c == 0 and l == 0 and pas.idx == 0 and not pas.samp:
                dump("d_vnew", vnew[0:C, :]); dump("d_o", o_sb)
            gate_norm_out(pas, o_sb, dgt[c][0:C, :], rv[l][0:C, RV_DNORM:RV_DNORM + 128], 4, 128, ycur[0:C, :])
            y_to_yT(pas, c, ycur)
            if pas.samp:
                seq_totals(gd, 4)

                def gdn_tail(vnew=vnew, kw=kw):
                    for i in range(NSEQ):
                        vzi = scr[1]
                        S.ts(vzi[0:C, :], vnew[0:C, :], msk[0:C, M_IND, i:i + 1], ALU.mult)
                        pu = PS()
                        for h in range(4):
                            hs = slice(h * 128, (h + 1) * 128)
                            S.mm(pu[:, hs], kw[0:C, hs], vzi[0:C, hs], True, True)
                        sn = big[1].v().re("p a b -> p (a b)")[:, (i % 2) * 512:(i % 2) * 512 + 512]
                        S.tt(sn.re("p (h v) -> p h v", v=128), sts[i].re("p (h v) -> p h v", v=128),
                             elast_all[:, i, 0:4].un(2).bc([128, 4, 128]), ALU.mult)
                        S.tt(sn, sn, pu[:, :], ALU.add)
                        S.dma_out("sp", o_sd[l, i].rearrange("h k v -> k h v"), sn.re("p (h v) -> p h v", v=128))
                        yield
                tails.append(gdn_tail())
        if pas.last and not pas.samp:
            S.dma_out("sp", o_pd[l].rearrange("h k v -> k h v"), Sd.v())

    def out_and_ffn(pas, l):
        T = pas.T
        mix = KSplit(S, carve("mix", 0, 8))
        for eb2 in range(4):
            proj_fm(w_out[l][:, eb2 * 256:(eb2 + 1) * 256], 8, 256, lambda kc: mergedT[:, kc, 0:T], T,
                    lambda j, ps, m, eb2=eb2: evac(mix[:, eb2 * 2 + j, 0:T], ps))
        postnorm_residual(l, 0, pas, mix)
        if "ffn" not in stages:
            return
        prenorm(l, 3, pas)
        fT = KSplit(S, carve("fT", 0, NSLOT))
        for f in range(22):
            wa = wload(w3(w_ffn_in[l][:, f * 128:(f + 1) * 128]), 8, 128)
            wb_ = wload(w3(w_ffn_in[l][:, FH + f * 128:FH + (f + 1) * 128]), 8, 128)
            pa = PS()
            for kc in range(8):
                S.mm(pa[:, 0:T], wa[:, kc, :], hT[:, kc, 0:T], kc == 0, kc == 7)
            pb = PS()
            for kc in range(8):
                S.mm(pb[:, 0:T], wb_[:, kc, :], hT[:, kc, 0:T], kc == 0, kc == 7)
            sg = pl[2 + f % 3][:, 0:T]
            S.act(sg, pa[:, 0:T], AF.Tanh, scale=0.5)
            S.stt(sg, sg, 1.0, pa[:, 0:T], ALU.add, ALU.mult)
            S.stt(fT[:, f, 0:T], sg, 0.5, pb[:, 0:T], ALU.mult, ALU.mult)
        for ep in range(4):
            pso = [PS(), PS()]
            for kg in range(3):
                nf = 8 if kg < 2 else 6
                wo = wload(w_ffn_out[l][kg * 1024:kg * 1024 + nf * 128, ep * 256:(ep + 1) * 256].rearrange("(kc p) n -> p kc n", p=128), nf, 256)
                for jj in range(2):
                    for fc in range(nf):
                        f = kg * 8 + fc
                        S.mm(pso[jj][:, 0:T], wo[:, fc, jj * 128:(jj + 1) * 128], fT[:, f, 0:T], f == 0, f == 21)
            for jj in range(2):
                evac(mergedT[:, ep * 2 + jj, 0:T], pso[jj][:, 0:T])
        postnorm_residual(l, 3, pas, mergedT)

    passes = [Pass(False, i, npass) for i in range(npass)]
    if do_sample:
        passes.append(Pass(True, 0, npass))
    for pas in passes:
        T = pas.T
        for kc in range(8):
            if pas.samp:
                S.dma_in("sp", xT[:, kc, 0:T], xTs[kc * 128:(kc + 1) * 128, :])
            else:
                S.dma_in("sp", xT[:, kc, 0:T], xTp[kc * 128:(kc + 1) * 128, pas.idx * 512:(pas.idx + 1) * 512])
        for l in range(nlayers):
            prenorm(l, 0, pas)
            if dbg and pas.idx == 0 and l == 0 and not pas.samp:
                dump("hT0", hT.v())
            first = True
            for n, (name, fn) in enumerate((("gla", gla), ("ssd", ssd), ("gdn", gdn))):
                if name in stages:
                    del tails[:]
                    fn(pas, l)
                    if dbg and pas.idx == 0 and l == 0 and not pas.samp:
                        dump("yT_" + name, yT.v())
                    if dbg and pas.samp and l == 0:
                        dump("yTs_" + name, yT[:, :, 0:64])
                    run_gens([branch_merge(l, n, pas, first)] + list(tails))
                    del tails[:]
                    first = False
            out_and_ffn(pas, l)
        for kc in range(8):
            if pas.samp:
                S.dma_out("sp", yTs[kc * 128:(kc + 1) * 128, :], xT[:, kc, 0:T])
            else:
                S.dma_out("sp", yTp[kc * 128:(kc + 1) * 128, pas.idx * 512:(pas.idx + 1) * 512], xT[:, kc, 0:T])
    S.emit()
    es.close()
    return nc


def _c(a):
    return np.ascontiguousarray(a, dtype=np.float32)


def make_in_maps(inp):
    masks = _masks()
    rowv = np.concatenate([inp["g_gla_norm"], inp["ssd_dt_bias"], inp["ssd_a_log"], inp["ssd_d"], inp["g_ssd_norm"],
                           inp["gdn_dt_bias"], inp["gdn_a_log"], inp["g_gdn_norm"]], axis=1)[:, None, :]
    shared = {
        "w_ada": _c(inp["w_ada"]),
        "b_adaT": _c(inp["b_ada"].reshape(2, 48, 128).transpose(0, 2, 1)),
        "gainsT": _c(np.stack([inp["g_pre_mix"], inp["g_post_mix"], inp["g_pre_ffn"], inp["g_post_ffn"]], axis=1)
                     .reshape(2, 4, 8, 128).transpose(0, 3, 1, 2)),
        "w_in": _c(inp["w_in"]),
        "w_gate": _c(inp["w_gla_gate"]),
        "b_gate": _c(inp["b_gla_gate"][:, None, :]),
        "rowvec": _c(rowv),
        "w_sconvT": _c(inp["w_ssd_conv"].reshape(2, 4, 8, 128).transpose(0, 3, 2, 1)),
        "b_sconvT": _c(inp["b_ssd_conv"].reshape(2, 8, 128).transpose(0, 2, 1)),
        "w_dconvT": _c(inp["w_gdn_conv"].reshape(2, 4, 12, 128).transpose(0, 3, 2, 1)),
        "w_branch": _c(inp["w_branch"]),
        "w_out": _c(inp["w_out"]),
        "w_ffn_in": _c(inp["w_ffn_in"]),
        "w_ffn_out": _c(inp["w_ffn_out"]),
        "cst": masks,
    }
    maps = []
    for i in range(NCORE):
        sl = slice(NSEQ * i, NSEQ * (i + 1))
        m = dict(shared)
        m["xTp"] = _c(inp["x_prompt"][i].T)
        m["xTs"] = _c(inp["x_sample"][sl].reshape(NSEQ * LS, D).T)
        m["cT"] = _c(np.concatenate([inp["c_prompt"][i:i + 1], inp["c_sample"][sl]], axis=0).T)
        m["st_gla"] = _c(inp["state_gla"][:, sl])
        m["st_ssd"] = _c(inp["state_ssd"][:, sl])
        m["cv_ssd"] = _c(inp["cache_ssd_conv"][:, sl])
        m["st_gdn"] = _c(inp["state_gdn"][:, sl])
        m["cv_gdn"] = _c(inp["cache_gdn_conv"][:, sl])
        maps.append(m)
    return maps


def gather(results):
    n = len(results)
    y_p = np.stack([r["yTp"].T for r in results])
    y_s = np.concatenate([r["yTs"].T.reshape(NSEQ, LS, D) for r in results])

    def cat1(k):
        return np.concatenate([r[k][:, None] for r in results], axis=1)

    def cats(k):
        return np.concatenate([r[k] for r in results], axis=1)

    outs = (y_p, y_s, cat1("o_pg"), cat1("o_ps"), cat1("o_pcs"), cat1("o_pd"), cat1("o_pcd"),
            cats("o_sg"), cats("o_ss"), cats("o_scs"), cats("o_sd"), cats("o_scd"))
    return tuple(np.ascontiguousarray(o, dtype=np.float32) for o in outs)


def kernel(**inputs):
    inp = {k: np.asarray(v) for k, v in inputs.items()}
    nc = build()
    maps = make_in_maps(inp)
    res = run_bass_kernel_spmd(nc, maps, core_ids=list(range(NCORE)))
    return gather(res.results)
```

```python
import math
from contextlib import ExitStack
import numpy as np
import concourse.bass as bass
import concourse.mybir as mybir
from concourse.bass_utils import run_bass_kernel_spmd

F32 = mybir.dt.float32
F32R = mybir.dt.float32r
AF = mybir.ActivationFunctionType
ALU = mybir.AluOpType
AX = mybir.AxisListType

ENGS = ("pe", "act", "dve", "pool", "sp")
EPS = 1e-6
NCORE = 8
SEQ = 2048
NSEQ = 16
LS = 4
D = 1024
NIN = 8736
O_GQ, O_GK, O_GV, O_GR, O_GLR = 0, 512, 1024, 1536, 2048
O_SZ, O_SXBC, O_SDT = 2064, 2576, 3600
O_DQKV, O_DA, O_DB, O_DG, O_MG = 3608, 5144, 5148, 5152, 5664
FH = 2816
RV_GLAN, RV_SDTB, RV_SALOG, RV_SD, RV_SNORM, RV_DDTB, RV_DALOG, RV_DNORM, RV_N = 0, 128, 136, 144, 152, 664, 668, 672, 800
M_UI, M_SL, M_SU, M_BLK, M_GUI, M_GSL, M_ID = 0, 1, 2, 3, 4, 5, 6
M_GSU, M_GBLK = 11, 12
M_SOFF = 7
M_IND = 13
NMASK = 14


class V:
    __slots__ = ("buf", "ap")

    def __init__(self, buf, ap):
        self.buf = buf
        self.ap = ap

    def __getitem__(self, k):
        return V(self.buf, self.ap[k])

    def f32(self):
        return V(self.buf, self.ap.bitcast(F32))

    def r(self):
        return V(self.buf, self.ap.bitcast(F32R))

    def bc(self, shape):
        return V(self.buf, self.ap.broadcast_to(list(shape)))

    def un(self, axis):
        return V(self.buf, self.ap.unsqueeze(axis))

    def re(self, pat, **kw):
        return V(self.buf, self.ap.rearrange(pat, **kw))

    @property
    def shape(self):
        return self.ap.shape


class Buf:
    __slots__ = ("base", "name", "w", "r", "dsem", "dcnt", "rr", "fam", "dkey")

    def __init__(self, base_ap, name, rr=False):
        self.base = base_ap
        self.name = name
        self.rr = rr
        self.fam = []
        self.dkey = None
        self.w = None
        self.r = {}
        self.dsem = None
        self.dcnt = 0

    def __getitem__(self, k):
        return V(self, self.base[k])

    def v(self):
        return V(self, self.base)


class KSplit:
    def __init__(self, sched, buf):
        self.buf = buf
        self.ch = [sched.sub(buf, buf.base[:, k, :], "%s_k%d" % (buf.name, k)) for k in range(buf.base.shape[1])]

    def __getitem__(self, key):
        if isinstance(key, tuple) and len(key) == 3 and isinstance(key[1], int):
            return self.ch[key[1]][key[0], key[2]]
        return self.buf[key]

    def v(self):
        return self.buf.v()


class Ins:
    __slots__ = ("eng", "fn", "deps", "need", "val", "dma", "dbuf", "dval", "pos", "dk")

    def __init__(self, eng, fn):
        self.eng = eng
        self.fn = fn
        self.deps = []
        self.need = False
        self.val = 0
        self.dma = False
        self.dbuf = None
        self.dval = 0
        self.pos = 0


class Sched:
    def __init__(self, nc, es):
        self.nc = nc
        self.es = es
        self.q = {e: [] for e in ENGS}
        self.order = []
        self.nalias = 0
        self.dcnt = {}
        self.zsrc = self.sb("zsrc", [128, 1])
        self.memset(self.zsrc.v(), 0.0)

    def sb(self, name, shape, rr=False):
        t = self.es.enter_context(self.nc.sbuf_tensor(name, list(shape), F32))
        return Buf(t[:], name, rr)

    def psum(self, name, shape, dt=F32):
        t = self.es.enter_context(self.nc.psum_tensor(name, list(shape), dt))
        return Buf(t[:], name)

    def sub(self, parent, ap, name):
        b = Buf(ap, name, parent.rr)
        b.fam = [parent]
        parent.fam.append(b)
        return b

    def region(self, ap, name, parents=(), rr=False):
        b = Buf(ap, name, rr)
        for p0 in parents:
            for p in [p0] + p0.fam:
                if p.w is not None:
                    self.nalias += 1
                    b.r[("a", self.nalias)] = p.w
                for x in p.r.values():
                    self.nalias += 1
                    b.r[("a", self.nalias)] = x
        return b

    def _add(self, eng, fn, reads, writes, dma=False, dbuf=None):
        ins = Ins(eng, fn)
        ins.dma = dma
        deps = []
        for b0 in reads:
            for b in [b0] + b0.fam:
                if b.w is not None:
                    deps.append(b.w)
        for b0 in writes:
            for b in [b0] + b0.fam:
                if b.w is not None:
                    deps.append(b.w)
                deps.extend(b.r.values())
        seen = set()
        for d in deps:
            if id(d) not in seen and d is not ins:
                seen.add(id(d))
                ins.deps.append(d)
                d.need = True
        if dma:
            ins.dbuf = dbuf
            ins.dk = dbuf.dkey if dbuf.dkey is not None else id(dbuf)
            self.dcnt[ins.dk] = self.dcnt.get(ins.dk, 0) + 16
            ins.dval = self.dcnt[ins.dk]
        for b in reads:
            b.r[("d", id(ins)) if dma else eng] = ins
        for b in writes:
            b.w = ins
            b.r = {}
        ins.pos = len(self.order)
        self.q[eng].append(ins)
        self.order.append(ins)
        return ins

    @staticmethod
    def _bufs(vs):
        out = []
        for v in vs:
            if isinstance(v, V) and v.buf not in out:
                out.append(v.buf)
        return out

    def op(self, eng, fn, outs, ins):
        return self._add(eng, fn, self._bufs(ins), self._bufs(outs))

    @staticmethod
    def _o(v):
        return v.ap.bitcast(F32R) if v.buf.rr else v.ap

    def mm(self, out, lhsT, rhs, start=True, stop=True, tp=None):
        o, a, b = out.ap, lhsT.ap.bitcast(F32R), rhs.ap.bitcast(F32R)
        assert lhsT.buf.rr and rhs.buf.rr, (lhsT.buf.name, rhs.buf.name)
        if tp is None:
            return self.op("pe", lambda e: e.matmul(o, a, b, start=start, stop=stop), [out], [lhsT, rhs])
        return self.op("pe", lambda e: e.matmul(o, a, b, start=start, stop=stop, tile_position=tp), [out], [lhsT, rhs])

    def mm32(self, out, lhsT, rhs, start=True, stop=True, tp=None):
        o, a, b = out.ap, lhsT.ap, rhs.ap
        if tp is None:
            return self.op("pe", lambda e: e.matmul(o, a, b, start=start, stop=stop), [out], [lhsT, rhs])
        return self.op("pe", lambda e: e.matmul(o, a, b, start=start, stop=stop, tile_position=tp), [out], [lhsT, rhs])

    def act(self, out, in_, func, bias=0.0, scale=1.0, accum=None):
        o, i = self._o(out), in_.ap
        b = bias.ap if isinstance(bias, V) else bias
        s = scale.ap if isinstance(scale, V) else scale
        ac = accum.ap if accum is not None else None
        outs = [out] + ([accum] if accum is not None else [])
        return self.op("act", lambda e: e.activation(o, i, func, bias=b, scale=s, accum_out=ac),
                       outs, [in_, bias, scale])

    def tt(self, out, a, b, op, eng="dve"):
        o, x, y = self._o(out), a.ap, b.ap
        return self.op(eng, lambda e: e.tensor_tensor(o, x, y, op), [out], [a, b])

    def ts(self, out, a, s1, op0, s2=None, op1=None, eng="dve"):
        o, x = self._o(out), a.ap
        p1 = s1.ap if isinstance(s1, V) else s1
        p2 = s2.ap if isinstance(s2, V) else s2
        if op1 is None:
            return self.op(eng, lambda e: e.tensor_scalar(o, x, p1, None, op0), [out], [a, s1])
        return self.op(eng, lambda e: e.tensor_scalar(o, x, p1, p2, op0, op1), [out], [a, s1, s2])

    def stt(self, out, a, sc, b, op0, op1):
        o, x, y = self._o(out), a.ap, b.ap
        p = sc.ap if isinstance(sc, V) else sc
        return self.op("dve", lambda e: e.scalar_tensor_tensor(o, x, p, y, op0, op1), [out], [a, sc, b])

    def red(self, out, in_, op=None):
        o, i = self._o(out), in_.ap
        op = op or ALU.add
        return self.op("dve", lambda e: e.tensor_reduce(o, i, AX.X, op), [out], [in_])

    def cp(self, out, in_, eng="dve"):
        o, i = self._o(out), in_.ap
        if eng == "act":
            return self.op("act", lambda e: e.copy(o, i), [out], [in_])
        return self.op(eng, lambda e: e.tensor_copy(o, i), [out], [in_])

    def recip(self, out, in_):
        o, i = self._o(out), in_.ap
        return self.op("dve", lambda e: e.reciprocal(o, i), [out], [in_])

    def memset(self, out, val, eng="dve"):
        if out.buf.rr:
            z = self.zsrc
            shp = list(out.ap.shape)
            zin = bass.AP(z.base.tensor, z.base.offset, [[z.base.ap[0][0], shp[0]]] + [[0, n] for n in shp[1:]])
            return self.ts(out, V(z, zin), float(val), ALU.add)
        o = out.ap
        return self.op(eng, lambda e: e.memset(o, val), [out], [])

    def dma_in(self, eng, out, src_ap, **kw):
        o = self._o(out)
        if out.buf.rr:
            eng = "pool"
        return self._add(eng, lambda e: e.dma_start(out=o, in_=src_ap, **kw), [], [out.buf], dma=True, dbuf=out.buf)

    def dma_out(self, eng, dst_ap, in_, **kw):
        i = in_.ap
        return self._add(eng, lambda e: e.dma_start(out=dst_ap, in_=i, **kw), [in_.buf], [], dma=True, dbuf=in_.buf)

    def emit(self):
        nc, es = self.nc, self.es
        sems = {e: es.enter_context(nc.semaphore("s_" + e)) for e in ENGS}
        dsems = {}
        for ins in self.order:
            if ins.dma and ins.dk not in dsems:
                dsems[ins.dk] = es.enter_context(nc.semaphore("d%d" % len(dsems)))
        for e in ENGS:
            c = 0
            for ins in self.q[e]:
                if not ins.dma and ins.need:
                    c += 1
                    ins.val = c
        hist = {}
        for ins in self.order:
            if ins.dma:
                hist.setdefault(ins.dk, []).append((ins.pos, ins.dval))
        engobj = {"pe": nc.tensor, "act": nc.scalar, "dve": nc.vector, "pool": nc.gpsimd, "sp": nc.sync}
        block = es.enter_context(nc.Block())

        def run(e):
            eo = engobj[e]
            waited = {}
            for ins in self.q[e]:
                need = {}
                for d in ins.deps:
                    if d.dma:
                        hl = hist[d.dk]
                        lo, hi = 0, len(hl)
                        while lo < hi:
                            mid = (lo + hi) // 2
                            if hl[mid][0] < ins.pos:
                                lo = mid + 1
                            else:
                                hi = mid
                        v = hl[lo - 1][1] if lo > 0 else 0
                        key = ("d", d.dk)
                        sem = dsems[d.dk]
                    else:
                        if d.eng == e and e == "pe":
                            continue
                        v = d.val
                        key = d.eng
                        sem = sems[d.eng]
                    if waited.get(key, 0) >= v:
                        continue
                    if key not in need or need[key][1] < v:
                        need[key] = (sem, v)
                for key, (sem, v) in need.items():
                    eo.wait_ge(sem, v)
                    waited[key] = v
                bi = ins.fn(eo)
                if ins.dma:
                    bi.then_inc(dsems[ins.dk], 16)
                elif ins.need:
                    bi.then_inc(sems[e], 1)
            if e == "sp":
                for k, sem in dsems.items():
                    eo.wait_ge(sem, self.dcnt[k])

        @block.tensor
        def _(x):
            run("pe")

        @block.scalar
        def _(x):
            run("act")

        @block.vector
        def _(x):
            run("dve")

        @block.gpsimd
        def _(x):
            run("pool")

        @block.sync
        def _(x):
            run("sp")


def _masks():
    r = np.arange(128)[:, None]
    c = np.arange(128)[None, :]
    m = np.zeros((NMASK, 128, 128), np.float32)
    m[M_UI] = (r <= c)
    m[M_SL] = (r > c)
    m[M_SU] = (r < c)
    m[M_BLK] = 1.0
    blk64 = ((r // 64) == (c // 64))
    m[M_GUI] = m[M_UI] * blk64
    m[M_GSL] = m[M_SL] * blk64
    m[M_GSU] = m[M_SU] * blk64
    m[M_GBLK] = blk64
    m[M_ID] = (r == c)
    same = ((r // LS) == (c // LS)) & (r < 64) & (c < 64)
    for k in (M_UI, M_SL, M_SU, M_BLK):
        m[M_SOFF + k] = m[k] * same
    m[M_IND] = ((r // LS) == c) & (c < NSEQ) & (r < 64)
    return m


class Pass:
    def __init__(self, samp, idx, npass):
        self.samp = samp
        self.idx = idx
        self.T = 64 if samp else 512
        self.C = 64 if samp else 128
        self.NCH = 1 if samp else 4
        self.moff = M_SOFF if samp else 0
        self.first = (idx == 0)
        self.last = samp or idx == npass - 1


def build(nlayers=2, npass=4, do_sample=True, stages=("gla", "ssd", "gdn", "ffn"), dbg=None):
    nc = bass.Bass("TRN2", target_bir_lowering=False)

    def din(name, shape):
        return nc.dram_tensor(name, list(shape), F32, kind="ExternalInput").ap()

    def dout(name, shape):
        return nc.dram_tensor(name, list(shape), F32, kind="ExternalOutput").ap()

    xTp = din("xTp", [D, SEQ])
    xTs = din("xTs", [D, 64])
    cT = din("cT", [D, 17])
    st_gla = din("st_gla", [2, NSEQ, 4, 128, 128])
    st_ssd = din("st_ssd", [2, NSEQ, 8, 64, 128])
    cv_ssd = din("cv_ssd", [2, NSEQ, 3, 1024])
    st_gdn = din("st_gdn", [2, NSEQ, 4, 128, 128])
    cv_gdn = din("cv_gdn", [2, NSEQ, 3, 1536])
    w_ada = din("w_ada", [2, D, 6144])
    b_adaT = din("b_adaT", [2, 128, 48])
    gainsT = din("gainsT", [2, 128, 4, 8])
    w_in = din("w_in", [2, D, NIN])
    w_gate = din("w_gate", [2, 16, 512])
    b_gate = din("b_gate", [2, 1, 512])
    rowvec = din("rowvec", [2, 1, RV_N])
    w_sconvT = din("w_sconvT", [2, 128, 8, 4])
    b_sconvT = din("b_sconvT", [2, 128, 8])
    w_dconvT = din("w_dconvT", [2, 128, 12, 4])
    w_branch = din("w_branch", [2, 3, 512, D])
    w_out = din("w_out", [2, D, D])
    w_ffn_in = din("w_ffn_in", [2, D, 2 * FH])
    w_ffn_out = din("w_ffn_out", [2, FH, D])
    cst = din("cst", [NMASK, 128, 128])

    yTp = dout("yTp", [D, SEQ])
    yTs = dout("yTs", [D, 64])
    o_pg = dout("o_pg", [2, 4, 128, 128])
    o_ps = dout("o_ps", [2, 8, 64, 128])
    o_pcs = dout("o_pcs", [2, 3, 1024])
    o_pd = dout("o_pd", [2, 4, 128, 128])
    o_pcd = dout("o_pcd", [2, 3, 1536])
    o_sg = dout("o_sg", [2, NSEQ, 4, 128, 128])
    o_ss = dout("o_ss", [2, NSEQ, 8, 64, 128])
    o_scs = dout("o_scs", [2, NSEQ, 3, 1024])
    o_sd = dout("o_sd", [2, NSEQ, 4, 128, 128])
    o_scd = dout("o_scd", [2, NSEQ, 3, 1536])
    dbg_out = {}
    if dbg:
        for name, shape in dbg.items():
            dbg_out[name] = dout("dbg_" + name, shape)

    es = ExitStack()
    S = Sched(nc, es)
    es.enter_context(nc.allow_low_precision(reason="fp32r-rounded PE operands"))

    def dump(name, v):
        if name in dbg_out:
            S.dma_out("sp", dbg_out[name], v)

    msk = S.sb("msk", [128, NMASK, 128], True)
    S.dma_in("pool", msk.v(), cst.rearrange("m p c -> p m c"))
    xT = KSplit(S, S.sb("xT", [128, 8, 512]))
    hT = KSplit(S, S.sb("hT", [128, 8, 512], True))
    NSLOT = 23
    arena_t = es.enter_context(nc.sbuf_tensor("arena", [128, NSLOT, 512], F32))
    wbufs = [S.sb("wb%d" % i, [128, 8, 256], True) for i in range(3)]
    whalf = []
    for i in range(3):
        for k in range(2):
            whalf.append(S.sub(wbufs[i], wbufs[i].base[:, :, k * 128:(k + 1) * 128], "wh%d_%d" % (i, k)))
    banks = [S.psum("pb%d" % i, [128, 512], F32) for i in range(8)]
    dmod = [S.sb("dmod%d" % l, [128, 6, 8, 17]) for l in range(2)]
    gains = S.sb("gains", [128, 2, 4, 8])
    badd = S.sb("badd", [128, 2, 48])
    rv = [S.sb("rv%d" % l, [128, RV_N]) for l in range(2)]
    negA = S.sb("negA", [128, 2, 12])
    wsc = S.sb("wsc", [128, 2, 8, 4])
    bsc = S.sb("bsc", [128, 2, 8])
    wdc = S.sb("wdc", [128, 2, 12, 4])
    wga = S.sb("wga", [17, 512], True)
    glrA = S.sb("glrA", [17, 512], True)
    sT = S.sb("sT", [128, 8, 18], True)
    S_gla = [S.sb("S_gla%d" % l, [128, 4, 128], True) for l in range(2)]
    S_ssd = [S.sb("S_ssd%d" % l, [128, 8, 64], True) for l in range(2)]
    S_gdn = [S.sb("S_gdn%d" % l, [128, 4, 128], True) for l in range(2)]
    hist_s = [S.sb("hist_s%d" % l, [128, 8, 3]) for l in range(2)]
    hist_d = [S.sb("hist_d%d" % l, [128, 12, 3]) for l in range(2)]
    mergedT = KSplit(S, S.sb("mergedT", [128, 8, 512], True))
    yT = S.sb("yT", [128, 4, 512], True)
    scr = [S.sb("scr%d" % i, [128, 512], True) for i in range(7)]
    pl = [S.sb("pl%d" % i, [128, 512]) for i in range(5)]
    big = [S.sb("big%d" % i, [128, 8, 128], i != 1) for i in range(3)]
    hist_in = V(big[1], big[1].base.rearrange("p a b -> p (a b)")[:, 0:576].rearrange("p (t b j) -> p t b j", b=NSEQ, j=3))
    modt = V(big[1], big[1].base.rearrange("p a b -> p (a b)")[:, 0:816].rearrange("p (j n) -> p j n", n=17))
    sm = S.sb("sm", [128, 128])
    padz = S.sb("padz", [128, NSEQ, 64], True)
    zt = S.sb("zt", [128, 64], True)
    elast_all = S.sb("elast_all", [128, NSEQ, 8])
    gz = S.sb("gz", [64, NSEQ, 8], True)
    ecol = S.sb("ecol", [128, 4, NSEQ])
    smallT = S.sb("smallT", [128, 32])
    smr = S.sb("smr", [128, 16], True)
    sm2 = S.sb("sm2", [128, 16])
    betaAll = S.sb("betaAll", [128, 16])
    sm_b = S.sb("sm_b", [128, 128])
    smr_b = S.sb("smr_b", [128, 16], True)
    yT_h = [S.sub(yT, yT.base[:, 2 * k:2 * k + 2, :], "yT_h%d" % k) for k in range(2)]

    gsub = {}
    tails = []
    state = {"arsem": 0, "ps": 0, "wb": 0, "arena": [], "ev": 0, "pad": 0, "pinned": set()}

    def PS(pin=False):
        while (state["ps"] % 8) in state["pinned"]:
            state["ps"] += 1
        k = state["ps"] % 8
        state["ps"] += 1
        if pin:
            state["pinned"].add(k)
        return banks[k]

    def unpin(b):
        state["pinned"].discard(banks.index(b))

    def carve(name, s0, n):
        assert s0 + n <= NSLOT, (name, s0, n)
        parents = [b for (b, a0, a1) in state["arena"] if a0 < s0 + n and s0 < a1]
        state["arena"] = [(b, a0, a1) for (b, a0, a1) in state["arena"] if not (a0 < s0 + n and s0 < a1)]
        b = S.region(arena_t[:, s0:s0 + n, :], name, parents, True)
        b.dkey = ("ar", state["arsem"] % 24)
        state["arsem"] += 1
        state["arena"].append((b, s0, s0 + n))
        return b

    class Bump:
        def __init__(self):
            self.n = 0

        def fm(self, name, ntile, pas):
            if pas.samp:
                ns = (ntile * 64 + 511) // 512
                b = carve(name, self.n, ns)
                self.n += ns
                v = V(b, b.base.rearrange("p s c -> p (s c)")[:, 0:ntile * 64].rearrange("p (n t) -> p n t", t=64))
                return v
            b = carve(name, self.n, ntile)
            self.n += ntile
            return b.v()

        def raw(self, name, n):
            b = carve(name, self.n, n)
            self.n += n
            return b

        def tm(self, name, n=1):
            out = []
            for i in range(n):
                b = carve("%s%d" % (name, i), self.n, 1)
                self.n += 1
                out.append(b[:, 0, :])
            return out

    def M(pas, idx, rows=None, cols=None):
        C = pas.C
        k = idx if idx in (M_ID, M_IND) else pas.moff + idx
        return msk[0:(rows or C), k, 0:(cols or C)]

    def wload(src3, nk, ncols):
        if ncols <= 128:
            b = whalf[state["wb"] % 6]
            state["wb"] += 1
        else:
            if state["wb"] % 2:
                state["wb"] += 1
            b = wbufs[(state["wb"] % 6) // 2]
            state["wb"] += 2
        S.dma_in("pool", b[:, 0:nk, 0:ncols], src3)
        return b

    def w3(w2d):
        return w2d.rearrange("(kc p) n -> p kc n", p=128)

    def run_gens(gens):
        gens = list(gens)
        while gens:
            for g_ in list(gens):
                try:
                    next(g_)
                except StopIteration:
                    gens.remove(g_)

    def evac(out, in_):
        state["ev"] += 1
        S.cp(out, in_, "act" if state["ev"] % 2 else "dve")

    def proj_fm(w2d, nk, ncols, rhs_fn, T, sink):
        wb = wload(w3(w2d), nk, ncols)
        for j in range((ncols + 127) // 128):
            m = min(128, ncols - j * 128)
            ps = PS()
            for kc in range(nk):
                S.mm(ps[0:m, 0:T], wb[:, kc, j * 128:j * 128 + m], rhs_fn(kc), kc == 0, kc == nk - 1)
            sink(j, ps[0:m, 0:T], m)

    def proj_fm_wide(w2d, ncols, T, sink):
        for b0 in range(0, ncols, 256):
            nb = min(256, ncols - b0)
            proj_fm(w2d[:, b0:b0 + nb], 8, nb, lambda kc: hT[:, kc, 0:T], T,
                    lambda j, ps, m, b0=b0: sink(b0 // 128 + j, ps))

    def proj_tm(w2d, ncols, pas, sink):
        wb = wload(w3(w2d), 8, ncols)
        for c in range(pas.NCH):
            ps = PS()
            for kc in range(8):
                S.mm(ps[0:pas.C, 0:ncols], hT[:, kc, c * pas.C:(c + 1) * pas.C], wb[:, kc, 0:ncols], kc == 0, kc == 7)
            sink(c, ps[0:pas.C, 0:ncols])

    def proj_tm_wide(w2d, ncols, pas, dst, rnd):
        for b0 in range(0, ncols, 256):
            nb = min(256, ncols - b0)

            def sink(c, ps, b0=b0, nb=nb):
                o = dst[c][0:pas.C, b0:b0 + nb]
                evac(o if rnd else o, ps)
            proj_tm(w2d[:, b0:b0 + nb], nb, pas, sink)

    S.dma_in("sp", gains.v(), gainsT.rearrange("l p w k -> p l w k"))
    S.dma_in("sp", badd.v(), b_adaT.rearrange("l p j -> p l j"))
    S.dma_in("sp", wsc.v(), w_sconvT.rearrange("l p t j -> p l t j"))
    S.dma_in("sp", bsc.v(), b_sconvT.rearrange("l p t -> p l t"))
    S.dma_in("sp", wdc.v(), w_dconvT.rearrange("l p t j -> p l t j"))
    S.memset(glrA.v(), 1.0)
    S.memset(padz.v(), 0.0)
    S.memset(zt.v(), 0.0)
    for l in range(2):
        S.dma_in("sp", rv[l].v(), rowvec[l].partition_broadcast(128).rearrange("p o n -> p (o n)"))
        S.act(negA[:, l, 0:8], rv[l][:, RV_SALOG:RV_SALOG + 8], AF.Exp)
        S.act(negA[:, l, 8:12], rv[l][:, RV_DALOG:RV_DALOG + 4], AF.Exp)
        S.memset(hist_s[l].v(), 0.0)
        S.memset(hist_d[l].v(), 0.0)
        S.memset(S_gla[l].v(), 0.0)
        S.memset(S_ssd[l].v(), 0.0)
        S.memset(S_gdn[l].v(), 0.0)
    S.ts(negA.v(), negA.v(), -1.0, ALU.mult)
    S.ts(wsc.v(), wsc.v(), 0.5, ALU.mult)
    S.ts(bsc.v(), bsc.v(), 0.5, ALU.mult)
    S.ts(wdc.v(), wdc.v(), 0.5, ALU.mult)
    cin = pl[4][:, 0:136]
    ce = pl[1][:, 0:136]
    S.dma_in("sp", cin.re("p (k n) -> p k n", n=17), cT.rearrange("(k p) n -> p k n", p=128))
    S.act(ce, cin, AF.Exp, scale=-1.0)
    S.ts(ce, ce, 1.0, ALU.add)
    S.recip(ce, ce)
    S.memset(sT.v(), 0.0)
    S.tt(sT[:, :, 0:17], cin.re("p (k n) -> p k n", n=17), ce.re("p (k n) -> p k n", n=17), ALU.mult)
    for l in range(nlayers):
        ps = None
        for jb in range(24):
            wb = wload(w3(w_ada[l][:, jb * 256:(jb + 1) * 256]), 8, 256)
            for jj in range(2):
                j = jb * 2 + jj
                if j % 24 == 0:
                    ps = PS()
                for kc in range(8):
                    S.mm(ps[:, (j % 24) * 18:(j % 24) * 18 + 18], wb[:, kc, jj * 128:(jj + 1) * 128], sT[:, kc, :], kc == 0, kc == 7)
                if j % 24 == 23:
                    j0 = j - 23
                    S.tt(modt[:, j0:j0 + 24, :], ps[:, 0:432].re("p (j n) -> p j n", n=18)[:, :, 0:17],
                         badd[:, l, j0:j0 + 24].un(2).bc([128, 24, 17]), ALU.add)
        dm = dmod[l]
        for (which, sc0, sh0, gt0, gpre, gpost) in ((0, 8, 0, 16, 0, 1), (3, 32, 24, 40, 2, 3)):
            S.ts(dm[:, which, :, :], modt[:, sc0:sc0 + 8, :], 1.0, ALU.add)
            S.tt(dm[:, which, :, :], dm[:, which, :, :], gains[:, l, gpre, :].un(2).bc([128, 8, 17]), ALU.mult)
            S.cp(dm[:, which + 1, :, :], modt[:, sh0:sh0 + 8, :])
            S.tt(dm[:, which + 2, :, :], modt[:, gt0:gt0 + 8, :], gains[:, l, gpost, :].un(2).bc([128, 8, 17]), ALU.mult)
    dump("dmod0", dmod[0].v())

    def rms_fm(src_fn, T):
        ps = PS()
        for kc in range(8):
            sq = scr[kc % 2]
            S.act(sq[:, 0:T], src_fn(kc), AF.Square)
            S.mm(ps[:, 0:T], msk[:, M_BLK, :], sq[:, 0:T], kc == 0, kc == 7)
        S.act(pl[1][:, 0:T], ps[:, 0:T], AF.Ln, bias=EPS, scale=1.0 / D)
        S.act(pl[0][:, 0:T], pl[1][:, 0:T], AF.Exp, scale=-0.5)
        return pl[0][:, 0:T]

    def s3(v):
        return v.re("p (b j) -> p b j", j=LS)

    def modcol(l, which, kc, pas):
        if not pas.samp:
            return dmod[l][:, which, kc, 0:1]
        return dmod[l][:, which, kc, 1:17].un(2).bc([128, NSEQ, LS])

    def prenorm(l, which, pas):
        T = pas.T
        rstd = rms_fm(lambda kc: xT[:, kc, 0:T], T)
        for kc in range(8):
            t = pl[2 + kc % 3][:, 0:T]
            S.tt(t, xT[:, kc, 0:T], rstd, ALU.mult)
            if not pas.samp:
                S.act(hT[:, kc, 0:T], t, AF.Identity, bias=modcol(l, which + 1, kc, pas), scale=modcol(l, which, kc, pas))
            else:
                S.tt(s3(t), s3(t), modcol(l, which, kc, pas), ALU.mult)
                S.tt(s3(hT[:, kc, 0:T]), s3(t), modcol(l, which + 1, kc, pas), ALU.add)

    def postnorm_residual(l, which, pas, src):
        T = pas.T
        rstd = rms_fm(lambda kc: src[:, kc, 0:T], T)
        for kc in range(8):
            t = pl[2 + kc % 3][:, 0:T]
            S.tt(t, src[:, kc, 0:T], rstd, ALU.mult)
            if not pas.samp:
                S.stt(xT[:, kc, 0:T], t, modcol(l, which + 2, kc, pas), xT[:, kc, 0:T], ALU.mult, ALU.add)
            else:
                S.tt(s3(t), s3(t), modcol(l, which + 2, kc, pas), ALU.mult)
                S.tt(xT[:, kc, 0:T], xT[:, kc, 0:T], t, ALU.add)

    def sigmoid_from(out, src, scratch):
        S.act(scratch, src, AF.Tanh, scale=0.5)
        S.ts(out, scratch, 0.5, ALU.mult, 0.5, ALU.add)

    def silu_inplace(x, scratch, rnd=False):
        S.act(scratch, x, AF.Tanh, scale=0.5)
        S.stt(scratch, scratch, 1.0, x, ALU.add, ALU.mult)
        S.ts(x, scratch, 0.5, ALU.mult)

    def transposes(src, C, nblk, width=128):
        ps = PS()
        for j in range(nblk):
            S.mm(ps[0:width, j * C:(j + 1) * C], src[0:C, j * width:(j + 1) * width], msk[0:C, M_ID, 0:C], True, True)
        return ps[0:width, 0:nblk * C].re("p (j c) -> p j c", c=C)

    def branch_merge(l, n, pas, first):
        T = pas.T
        for e in range(8):
            wbg = wload(w3(w_in[l][:, O_MG + n * 1024 + e * 128:O_MG + n * 1024 + (e + 1) * 128]), 8, 128)
            wbb = wload(w3(w_branch[l, n][:, e * 128:(e + 1) * 128]), 4, 128)
            pg = PS()
            for kc in range(8):
                S.mm(pg[:, 0:T], wbg[:, kc, :], hT[:, kc, 0:T], kc == 0, kc == 7)
            py = PS()
            for wc in range(4):
                S.mm(py[:, 0:T], wbb[:, wc, :], yT[:, wc, 0:T], wc == 0, wc == 3)
            sg = pl[2 + e % 3][:, 0:T]
            S.act(sg, pg[:, 0:T], AF.Tanh, scale=0.5)
            S.stt(sg, sg, 1.0, py[:, 0:T], ALU.add, ALU.mult)
            if first:
                S.ts(mergedT[:, e, 0:T], sg, 0.5, ALU.mult)
            else:
                S.stt(mergedT[:, e, 0:T], sg, 0.5, mergedT[:, e, 0:T], ALU.mult, ALU.add)
            yield

    def gate_norm_out(pas, o_in, gate_tm, gn_bc, ngrp, gw, ycur):
        C = pas.C
        c1 = pl[0][0:C, :]
        c2 = pl[3][0:C, :]
        st1 = sm[0:C, 64:64 + ngrp]
        st2 = sm[0:C, 96:96 + ngrp]
        S.act(c1, o_in, AF.Square)
        S.red(st1, c1.re("p (g w) -> p g w", w=gw))
        S.act(st2, st1, AF.Ln, bias=EPS, scale=1.0 / gw)
        S.act(st1, st2, AF.Exp, scale=-0.5)
        S.tt(c1.re("p (g w) -> p g w", w=gw), o_in.re("p (g w) -> p g w", w=gw), st1.un(2).bc([C, ngrp, gw]), ALU.mult)
        S.tt(c1.re("p (g w) -> p g w", w=gn_bc.shape[-1]), c1.re("p (g w) -> p g w", w=gn_bc.shape[-1]),
             gn_bc.un(1).bc([C, 512 // gn_bc.shape[-1], gn_bc.shape[-1]]), ALU.mult)
        S.act(c2, gate_tm, AF.Tanh, scale=0.5)
        S.stt(c2, c2, 1.0, gate_tm, ALU.add, ALU.mult)
        S.stt(ycur, c2, 0.5, c1, ALU.mult, ALU.mult)

    def y_to_yT(pas, c, ycur):
        C = pas.C
        ps = transposes(ycur, C, 4)
        evac(yT[:, 0:4, c * C:(c + 1) * C], ps)

    def build_pad(src3):
        return build_pad_into(padz, padz.base, src3)

    def build_pad_into(buf, base, src3):
        dst = bass.AP(base.tensor, base.offset, [list(base.ap[0]), [64 + LS, NSEQ], [1, LS]])
        S.cp(V(buf, dst), src3)
        return V(buf, bass.AP(base.tensor, base.offset, [list(base.ap[0]), [64, NSEQ], [1, 64]]))

    def cum3(pas, G, n, ex):
        C = pas.C
        pc = PS()
        S.mm(pc[0:C, 0:n], M(pas, M_UI), G, True, True)
        S.mm(pc[0:C, n:2 * n], M(pas, M_SL), G, True, True)
        S.mm(pc[0:C, 2 * n:3 * n], M(pas, M_BLK), G, True, True)
        S.act(ex[0:C, 0:3 * n], pc[0:C, 0:3 * n], AF.Exp)

    def seq_totals(G, n):
        S.tt(gz[:, :, 0:n], msk[0:64, M_IND, 0:NSEQ].un(2).bc([64, NSEQ, n]), G.un(1).bc([64, NSEQ, n]), ALU.mult)
        pe_ = PS()
        S.mm(pe_[:, 0:NSEQ * n].re("p (i n) -> p i n", n=n), msk[0:64, M_BLK, :], gz[:, :, 0:n], True, True)
        S.act(elast_all[:, :, 0:n], pe_[:, 0:NSEQ * n].re("p (i n) -> p i n", n=n), AF.Exp)

    def conv_fm(pas, l, ps, raw, out, wcol, bcol, hist_v, scratch):
        T = pas.T
        S.cp(raw, ps, "act")
        S.act(out, ps, AF.Identity, bias=(bcol if bcol is not None else 0.0), scale=wcol(3))
        yield
        if not pas.samp:
            for s_ in (1, 2, 3):
                S.stt(out[:, s_:T], raw[:, 0:T - s_], wcol(3 - s_), out[:, s_:T], ALU.mult, ALU.add)
                S.stt(out[:, 0:s_], hist_v[:, 3 - s_:3], wcol(3 - s_), out[:, 0:s_], ALU.mult, ALU.add)
        else:
            r3, o3 = s3(raw), s3(out)
            for s_ in (1, 2, 3):
                S.stt(o3[:, :, s_:LS], r3[:, :, 0:LS - s_], wcol(3 - s_), o3[:, :, s_:LS], ALU.mult, ALU.add)
                S.stt(o3[:, :, 0:s_], hist_v[:, :, 3 - s_:3], wcol(3 - s_), o3[:, :, 0:s_], ALU.mult, ALU.add)
        yield
        S.act(scratch, out, AF.Tanh)
        S.stt(out, scratch, 1.0, out, ALU.add, ALU.mult)

    def load_hist_sample(bp, cv_l, ntile, name):
        n0 = bp.n
        nsl = (ntile * 128 + 511) // 512
        cvb_ = bp.raw(name, nsl)
        bp.n = n0
        cv2 = V(cvb_, cvb_.base.rearrange("p s c -> p (s c)"))
        S.dma_in("pool", cv2[0:48, 0:ntile * 128], cv_l.rearrange("b j c -> (b j) c"))
        for t0 in range(0, ntile, 4):
            ps = PS()
            for k in range(4):
                t = t0 + k
                S.mm(ps[:, k * 48:(k + 1) * 48], cv2[0:48, t * 128:(t + 1) * 128], msk[0:48, M_ID, 0:48], True, True)
            evac(hist_in[:, t0:t0 + 4, :, :].re("p t b j -> p t (b j)"), ps[:, 0:192].re("p (t n) -> p t n", n=48))

    def proj_conv(w2d, ncols, pas, sink, cache_out):
        T = pas.T
        win = []
        for bi, b0 in enumerate(range(0, ncols, 256)):
            wb = wload(w3(w2d[:, b0:b0 + 256]), 8, 256)
            for j in range(2):
                ps = PS()
                for kc in range(8):
                    S.mm(ps[:, 0:T], wb[:, kc, j * 128:(j + 1) * 128], hT[:, kc, 0:T], kc == 0, kc == 7)
                win.insert(0, sink(b0 // 128 + j, ps[:, 0:T]))
                for g_ in list(win):
                    try:
                        next(g_)
                    except StopIteration:
                        win.remove(g_)
            if pas.samp:
                ps2 = PS()
                for kc in range(8):
                    S.mm(ps2[0:64, 0:256], hT[:, kc, 0:64], wb[:, kc, 0:256], kc == 0, kc == 7)
                stg = scr[2 + bi % 2]
                evac(stg[0:64, 0:256], ps2[0:64, 0:256])
                for j in range(1, LS):
                    S.dma_out("sp", cache_out[:, j - 1, b0:b0 + 256], stg[j:64:LS, 0:256])
        run_gens(win)

    def gla(pas, l):
        T, C, NCH = pas.T, pas.C, pas.NCH
        bp = Bump()
        qT_ = bp.fm("g_qT", 4, pas)
        kT_ = bp.fm("g_kT", 4, pas)
        ktm = bp.tm("g_k", NCH)
        vtm = bp.tm("g_v", NCH)
        grt = bp.tm("g_gr", NCH)
        W = w_in[l]
        proj_fm_wide(W[:, O_GQ:O_GQ + 512], 512, T, lambda t, ps: evac(qT_[:, t, 0:T], ps))
        proj_fm_wide(W[:, O_GK:O_GK + 512], 512, T, lambda t, ps: evac(kT_[:, t, 0:T], ps))
        proj_tm_wide(W[:, O_GK:O_GK + 512], 512, pas, ktm, False)
        proj_tm_wide(W[:, O_GV:O_GV + 512], 512, pas, vtm, True)
        proj_tm_wide(W[:, O_GR:O_GR + 512], 512, pas, grt, False)
        proj_fm(W[:, O_GLR:O_GLR + 16], 8, 16, lambda kc: hT[:, kc, 0:T], T,
                lambda j, ps, m: evac(glrA[0:16, 0:T], ps))
        S.dma_in("pool", wga[0:16, :], w_gate[l])
        S.dma_in("pool", wga[16:17, :], b_gate[l])
        Sg = S_gla[l]
        sts = None
        if pas.samp:
            sts = bp.tm("g_S", NSEQ)
            for i in range(NSEQ):
                S.dma_in("pool", sts[i].re("p (h v) -> p h v", v=128), st_gla[l, i].rearrange("h k v -> k h v"))
        spl, ebx, enb, qdT, kdT, AmT, kd2, ycur = scr[2], pl[1], pl[2], scr[3], scr[4], scr[5], scr[6], scr[0]
        if not pas.samp:
            def gs3(key, parent, ap):
                kk = (key, id(parent))
                if kk not in gsub:
                    gsub[kk] = S.sub(parent, ap, key)
                return gsub[kk]
            gb_ = {}
            for hp in range(2):
                hc = slice(hp * 2 * C, (hp + 1) * 2 * C)
                gb_[hp] = dict(
                    EB=gs3("gEB%d" % hp, ebx, ebx.base[:, hc]),
                    ENB=gs3("gENB%d" % hp, enb, enb.base[:, hc]),
                    QD=gs3("gQD%d" % hp, qdT, qdT.base[:, hc]),
                    KD=gs3("gKD%d" % hp, kdT, kdT.base[:, hc]),
                    AM=gs3("gAM%d" % hp, AmT, AmT.base[:, hc]),
                    YC=gs3("gYC%d" % hp, ycur, ycur.base[:, hp * 256:(hp + 1) * 256]),
                    C1=gs3("gC1%d" % hp, pl[0], pl[0].base[:, hp * 256:(hp + 1) * 256]),
                    C2=gs3("gC2%d" % hp, pl[3], pl[3].base[:, hp * 256:(hp + 1) * 256]),
                    SG=gs3("gSG%d" % hp, Sg, Sg.base[:, 2 * hp:2 * hp + 2, :]),
                    SM=gs3("gSM%d" % hp, sm2, sm2.base[:, hp * 4:hp * 4 + 4]),
                )

        def gla_chain(hp, c, cs):
            B = gb_[hp]
            h0 = 2 * hp
            hv = slice(h0 * 128, (h0 + 2) * 128)
            H2 = 2 * C

            def v3(b):
                return b[:, :].re("p (h c) -> p h c", c=C)
            eb3, enb3, qd3, kd3 = v3(B["EB"]), v3(B["ENB"]), v3(B["QD"]), v3(B["KD"])
            Am3 = B["AM"][0:C, :].re("p (h c) -> p h c", c=C)
            pb = PS()
            for k in range(2):
                S.mm(pb[:, k * C:(k + 1) * C], spl[0:C, (h0 + k) * 128:(h0 + k + 1) * 128], M(pas, M_UI), True, True)
            pb3 = pb[:, 0:H2].re("p (h c) -> p h c", c=C)
            S.act(eb3, pb3, AF.Exp, scale=-1.0 / 16.0)
            S.act(enb3, pb3, AF.Exp, scale=1.0 / 16.0)
            yield
            S.stt(qd3, qT_[:, h0:h0 + 2, cs], float(128 ** -0.5), eb3, ALU.mult, ALU.mult)
            S.tt(kd3, kT_[:, h0:h0 + 2, cs], enb3, ALU.mult)
            yield
            pa = PS()
            for k in range(2):
                S.mm(pa[0:C, k * C:(k + 1) * C], kd3[:, k, :], qd3[:, k, :], True, True)
            S.tt(Am3, pa[0:C, 0:H2].re("p (h c) -> p h c", c=C), M(pas, M_UI).un(1).bc([C, 2, C]), ALU.mult)
            yield
            po = PS(pin=True)
            SGh = B["SG"]
            for k in range(2):
                ks = slice(k * 128, (k + 1) * 128)
                S.mm(po[0:C, ks], Am3[:, k, :], vtm[c][0:C, (h0 + k) * 128:(h0 + k + 1) * 128], True, False)
                S.mm(po[0:C, ks], qd3[:, k, :], SGh[:, k, :], False, True)
            o_in = po[0:C, 0:256]
            c1 = B["C1"][0:C, :]
            c2 = B["C2"][0:C, :]
            st1 = B["SM"][0:C, 0:2]
            st2 = B["SM"][0:C, 2:4]
            gate = grt[c][0:C, hv]
            S.act(c1, o_in, AF.Square)
            S.red(st1, c1.re("p (g w) -> p g w", w=128))
            S.act(st2, st1, AF.Ln, bias=EPS, scale=1.0 / 128)
            S.act(st1, st2, AF.Exp, scale=-0.5)
            yield
            S.tt(c1.re("p (g w) -> p g w", w=128), o_in.re("p (g w) -> p g w", w=128), st1.un(2).bc([C, 2, 128]), ALU.mult)
            unpin(po)
            S.tt(c1.re("p (g w) -> p g w", w=128), c1.re("p (g w) -> p g w", w=128),
                 rv[l][0:C, RV_GLAN:RV_GLAN + 128].un(1).bc([C, 2, 128]), ALU.mult)
            S.act(c2, gate, AF.Tanh, scale=0.5)
            S.stt(c2, c2, 1.0, gate, ALU.add, ALU.mult)
            yc = B["YC"][0:C, :]
            S.stt(yc, c2, 0.5, c1, ALU.mult, ALU.mult)
            yield
            pt_ = PS()
            for k in range(2):
                S.mm(pt_[:, k * C:(k + 1) * C], yc[:, k * 128:(k + 1) * 128], msk[0:C, M_ID, 0:C], True, True)
            S.cp(yT_h[hp][:, :, c * C:(c + 1) * C], pt_[:, 0:H2].re("p (j c) -> p j c", c=C), "act")
            yield
            pu = PS()
            for k in range(2):
                ks = slice(k * 128, (k + 1) * 128)
                S.mm(pu[:, ks], kd2[0:C, (h0 + k) * 128:(h0 + k + 1) * 128], vtm[c][0:C, (h0 + k) * 128:(h0 + k + 1) * 128], True, True)
            for k in range(2):
                ks = slice(k * 128, (k + 1) * 128)
                S.stt(SGh[:, k, :], SGh[:, k, :], eb3[:, k, C - 1:C], pu[:, ks], ALU.mult, ALU.add)

        for c in range(NCH):
            cs = slice(c * C, (c + 1) * C)
            plg = PS()
            S.mm(plg[0:C, :], glrA[0:17, cs], wga[:, :], True, True)
            S.act(pl[0][0:C, :], plg[0:C, :], AF.Exp, scale=-1.0)
            S.act(spl[0:C, :], pl[0][0:C, :], AF.Ln, bias=1.0)
            if not pas.samp:
                pr = PS()
                S.mm(pr[0:C, :], M(pas, M_SL), spl[0:C, :], True, True)
                S.act(pl[4][0:C, :], pr[0:C, :], AF.Exp, scale=-1.0 / 16.0)
                S.tt(kd2[0:C, :], ktm[c][0:C, :], pl[4][0:C, :], ALU.mult)
                gens = [gla_chain(hp, c, cs) for hp in range(2)]
                while gens:
                    for gq_ in list(gens):
                        try:
                            next(gq_)
                        except StopIteration:
                            gens.remove(gq_)
                continue
            pb = PS()
            for h in range(4):
                S.mm(pb[:, h * C:(h + 1) * C], spl[0:C, h * 128:(h + 1) * 128], M(pas, M_UI), True, True)
            pb3 = pb[:, 0:4 * C].re("p (h c) -> p h c", c=C)
            eb3 = ebx[:, 0:4 * C].re("p (h c) -> p h c", c=C)
            enb3 = enb[:, 0:4 * C].re("p (h c) -> p h c", c=C)
            qd3 = qdT[:, 0:4 * C].re("p (h c) -> p h c", c=C)
            kd3 = kdT[:, 0:4 * C].re("p (h c) -> p h c", c=C)
            S.act(eb3, pb3, AF.Exp, scale=-1.0 / 16.0)
            S.act(enb3, pb3, AF.Exp, scale=1.0 / 16.0)
            S.stt(qd3, qT_[:, :, cs], float(128 ** -0.5), eb3, ALU.mult, ALU.mult)
            S.tt(kd3, kT_[:, :, cs], enb3, ALU.mult)
            pa = PS()
            for h in range(4):
                S.mm(pa[0:C, h * C:(h + 1) * C], kd3[:, h, :], qd3[:, h, :], True, True)
            Am3 = AmT[0:C, 0:4 * C].re("p (h c) -> p h c", c=C)
            S.tt(Am3, pa[0:C, 0:4 * C].re("p (h c) -> p h c", c=C), M(pas, M_UI).un(1).bc([C, 4, C]), ALU.mult)
            po = PS()
            for h in range(4):
                hs = slice(h * 128, (h + 1) * 128)
                S.mm(po[0:C, hs], Am3[:, h, :], vtm[c][0:C, hs], True, False)
                if not pas.samp:
                    S.mm(po[0:C, hs], qd3[:, h, :], Sg[:, h, :], False, True)
                else:
                    pad = build_pad(s3(qd3[:, h, :]))
                    for i in range(NSEQ):
                        S.mm(po[0:C, hs], pad[:, i, :], sts[i][:, hs], False, i == NSEQ - 1)
            gate_norm_out(pas, po[0:C, :], grt[c][0:C, :], rv[l][0:C, RV_GLAN:RV_GLAN + 128], 4, 128, ycur[0:C, :])
            y_to_yT(pas, c, ycur)
            pr = PS()
            S.mm(pr[0:C, :], M(pas, M_SL), spl[0:C, :], True, True)
            S.act(pl[0][0:C, :], pr[0:C, :], AF.Exp, scale=-1.0 / 16.0)
            S.tt(kd2[0:C, :], ktm[c][0:C, :], pl[0][0:C, :], ALU.mult)
            if not pas.samp:
                pu = PS()
                for h in range(4):
                    hs = slice(h * 128, (h + 1) * 128)
                    S.mm(pu[:, hs], kd2[0:C, hs], vtm[c][0:C, hs], True, True)
                for h in range(4):
                    hs = slice(h * 128, (h + 1) * 128)
                    S.stt(Sg[:, h, :], Sg[:, h, :], eb3[:, h, C - 1:C], pu[:, hs], ALU.mult, ALU.add)
            else:
                def gla_tail(c=c, eb3=eb3):
                    for i in range(NSEQ):
                        vzi = scr[1]
                        S.ts(vzi[0:C, :], vtm[c][0:C, :], msk[0:C, M_IND, i:i + 1], ALU.mult)
                        pu = PS()
                        for h in range(4):
                            hs = slice(h * 128, (h + 1) * 128)
                            S.mm(pu[:, hs], kd2[0:C, hs], vzi[0:C, hs], True, True)
                        sn = big[1].v().re("p a b -> p (a b)")[:, (i % 2) * 512:(i % 2) * 512 + 512]
                        S.tt(sn.re("p (h v) -> p h v", v=128), sts[i].re("p (h v) -> p h v", v=128),
                             eb3[:, :, LS * i + LS - 1:LS * i + LS].bc([128, 4, 128]), ALU.mult)
                        S.tt(sn, sn, pu[:, :], ALU.add)
                        S.dma_out("sp", o_sg[l, i].rearrange("h k v -> k h v"), sn.re("p (h v) -> p h v", v=128))
                        yield
                tails.append(gla_tail())
        if pas.last and not pas.samp:
            S.dma_out("sp", o_pg[l].rearrange("h k v -> k h v"), Sg.v())

    def ssd(pas, l):
        T, C, NCH = pas.T, pas.C, pas.NCH
        bp = Bump()
        cvo = bp.fm("s_cv", 8, pas)
        cvo_t = [S.sub(cvo.buf, cvo.ap[:, t, :], "s_cv_t%d" % t) for t in range(8)]
        szt = bp.tm("s_z", NCH)
        xtm = bp.tm("s_x", NCH)
        Btm = bp.tm("s_B", NCH)
        W = w_in[l]
        hs_ = hist_s[l]
        if pas.samp:
            load_hist_sample(bp, cv_ssd[l], 8, "s_cvin")

        def conv_sink(t, ps):
            raw = pl[3 + t % 2][:, 0:T]
            g_ = conv_fm(pas, l, ps, raw, cvo_t[t][:, 0:T], lambda j: wsc[:, l, t, j:j + 1], bsc[:, l, t:t + 1],
                         hist_in[:, t, :, :] if pas.samp else hs_[:, t, :], pl[t % 3][:, 0:T])
            next(g_)
            yield
            next(g_)
            if not pas.samp:
                S.cp(hs_[:, t, :], raw[:, T - 3:T])
                if pas.last:
                    S.dma_out("sp", o_pcs[l][:, t * 128:(t + 1) * 128].rearrange("j p -> p j"), hs_[:, t, :],
                              allow_slow_non_contiguous=True)
            yield
            for _ in g_:
                pass

        proj_conv(W[:, O_SXBC:O_SXBC + 1024], 1024, pas, conv_sink, o_scs[l])
        proj_tm_wide(W[:, O_SZ:O_SZ + 512], 512, pas, szt, False)
        for c_ in range(NCH):
            silu_inplace(szt[c_][0:C, :], pl[c_ % 3][0:C, :])
        dts = smallT

        def dt_sink(c, ps):
            evac(dts[0:C, c * 8:(c + 1) * 8], ps)
        proj_tm(W[:, O_SDT:O_SDT + 8], 8, pas, dt_sink)
        ST = S_ssd[l]
        Lh, E, MT = big[0], big[1], big[2]
        ycur = scr[0]
        ex = sm
        nat_slots = bp.tm("s_nat", 3) if pas.samp else None
        if pas.samp:
            CTz = [None, None]
            ctzb = [bp.raw("s_ctz%d" % g, 2) for g in range(2)]
            for g in range(2):
                S.memset(ctzb[g].v(), 0.0)
        if not pas.samp:
            def gs2(key, parent, ap):
                kk = (key, id(parent))
                if kk not in gsub:
                    gsub[kk] = S.sub(parent, ap, key)
                return gsub[kk]
            sb_ = {}
            for g in range(2):
                sb_[g] = dict(
                    LH=gs2("sLH%d" % g, big[0], big[0].base[:, 4 * g:4 * g + 4, :]),
                    E=gs2("sE%d" % g, big[1], big[1].base[:, 4 * g:4 * g + 4, :]),
                    MT=gs2("sMT%d" % g, big[2], big[2].base[:, 4 * g:4 * g + 4, :]),
                    CB=gs2("sCB%d" % g, pl[1], pl[1].base[:, g * 256:(g + 1) * 256]),
                    T1=gs2("sT1%d" % g, pl[2], pl[2].base[:, g * 256:(g + 1) * 256]),
                    T2=gs2("sT2%d" % g, pl[3], pl[3].base[:, g * 256:(g + 1) * 256]),
                    C1=gs2("sC1%d" % g, pl[0], pl[0].base[:, g * 256:(g + 1) * 256]),
                    XW=gs2("sXW%d" % g, scr[3], scr[3].base[:, g * 256:(g + 1) * 256]),
                    YC=gs2("sYC%d" % g, scr[0], scr[0].base[:, g * 256:(g + 1) * 256]),
                    ST=gs2("sST%d" % g, ST, ST.base[:, 4 * g:4 * g + 4, :]),
                    SM=gs2("sSM%d" % g, sm2, sm2.base[:, 8 + g * 4:8 + g * 4 + 4]),
                )

        def ssd_chain(g, c, cs, dt, dtA, wts, ecum, elast):
            B = sb_[g]
            h4 = slice(4 * g, 4 * g + 4)
            gv = slice(g * 256, (g + 1) * 256)
            Lh3 = B["LH"][0:C, :, 0:C]
            E3 = B["E"][0:C, :, 0:C]
            MT3 = B["MT"][0:C, :, 0:C]
            S.tt(Lh3, M(pas, M_UI).un(1).bc([C, 4, C]), dtA[:, h4].un(2).bc([C, 4, C]), ALU.mult)
            yield
            pseg = PS()
            S.mm(pseg[0:C, 0:4 * C].re("p (h c) -> p h c", c=C), M(pas, M_SL), Lh3, True, True)
            S.act(E3, pseg[0:C, 0:4 * C].re("p (h c) -> p h c", c=C), AF.Exp)
            pcb = PS()
            S.mm(pcb[0:C, 0:C], cvo[:, 4 + g, cs], cvo[:, 6 + g, cs], True, True)
            CBm = B["CB"][0:C, 0:C]
            S.tt(CBm, pcb[0:C, 0:C], M(pas, M_UI), ALU.mult)
            yield
            S.tt(E3, E3, dt[:, h4].un(2).bc([C, 4, C]), ALU.mult)
            S.tt(MT3, E3, CBm.un(1).bc([C, 4, C]), ALU.mult)
            yield
            py = PS()
            for hh in range(4):
                h = 4 * g + hh
                S.mm(py[0:C, hh * 64:(hh + 1) * 64], MT3[:, hh, :], xtm[c][0:C, h * 64:(h + 1) * 64], True, True)
            pz = PS()
            STg = B["ST"]
            S.mm(pz[0:C, 0:256], cvo[:, 6 + g, cs], STg[:, :, :].re("p h q -> p (h q)"), True, True)
            t1 = B["T1"][0:C, :]
            S.tt(t1.re("p (h q) -> p h q", q=64), pz[0:C, 0:256].re("p (h q) -> p h q", q=64), ecum[:, h4].un(2).bc([C, 4, 64]), ALU.mult)
            S.tt(t1, t1, py[0:C, 0:256], ALU.add)
            yield
            t2 = B["T2"][0:C, :]
            S.tt(t2.re("p (h q) -> p h q", q=64), xtm[c][0:C, gv].re("p (h q) -> p h q", q=64),
                 rv[l][0:C, RV_SD + 4 * g:RV_SD + 4 * g + 4].un(2).bc([C, 4, 64]), ALU.mult)
            S.tt(t1, t1, t2, ALU.add)
            S.tt(t1, t1, szt[c][0:C, gv], ALU.mult)
            yield
            c1 = B["C1"][0:C, :]
            st1 = B["SM"][0:C, 0:1]
            st2 = B["SM"][0:C, 2:3]
            S.act(c1, t1, AF.Square)
            S.red(st1, c1.re("p (g w) -> p g w", w=256))
            S.act(st2, st1, AF.Ln, bias=EPS, scale=1.0 / 256)
            S.act(st1, st2, AF.Exp, scale=-0.5)
            yield
            yc = B["YC"][0:C, :]
            S.stt(yc, t1, st1, rv[l][0:C, RV_SNORM + g * 256:RV_SNORM + (g + 1) * 256], ALU.mult, ALU.mult)
            pt_ = PS()
            for k in range(2):
                S.mm(pt_[:, k * C:(k + 1) * C], yc[:, k * 128:(k + 1) * 128], msk[0:C, M_ID, 0:C], True, True)
            S.cp(yT_h[g][:, :, c * C:(c + 1) * C], pt_[:, 0:2 * C].re("p (j c) -> p j c", c=C), "act")
            yield
            xw = B["XW"][0:C, :]
            S.tt(xw.re("p (h q) -> p h q", q=64), xtm[c][0:C, gv].re("p (h q) -> p h q", q=64),
                 wts[:, h4].un(2).bc([C, 4, 64]), ALU.mult)
            pu = PS()
            S.mm(pu[:, 0:256], Btm[c][0:C, g * 128:(g + 1) * 128], xw, True, True)
            t3 = B["T2"][:, :]
            S.tt(t3.re("p (h q) -> p h q", q=64), STg.v(), elast[:, h4].un(2).bc([128, 4, 64]), ALU.mult)
            S.tt(STg.v().re("p h q -> p (h q)"), t3, pu[:, 0:256], ALU.add)

        def ssd_pro(c, par, hold):
            cs = slice(c * C, (c + 1) * C)
            smx, smrx = (sm, sm_b)[par], (smr, smr_b)[par]
            px = transposes_fm(cvo, 0, 4, cs, C)
            S.cp(xtm[c][0:C, :].re("p (j w) -> p j w", w=128), px, "act")
            yield
            pB = transposes_fm(cvo, 4, 2, cs, C)
            S.cp(Btm[c][0:C, 0:256].re("p (j w) -> p j w", w=128), pB, "act")
            yield
            dt = smx[0:C, 24:32]
            dtA = smrx[0:C, 0:8]
            wts = smx[0:C, 40:48]
            S.tt(dt, dts[0:C, c * 8:(c + 1) * 8], rv[l][0:C, RV_SDTB:RV_SDTB + 8], ALU.add)
            S.act(dt, dt, AF.Exp)
            S.act(dt, dt, AF.Ln, bias=1.0)
            S.tt(dtA, dt, negA[0:C, l, 0:8], ALU.mult)
            yield
            cum3(pas, dtA, 8, smx)
            ecum, erev, elast = smx[0:C, 0:8], smx[0:C, 8:16], smx[0:C, 16:24]
            S.tt(wts, erev, dt, ALU.mult)
            hold["res"] = (cs, dt, dtA, wts, ecum, erev, elast, smx)

        if not pas.samp:
            hold = {}
            run_gens([ssd_pro(0, 0, hold)])
            for c in range(NCH):
                nhold = {}
                cs, dt, dtA, wts, ecum, erev, elast, ex = hold["res"]
                gens = [ssd_chain(g, c, cs, dt, dtA, wts, ecum, elast) for g in range(2)]
                if c + 1 < NCH:
                    gens.append(ssd_pro(c + 1, (c + 1) % 2, nhold))
                run_gens(gens)
                hold = nhold
        for c in (range(NCH) if pas.samp else []):
            hold = {}
            run_gens([ssd_pro(c, 0, hold)])
            cs, dt, dtA, wts, ecum, erev, elast, ex = hold["res"]
            if not pas.samp:
                continue
            Lh3 = Lh[0:C, :, 0:C]
            S.tt(Lh3, M(pas, M_SL).un(1).bc([C, 8, C]), dtA.un(2).bc([C, 8, C]), ALU.mult)
            for half in range(2):
                pseg = PS()
                for hh in range(4):
                    h = half * 4 + hh
                    S.mm(pseg[0:C, hh * C:(hh + 1) * C], Lh3[:, h, :], M(pas, M_UI), True, True)
                S.act(E[0:C, half * 4:(half + 1) * 4, 0:C], pseg[0:C, 0:4 * C].re("p (h c) -> p h c", c=C), AF.Exp)
            pcb = PS()
            for g in range(2):
                S.mm(pcb[0:C, g * C:(g + 1) * C], cvo[:, 4 + g, cs], cvo[:, 6 + g, cs], True, True)
            CBm = pl[1][0:C, 0:2 * C].re("p (g c) -> p g c", c=C)
            S.tt(CBm, pcb[0:C, 0:2 * C].re("p (g c) -> p g c", c=C), M(pas, M_UI).un(1).bc([C, 2, C]), ALU.mult)
            E3 = E[0:C, :, 0:C]
            MT3 = MT[0:C, :, 0:C]
            S.tt(E3, E3, dt.un(2).bc([C, 8, C]), ALU.mult)
            for g in range(2):
                S.tt(MT3[:, g * 4:(g + 1) * 4, :], E3[:, g * 4:(g + 1) * 4, :], CBm[:, g:g + 1, :].bc([C, 4, C]), ALU.mult)
            py = PS(pin=True)
            for h in range(8):
                S.mm(py[0:C, h * 64:(h + 1) * 64], MT3[:, h, :], xtm[c][0:C, h * 64:(h + 1) * 64], True, True)
            pz = PS(pin=True)
            if not pas.samp:
                for g in range(2):
                    S.mm(pz[0:C, g * 256:(g + 1) * 256], cvo[:, 6 + g, cs], ST[:, g * 4:(g + 1) * 4, :].re("p h q -> p (h q)"), True, True)
            else:
                for g in range(2):
                    CTz[g] = build_pad_into(ctzb[g], ctzb[g].base, s3(cvo[:, 6 + g, 0:T]))
                S.mm(pz[0:C, :], zt[:, 0:C], msk[:, 0:4, :].re("p a b -> p (a b)"), True, True)
                xw = scr[3]
                S.tt(xw[0:C, :].re("p (h q) -> p h q", q=64), xtm[c][0:C, :].re("p (h q) -> p h q", q=64),
                     wts.un(2).bc([C, 8, 64]), ALU.mult)
                dtrep = scr[2][0:C, :].re("p (a m) -> p a m", m=128)
                for a in range(4):
                    S.cp(dtrep[:, a, :].re("p (b q) -> p b q", q=64), dtA[:, 2 * a:2 * a + 2].un(2).bc([C, 2, 64]))
                pe_ = PS()
                for a in range(4):
                    S.mm(pe_[:, a * NSEQ:(a + 1) * NSEQ], dtrep[:, a, :], msk[0:C, M_IND, 0:NSEQ], True, True)
                S.act(ecol.v().re("p a i -> p (a i)"), pe_[:, 0:4 * NSEQ], AF.Exp)
                for i in range(NSEQ):
                    nat = nat_slots[i % 3].re("p (a n) -> p a n", n=128)
                    S.dma_in("pool", nat, st_ssd[l, i].rearrange("(a b) q n -> (b q) a n", b=2))
                    pt = PS()
                    for a in range(4):
                        S.mm(pt[:, a * 128:(a + 1) * 128], nat[:, a, :], msk[:, M_ID, :], True, True)
                    sti = scr[(5, 6)[i % 2]]
                    evac(sti[:, :], pt[:, :])
                    for g in range(2):
                        S.mm(pz[0:C, g * 256:(g + 1) * 256], CTz[g][:, i, :], sti[:, g * 256:(g + 1) * 256], False, i == NSEQ - 1)
                    Bz = scr[1]
                    S.ts(Bz[0:C, 0:256], Btm[c][0:C, 0:256], msk[0:C, M_IND, i:i + 1], ALU.mult)
                    pu = PS()
                    for a in range(4):
                        g = a // 2
                        S.mm(pu[:, a * 128:(a + 1) * 128], xw[0:C, a * 128:(a + 1) * 128], Bz[0:C, g * 128:(g + 1) * 128], True, True)
                    sn = big[1].v().re("p a b -> p (a b)")[:, (i % 2) * 512:(i % 2) * 512 + 512].re("p (a n) -> p a n", n=128)
                    for a in range(4):
                        S.stt(sn[:, a, :], nat[:, a, :], ecol[:, a, i:i + 1], pu[:, a * 128:(a + 1) * 128], ALU.mult, ALU.add)
                    S.dma_out("sp", o_ss[l, i].rearrange("(a b) q n -> (b q) a n", b=2), sn)
            t1 = pl[2][0:C, :]
            S.tt(t1.re("p (h q) -> p h q", q=64), pz[0:C, :].re("p (h q) -> p h q", q=64), ecum.un(2).bc([C, 8, 64]), ALU.mult)
            S.tt(t1, t1, py[0:C, :], ALU.add)
            unpin(py)
            unpin(pz)
            t2 = pl[3][0:C, :]
            S.tt(t2.re("p (h q) -> p h q", q=64), xtm[c][0:C, :].re("p (h q) -> p h q", q=64),
                 rv[l][0:C, RV_SD:RV_SD + 8].un(2).bc([C, 8, 64]), ALU.mult)
            S.tt(t1, t1, t2, ALU.add)
            S.tt(t1, t1, szt[c][0:C, :], ALU.mult)
            c1 = pl[0][0:C, :]
            st1 = sm[0:C, 64:66]
            st2 = sm[0:C, 96:98]
            S.act(c1, t1, AF.Square)
            S.red(st1, c1.re("p (g w) -> p g w", w=256))
            S.act(st2, st1, AF.Ln, bias=EPS, scale=1.0 / 256)
            S.act(st1, st2, AF.Exp, scale=-0.5)
            S.tt(t1.re("p (g w) -> p g w", w=256), t1.re("p (g w) -> p g w", w=256), st1.un(2).bc([C, 2, 256]), ALU.mult)
            S.tt(ycur[0:C, :], t1, rv[l][0:C, RV_SNORM:RV_SNORM + 512], ALU.mult)
            y_to_yT(pas, c, ycur)
            if not pas.samp:
                xw = scr[3]
                S.tt(xw[0:C, :].re("p (h q) -> p h q", q=64), xtm[c][0:C, :].re("p (h q) -> p h q", q=64),
                     wts.un(2).bc([C, 8, 64]), ALU.mult)
                pu = PS()
                for g in range(2):
                    S.mm(pu[:, g * 256:(g + 1) * 256], Btm[c][0:C, g * 128:(g + 1) * 128], xw[0:C, g * 256:(g + 1) * 256], True, True)
                t3 = pl[3]
                S.tt(t3.v().re("p (h q) -> p h q", q=64), ST.v(), elast.un(2).bc([128, 8, 64]), ALU.mult)
                S.tt(ST.v().re("p h q -> p (h q)"), t3.v(), pu[:, :], ALU.add)
        if pas.last and not pas.samp:
            pt = PS()
            for a in range(4):
                S.mm(pt[:, a * 128:(a + 1) * 128], ST[:, 2 * a:2 * a + 2, :].re("p h q -> p (h q)"), msk[:, M_ID, :], True, True)
            sn = big[1].v().re("p a b -> p (a b)")[:, 0:512]
            evac(sn, pt[:, :])
            S.dma_out("sp", o_ps[l].rearrange("(a b) q n -> (b q) a n", b=2), sn.re("p (a n) -> p a n", n=128))

    def transposes_fm(fmv, t0, nt, cs, C):
        ps = PS()
        for j in range(nt):
            S.mm(ps[0:C, j * 128:(j + 1) * 128], fmv[:, t0 + j, cs], msk[:, M_ID, :], True, True)
        return ps[0:C, 0:nt * 128].re("p (j w) -> p j w", w=128)

    def gdn(pas, l):
        T, C, NCH = pas.T, pas.C, pas.NCH
        bp = Bump()
        cvo = bp.fm("d_cv", 12, pas)
        cvo_t = [S.sub(cvo.buf, cvo.ap[:, t, :], "d_cv_t%d" % t) for t in range(12)]
        dgt = bp.tm("d_g", NCH)
        extra = bp.tm("d_x", 4)
        vtm_, kbtm, kbT_, ycur = extra
        W = w_in[l]
        hd_ = hist_d[l]
        if pas.samp:
            load_hist_sample(bp, cv_gdn[l], 12, "d_cvin")
        norm_todo = []

        def conv_sink(t, ps):
            raw = pl[3 + t % 2][:, 0:T]
            g_ = conv_fm(pas, l, ps, raw, cvo_t[t][:, 0:T], lambda j: wdc[:, l, t, j:j + 1], None,
                         hist_in[:, t, :, :] if pas.samp else hd_[:, t, :], pl[t % 3][:, 0:T])
            next(g_)
            yield
            next(g_)
            if not pas.samp:
                S.cp(hd_[:, t, :], raw[:, T - 3:T])
                if pas.last:
                    S.dma_out("sp", o_pcd[l][:, t * 128:(t + 1) * 128].rearrange("j p -> p j"), hd_[:, t, :],
                              allow_slow_non_contiguous=True)
            yield
            for _ in g_:
                pass
            if t < 8:
                norm_todo.append(t)

        def l2norm_tiles():
            for t in norm_todo:
                sq = scr[t % 2][:, 0:T]
                S.act(sq, cvo_t[t][:, 0:T], AF.Square)
                pn = PS()
                S.mm(pn[:, 0:T], msk[:, M_BLK, :], sq, True, True)
                S.act(sq, pn[:, 0:T], AF.Ln, bias=EPS)
                S.act(sq, sq, AF.Exp, scale=-0.5, bias=(math.log(128 ** -0.5) if t < 4 else 0.0))
                S.tt(cvo_t[t][:, 0:T], cvo_t[t][:, 0:T], sq, ALU.mult)

        proj_conv(W[:, O_DQKV:O_DQKV + 1536], 1536, pas, conv_sink, o_scd[l])
        l2norm_tiles()
        proj_tm_wide(W[:, O_DG:O_DG + 512], 512, pas, dgt, False)
        dab = smallT

        def ab_sink(c, ps):
            evac(dab[0:C, c * 8:(c + 1) * 8], ps)
        proj_tm(W[:, O_DA:O_DA + 8], 8, pas, ab_sink)
        S.act(betaAll[0:C, 0:4 * NCH].re("p (c h) -> p c h", h=4),
              dab[0:C, 0:8 * NCH].re("p (c h) -> p c h", h=8)[:, :, 4:8], AF.Tanh, scale=0.5)
        S.ts(betaAll[0:C, 0:4 * NCH], betaAll[0:C, 0:4 * NCH], 0.5, ALU.mult, 0.5, ALU.add)
        Sd = S_gdn[l]
        sts = None
        if pas.samp:
            sts = bp.tm("d_S", NSEQ)
            for i in range(NSEQ):
                S.dma_in("pool", sts[i].re("p (h v) -> p h v", v=128), st_gdn[l, i].rearrange("h k v -> k h v"))
        nlev = 1 if pas.samp else 5
        GM = {M_UI: M_GUI, M_SL: M_GSL, M_SU: M_GSU, M_BLK: M_GBLK}

        def mk(idx):
            return M(pas, idx) if pas.samp else msk[0:C, GM[idx], 0:C]
        blocks = [(0, 64)] if pas.samp else [(0, 64), (64, 128)]
        if not pas.samp:
            NHD = 2
            H2 = NHD * C

            def gs(key, parent, ap):
                kk = (key, id(parent))
                if kk not in gsub:
                    gsub[kk] = S.sub(parent, ap, key)
                return gsub[kk]
            cb = {}
            for hp in range(4 // NHD):
                q0, q1 = hp * NHD, (hp + 1) * NHD
                tg = "n%d_%d" % (NHD, hp)
                cb[hp] = dict(
                    LA=gs("LA" + tg, big[0], big[0].base[:, q0:q1, :]),
                    LB=gs("LB" + tg, big[0], big[0].base[:, 4 + q0:4 + q1, :]),
                    EA=gs("EA" + tg, big[1], big[1].base[:, q0:q1, :]),
                    EB=gs("EB" + tg, big[1], big[1].base[:, 4 + q0:4 + q1, :]),
                    AQ=gs("AQ" + tg, scr[2], scr[2].base[:, hp * H2:(hp + 1) * H2]),
                    P=gs("P" + tg, pl[1], pl[1].base[:, hp * H2:(hp + 1) * H2]),
                    X0=gs("X0" + tg, pl[2], pl[2].base[:, hp * H2:(hp + 1) * H2]),
                    XT0=gs("XT0" + tg, pl[3], pl[3].base[:, hp * H2:(hp + 1) * H2]),
                    X1=gs("X1" + tg, pl[4], pl[4].base[:, hp * H2:(hp + 1) * H2]),
                    XT1=gs("XT1" + tg, pl[0], pl[0].base[:, hp * H2:(hp + 1) * H2]),
                    WT=gs("WT" + tg, scr[3], scr[3].base[:, hp * H2:(hp + 1) * H2]),
                    VN=gs("VN" + tg, scr[4], scr[4].base[:, q0 * 128:q1 * 128]),
                    SD=gs("SD" + tg, Sd, Sd.base[:, q0:q1, :]),
                    ST=gs("ST" + tg, sm2, sm2.base[:, q0 * 2:q1 * 2]),
                    YC=gs("YC" + tg, ycur.buf, ycur.ap[:, q0 * 128:q1 * 128]),
                    YT=gs("YT" + tg, yT, yT.base[:, q0:q1, :]),
                )

        def gdn_chain(hp, c, cs, beta, gd, bec, ecum, kn, kbT3, kw, el2, vt):
            B = cb[hp]
            h0 = NHD * hp
            HW = NHD * 128
            hv = slice(h0 * 128, (h0 + NHD) * 128)

            def v3(buf):
                return buf[0:C, :].re("p (h c) -> p h c", c=C) if len(buf.base.shape) == 2 else buf[0:C, :, 0:C]
            LA, LB, EA, EB = v3(B["LA"]), v3(B["LB"]), v3(B["EA"]), v3(B["EB"])
            AqT, P = v3(B["AQ"]), v3(B["P"])
            gdh = gd[:, h0:h0 + NHD]
            S.tt(LA, mk(M_SL).un(1).bc([C, NHD, C]), gdh.un(2).bc([C, NHD, C]), ALU.mult)
            S.tt(LB, mk(M_UI).un(1).bc([C, NHD, C]), gdh.un(2).bc([C, NHD, C]), ALU.mult)
            yield
            psa = PS()
            S.mm(psa[0:C, 0:H2].re("p (h c) -> p h c", c=C), mk(M_SL), LB, True, True)
            S.act(EA, psa[0:C, 0:H2].re("p (h c) -> p h c", c=C), AF.Exp)
            psb = PS()
            S.mm(psb[0:C, 0:H2].re("p (h c) -> p h c", c=C), mk(M_UI), LA, True, True)
            S.act(EB, psb[0:C, 0:H2].re("p (h c) -> p h c", c=C), AF.Exp)
            yield
            pgm = PS()
            for k in range(NHD):
                S.mm(pgm[0:C, k * C:(k + 1) * C], kbT3[:, h0 + k, :], kbT3[:, h0 + k, :], True, True)
            pg3 = pgm[0:C, 0:H2].re("p (h c) -> p h c", c=C)
            paq = PS()
            for k in range(NHD):
                S.mm(paq[0:C, k * C:(k + 1) * C], cvo[:, 4 + h0 + k, cs], cvo[:, h0 + k, cs], True, True)
            S.tt(EA, EA, mk(M_UI).un(1).bc([C, NHD, C]), ALU.mult)
            S.tt(AqT, paq[0:C, 0:H2].re("p (h c) -> p h c", c=C), EA, ALU.mult)
            X, XT = EA, EB
            S.tt(X, pg3, EA, ALU.mult)
            S.tt(XT, pg3, EB, ALU.mult)
            yield
            S.stt(X, X, -1.0, msk[0:C, M_GSU, 0:C].un(1).bc([C, NHD, C]), ALU.mult, ALU.mult)
            S.stt(XT, XT, -1.0, mk(M_SL).un(1).bc([C, NHD, C]), ALU.mult, ALU.mult)
            S.tt(P, X, msk[0:C, M_ID, 0:C].un(1).bc([C, NHD, C]), ALU.add)
            yield
            Xc, XTc = X, XT
            for lev in range(nlev):
                lastlev = (lev == nlev - 1)
                Xn = v3(B["X0"] if lev % 2 == 0 else B["X1"])
                XTn = v3(B["XT0"] if lev % 2 == 0 else B["XT1"])
                pxt = PS()
                for k in range(NHD):
                    S.mm32(pxt[0:C, k * C:(k + 1) * C], Xc[:, k, :], XTc[:, k, :], True, True)
                S.cp(XTn, pxt[0:C, 0:H2].re("p (h c) -> p h c", c=C), "act")
                if not lastlev:
                    px_ = PS()
                    for k in range(NHD):
                        S.mm32(px_[0:C, k * C:(k + 1) * C], XTc[:, k, :], Xc[:, k, :], True, True)
                    S.cp(Xn, px_[0:C, 0:H2].re("p (h c) -> p h c", c=C), "act")
                yield
                pp = PS()
                for k in range(NHD):
                    S.mm32(pp[0:C, k * C:(k + 1) * C], XTn[:, k, :], P[:, k, :], True, True)
                S.tt(P, P, pp[0:C, 0:H2].re("p (h c) -> p h c", c=C), ALU.add)
                Xc, XTc = Xn, XTn
                yield
            vb = B["LA"][0:C, :, :]
            kbd = B["LB"][0:C, :, :]
            S.tt(vb, vt[0:C, hv].re("p (h w) -> p h w", w=128), beta[:, h0:h0 + NHD].un(2).bc([C, NHD, 128]), ALU.mult)
            S.tt(kbd, kn[0:C, hv].re("p (h w) -> p h w", w=128), bec[:, h0:h0 + NHD].un(2).bc([C, NHD, 128]), ALU.mult)
            yield
            pw = PS()
            for k in range(NHD):
                S.mm32(pw[:, k * C:(k + 1) * C], kbd[:, k, :], P[:, k, :], True, True)
            WT = v3(B["WT"])
            S.ts(WT, pw[:, 0:H2].re("p (h c) -> p h c", c=C), -1.0, ALU.mult)
            yield
            vnew = B["VN"]
            SDh = B["SD"]
            o_sb = B["X0"][0:C, :]
            for b, (r0, r1) in enumerate(blocks):
                rs = slice(r0, r1)
                pvn = PS()
                pqs = PS()
                for k in range(NHD):
                    ks = slice(k * 128, (k + 1) * 128)
                    S.mm32(pvn[0:C, ks], P[:, k, :], vb[:, k, :], True, False)
                    S.mm32(pvn[0:C, ks], WT[:, k, :], SDh[:, k, :], False, True)
                evac(vnew[rs, :], pvn[rs, 0:HW])
                for k in range(NHD):
                    ks = slice(k * 128, (k + 1) * 128)
                    S.mm(pqs[0:C, ks], cvo[:, h0 + k, cs], SDh[:, k, :], True, True)
                S.tt(o_sb[rs, :].re("p (h w) -> p h w", w=128), pqs[rs, 0:HW].re("p (h w) -> p h w", w=128),
                     ecum[rs, h0:h0 + NHD].un(2).bc([r1 - r0, NHD, 128]), ALU.mult)
                yield
                pu = PS()
                for k in range(NHD):
                    ks = slice(k * 128, (k + 1) * 128)
                    S.mm(pu[:, ks], kw[rs, (h0 + k) * 128:(h0 + k + 1) * 128], vnew[rs, ks], True, True)
                t3 = B["XT1"][:, :]
                S.tt(t3.re("p (h w) -> p h w", w=128), SDh.v(), el2[:, b, h0:h0 + NHD].un(2).bc([128, NHD, 128]), ALU.mult)
                S.tt(SDh.v(), t3.re("p (h w) -> p h w", w=128), pu[:, 0:HW].re("p (h w) -> p h w", w=128), ALU.add)
                yield
            pav = PS()
            for k in range(NHD):
                ks = slice(k * 128, (k + 1) * 128)
                S.mm(pav[0:C, ks], AqT[:, k, :], vnew[0:C, ks], True, True)
            S.tt(o_sb, o_sb, pav[0:C, 0:HW], ALU.add)
            yield
            c1 = B["XT0"][0:C, :]
            c2 = B["X1"][0:C, :]
            st1 = B["ST"][0:C, 0:NHD]
            st2 = B["ST"][0:C, NHD:2 * NHD]
            gate = dgt[c][0:C, hv]
            S.act(c1, o_sb, AF.Square)
            S.red(st1, c1.re("p (g w) -> p g w", w=128))
            S.act(st2, st1, AF.Ln, bias=EPS, scale=1.0 / 128)
            S.act(st1, st2, AF.Exp, scale=-0.5)
            yield
            S.tt(c1.re("p (g w) -> p g w", w=128), o_sb.re("p (g w) -> p g w", w=128), st1.un(2).bc([C, NHD, 128]), ALU.mult)
            S.tt(c1.re("p (g w) -> p g w", w=128), c1.re("p (g w) -> p g w", w=128),
                 rv[l][0:C, RV_DNORM:RV_DNORM + 128].un(1).bc([C, NHD, 128]), ALU.mult)
            S.act(c2, gate, AF.Tanh, scale=0.5)
            S.stt(c2, c2, 1.0, gate, ALU.add, ALU.mult)
            yc = B["YC"][0:C, :]
            S.stt(yc, c2, 0.5, c1, ALU.mult, ALU.mult)
            yield
            pt_ = PS()
            for k in range(NHD):
                S.mm(pt_[:, k * C:(k + 1) * C], yc[:, k * 128:(k + 1) * 128], msk[0:C, M_ID, 0:C], True, True)
            evac(B["YT"][:, :, c * C:(c + 1) * C], pt_[:, 0:H2].re("p (j c) -> p j c", c=C))

        alt_ = bp.tm("d_alt", 3) if not pas.samp else None

        def gdn_pro(c, par, hold):
            cs = slice(c * C, (c + 1) * C)
            smx, smrx = (sm, sm_b)[par], (smr, smr_b)[par]
            vt = vtm_ if par == 0 else alt_[0]
            kbt = kbtm if par == 0 else alt_[1]
            kbTx = kbT_ if par == 0 else alt_[2]
            kn = scr[6] if par == 0 else scr[0]
            kw = scr[5] if par == 0 else scr[1]
            beta = betaAll[0:C, c * 4:(c + 1) * 4]
            gd = smrx[0:C, 0:4]
            bec = smx[0:C, 32:36]
            S.tt(gd, dab[0:C, c * 8:c * 8 + 4], rv[l][0:C, RV_DDTB:RV_DDTB + 4], ALU.add)
            S.act(gd, gd, AF.Exp)
            S.act(gd, gd, AF.Ln, bias=1.0)
            S.tt(gd, gd, negA[0:C, l, 8:12], ALU.mult)
            yield
            pc = PS()
            S.mm(pc[0:C, 0:4], mk(M_UI), gd, True, True)
            S.mm(pc[0:C, 4:8], mk(M_SL), gd, True, True)
            S.mm(pc[0:C, 8:12], mk(M_BLK), gd, True, True)
            S.act(smx[0:C, 0:12], pc[0:C, 0:12], AF.Exp)
            ecum, erev, elast = smx[0:C, 0:4], smx[0:C, 4:8], smx[0:C, 8:12]
            S.tt(bec, beta, ecum, ALU.mult)
            yield
            pv = transposes_fm(cvo, 8, 4, cs, C)
            evac(vt[0:C, :].re("p (j w) -> p j w", w=128), pv)
            yield
            pk = transposes_fm(cvo, 4, 4, cs, C)
            evac(kn[0:C, :].re("p (j w) -> p j w", w=128), pk)
            yield
            S.tt(kbt[0:C, :].re("p (h w) -> p h w", w=128), kn[0:C, :].re("p (h w) -> p h w", w=128),
                 beta.un(2).bc([C, 4, 128]), ALU.mult)
            yield
            pkb = transposes(kbt, C, 4)
            kbT3 = kbTx[:, 0:4 * C].re("p (h c) -> p h c", c=C)
            evac(kbT3, pkb)
            yield
            el2 = None
            if not pas.samp:
                S.tt(kw[0:C, :].re("p (h w) -> p h w", w=128), kn[0:C, :].re("p (h w) -> p h w", w=128),
                     erev.un(2).bc([C, 4, 128]), ALU.mult)
                gz2 = smrx[:, 8:16].re("p (b h) -> p b h", h=4)
                for b, (r0, r1) in enumerate(blocks):
                    S.ts(gz2[:, b, :], gd, msk[:, M_GBLK, r0:r0 + 1], ALU.mult)
                pe2 = PS()
                S.mm(pe2[:, 0:8], msk[:, M_BLK, :], smrx[:, 8:16], True, True)
                el2 = smx[:, 48:56].re("p (b h) -> p b h", h=4)
                S.act(smx[:, 48:56], pe2[:, 0:8], AF.Exp)
            hold["res"] = (cs, beta, gd, bec, ecum, erev, elast, kn, kbT3, kw, el2, vt, smx)

        if not pas.samp:
            hold = {}
            run_gens([gdn_pro(0, 0, hold)])
            for c in range(NCH):
                nhold = {}
                cs, beta, gd, bec, ecum, erev, elast, kn, kbT3, kw, el2, vt, ex = hold["res"]
                gens = [gdn_chain(hp, c, cs, beta, gd, bec, ecum, kn, kbT3, kw, el2, vt) for hp in range(4 // NHD)]
                if c + 1 < NCH:
                    gens.append(gdn_pro(c + 1, (c + 1) % 2, nhold))
                run_gens(gens)
                hold = nhold
        for c in (range(NCH) if pas.samp else []):
            hold = {}
            run_gens([gdn_pro(c, 0, hold)])
            cs, beta, gd, bec, ecum, erev, elast, kn, kbT3, kw, el2, vt, ex = hold["res"]
            Lh = big[0]
            S.tt(Lh[0:C, 0:4, 0:C], mk(M_SL).un(1).bc([C, 4, C]), gd.un(2).bc([C, 4, C]), ALU.mult)
            S.tt(Lh[0:C, 4:8, 0:C], mk(M_UI).un(1).bc([C, 4, C]), gd.un(2).bc([C, 4, C]), ALU.mult)
            EA = big[1][0:C, 0:4, 0:C]
            EB = big[1][0:C, 4:8, 0:C]
            psa = PS()
            for h in range(4):
                S.mm(psa[0:C, h * C:(h + 1) * C], Lh[0:C, h, 0:C], mk(M_UI), True, True)
            S.act(EA, psa[0:C, 0:4 * C].re("p (h c) -> p h c", c=C), AF.Exp)
            psb = PS()
            for h in range(4):
                S.mm(psb[0:C, h * C:(h + 1) * C], Lh[0:C, 4 + h, 0:C], mk(M_SL), True, True)
            S.act(EB, psb[0:C, 0:4 * C].re("p (h c) -> p h c", c=C), AF.Exp)
            pgm = PS()
            for h in range(4):
                S.mm(pgm[0:C, h * C:(h + 1) * C], kbT3[:, h, :], kbT3[:, h, :], True, True)
            pg3 = pgm[0:C, 0:4 * C].re("p (h c) -> p h c", c=C)
            S.tt(EA, EA, mk(M_UI).un(1).bc([C, 4, C]), ALU.mult)
            paq = PS()
            for h in range(4):
                S.mm(paq[0:C, h * C:(h + 1) * C], cvo[:, 4 + h, cs], cvo[:, h, cs], True, True)
            AqT = scr[2][0:C, 0:4 * C].re("p (h c) -> p h c", c=C)
            S.tt(AqT, paq[0:C, 0:4 * C].re("p (h c) -> p h c", c=C), EA, ALU.mult)
            X, XT = EA, EB
            S.tt(X, pg3, EA, ALU.mult)
            S.tt(X, X, mk(M_SU).un(1).bc([C, 4, C]), ALU.mult)
            S.ts(X, X, -1.0, ALU.mult)
            S.tt(XT, pg3, EB, ALU.mult)
            S.tt(XT, XT, mk(M_SL).un(1).bc([C, 4, C]), ALU.mult)
            S.ts(XT, XT, -1.0, ALU.mult)
            P = pl[1][0:C, 0:4 * C].re("p (h c) -> p h c", c=C)
            S.tt(P, X, msk[0:C, M_ID, 0:C].un(1).bc([C, 4, C]), ALU.add)
            Xc, XTc = X, XT
            for lev in range(nlev):
                lastlev = (lev == nlev - 1)
                Xn = (pl[2] if lev % 2 == 0 else pl[4])[0:C, 0:4 * C].re("p (h c) -> p h c", c=C)
                XTn = (pl[3] if lev % 2 == 0 else pl[0])[0:C, 0:4 * C].re("p (h c) -> p h c", c=C)
                pxt = PS()
                for h in range(4):
                    S.mm32(pxt[0:C, h * C:(h + 1) * C], Xc[:, h, :], XTc[:, h, :], True, True)
                S.cp(XTn, pxt[0:C, 0:4 * C].re("p (h c) -> p h c", c=C), "act")
                if not lastlev:
                    px_ = PS()
                    for h in range(4):
                        S.mm32(px_[0:C, h * C:(h + 1) * C], XTc[:, h, :], Xc[:, h, :], True, True)
                    S.cp(Xn, px_[0:C, 0:4 * C].re("p (h c) -> p h c", c=C), "dve")
                pp = PS()
                for h in range(4):
                    S.mm32(pp[0:C, h * C:(h + 1) * C], msk[0:C, M_ID, 0:C], P[:, h, :], True, False)
                    S.mm32(pp[0:C, h * C:(h + 1) * C], XTn[:, h, :], P[:, h, :], False, True)
                S.cp(P, pp[0:C, 0:4 * C].re("p (h c) -> p h c", c=C), "dve")
                Xc, XTc = Xn, XTn
            if c == 0 and l == 0 and pas.idx == 0 and not pas.samp:
                dump("d_kn", kn[0:C, :]); dump("d_beta", beta); dump("d_gd", gd); dump("d_vtm", vtm_[0:C, :])
                dump("d_P", P); dump("d_X", X); dump("d_AqT", AqT); dump("d_ex", ex[0:C, 0:12])
            vb = big[0][0:C, 0:4, :]
            kbd = big[0][0:C, 4:8, :]
            S.tt(vb, vtm_[0:C, :].re("p (h w) -> p h w", w=128), beta.un(2).bc([C, 4, 128]), ALU.mult)
            S.tt(kbd, kn[0:C, :].re("p (h w) -> p h w", w=128), bec.un(2).bc([C, 4, 128]), ALU.mult)
            pw = PS()
            for h in range(4):
                S.mm32(pw[:, h * C:(h + 1) * C], kbd[:, h, :], P[:, h, :], True, True)
            WT = scr[3][:, 0:4 * C].re("p (h c) -> p h c", c=C)
            S.ts(WT, pw[:, 0:4 * C].re("p (h c) -> p h c", c=C), -1.0, ALU.mult)
            vnew = scr[4]
            kw = scr[5]
            S.tt(kw[0:C, :].re("p (h w) -> p h w", w=128), kn[0:C, :].re("p (h w) -> p h w", w=128),
                 erev.un(2).bc([C, 4, 128]), ALU.mult)
            o_sb = pl[2][0:C, :]
            if not pas.samp:
                gz2 = smr[:, 8:16].re("p (b h) -> p b h", h=4)
                for b, (r0, r1) in enumerate(blocks):
                    S.ts(gz2[:, b, :], gd, msk[:, M_GBLK, r0:r0 + 1], ALU.mult)
                pe2 = PS()
                S.mm(pe2[:, 0:8], msk[:, M_BLK, :], smr[:, 8:16], True, True)
                el2 = sm[:, 48:56].re("p (b h) -> p b h", h=4)
                S.act(sm[:, 48:56], pe2[:, 0:8], AF.Exp)
            for b, (r0, r1) in enumerate(blocks):
                rs = slice(r0, r1)
                pvn = PS()
                pqs = PS()
                if pas.samp:
                    S.mm(pqs[0:C, :], zt[:, 0:C], msk[:, 0:4, :].re("p a b -> p (a b)"), True, True)
                for h in range(4):
                    hs = slice(h * 128, (h + 1) * 128)
                    S.mm32(pvn[0:C, hs], P[:, h, :], vb[:, h, :], True, False)
                    if not pas.samp:
                        S.mm32(pvn[0:C, hs], WT[:, h, :], Sd[:, h, :], False, True)
                    else:
                        pad = build_pad(s3(WT[:, h, :]))
                        for i in range(NSEQ):
                            S.mm32(pvn[0:C, hs], pad[:, i, :], sts[i][:, hs], False, i == NSEQ - 1)
                evac(vnew[rs, :], pvn[rs, :])
                for h in range(4):
                    hs = slice(h * 128, (h + 1) * 128)
                    if not pas.samp:
                        S.mm(pqs[0:C, hs], cvo[:, h, cs], Sd[:, h, :], True, True)
                    else:
                        pad = build_pad(s3(cvo[:, h, 0:T]))
                        for i in range(NSEQ):
                            S.mm(pqs[0:C, hs], pad[:, i, :], sts[i][:, hs], False, i == NSEQ - 1)
                S.tt(o_sb[rs, :].re("p (h w) -> p h w", w=128), pqs[rs, :].re("p (h w) -> p h w", w=128),
                     ecum[rs, :].un(2).bc([r1 - r0, 4, 128]), ALU.mult)
                if not pas.samp:
                    pu = PS()
                    for h in range(4):
                        hs = slice(h * 128, (h + 1) * 128)
                        S.mm(pu[:, hs], kw[rs, hs], vnew[rs, hs], True, True)
                    t3 = pl[4]
                    S.tt(t3.v().re("p (h w) -> p h w", w=128), Sd.v(), el2[:, b, :].un(2).bc([128, 4, 128]), ALU.mult)
                    S.tt(Sd.v().re("p h w -> p (h w)"), t3.v(), pu[:, :], ALU.add)
            pav = PS()
            for h in range(4):
                hs = slice(h * 128, (h + 1) * 128)
                S.mm(pav[0:C, hs], AqT[:, h, :], vnew[0:C, hs], True, True)
            S.tt(o_sb, o_sb, pav[0:C, :], ALU.add)
            if c == 0 and l == 0 and pas.idx == 0 and not pas.samp:
                dump("d_vnew", vnew[0:C, :]); dump("d_o", o_sb)
            gate_norm_out(pas, o_sb, dgt[c][0:C, :], rv[l][0:C, RV_DNORM:RV_DNORM + 128], 4, 128, ycur[0:C, :])
            y_to_yT(pas, c, ycur)
            if pas.samp:
                seq_totals(gd, 4)

                def gdn_tail(vnew=vnew, kw=kw):
                    for i in range(NSEQ):
                        vzi = scr[1]
                        S.ts(vzi[0:C, :], vnew[0:C, :], msk[0:C, M_IND, i:i + 1], ALU.mult)
                        pu = PS()
                        for h in range(4):
                            hs = slice(h * 128, (h + 1) * 128)
                            S.mm(pu[:, hs], kw[0:C, hs], vzi[0:C, hs], True, True)
                        sn = big[1].v().re("p a b -> p (a b)")[:, (i % 2) * 512:(i % 2) * 512 + 512]
                        S.tt(sn.re("p (h v) -> p h v", v=128), sts[i].re("p (h v) -> p h v", v=128),
                             elast_all[:, i, 0:4].un(2).bc([128, 4, 128]), ALU.mult)
                        S.tt(sn, sn, pu[:, :], ALU.add)
                        S.dma_out("sp", o_sd[l, i].rearrange("h k v -> k h v"), sn.re("p (h v) -> p h v", v=128))
                        yield
                tails.append(gdn_tail())
        if pas.last and not pas.samp:
            S.dma_out("sp", o_pd[l].rearrange("h k v -> k h v"), Sd.v())

    def out_and_ffn(pas, l):
        T = pas.T
        mix = KSplit(S, carve("mix", 0, 8))
        for eb2 in range(4):
            proj_fm(w_out[l][:, eb2 * 256:(eb2 + 1) * 256], 8, 256, lambda kc: mergedT[:, kc, 0:T], T,
                    lambda j, ps, m, eb2=eb2: evac(mix[:, eb2 * 2 + j, 0:T], ps))
        postnorm_residual(l, 0, pas, mix)
        if "ffn" not in stages:
            return
        prenorm(l, 3, pas)
        fT = KSplit(S, carve("fT", 0, NSLOT))
        for f in range(22):
            wa = wload(w3(w_ffn_in[l][:, f * 128:(f + 1) * 128]), 8, 128)
            wb_ = wload(w3(w_ffn_in[l][:, FH + f * 128:FH + (f + 1) * 128]), 8, 128)
            pa = PS()
            for kc in range(8):
                S.mm(pa[:, 0:T], wa[:, kc, :], hT[:, kc, 0:T], kc == 0, kc == 7)
            pb = PS()
            for kc in range(8):
                S.mm(pb[:, 0:T], wb_[:, kc, :], hT[:, kc, 0:T], kc == 0, kc == 7)
            sg = pl[2 + f % 3][:, 0:T]
            S.act(sg, pa[:, 0:T], AF.Tanh, scale=0.5)
            S.stt(sg, sg, 1.0, pa[:, 0:T], ALU.add, ALU.mult)
            S.stt(fT[:, f, 0:T], sg, 0.5, pb[:, 0:T], ALU.mult, ALU.mult)
        for ep in range(4):
            pso = [PS(), PS()]
            for kg in range(3):
                nf = 8 if kg < 2 else 6
                wo = wload(w_ffn_out[l][kg * 1024:kg * 1024 + nf * 128, ep * 256:(ep + 1) * 256].rearrange("(kc p) n -> p kc n", p=128), nf, 256)
                for jj in range(2):
                    for fc in range(nf):
                        f = kg * 8 + fc
                        S.mm(pso[jj][:, 0:T], wo[:, fc, jj * 128:(jj + 1) * 128], fT[:, f, 0:T], f == 0, f == 21)
            for jj in range(2):
                evac(mergedT[:, ep * 2 + jj, 0:T], pso[jj][:, 0:T])
        postnorm_residual(l, 3, pas, mergedT)

    passes = [Pass(False, i, npass) for i in range(npass)]
    if do_sample:
        passes.append(Pass(True, 0, npass))
    for pas in passes:
        T = pas.T
        for kc in range(8):
            if pas.samp:
                S.dma_in("sp", xT[:, kc, 0:T], xTs[kc * 128:(kc + 1) * 128, :])
            else:
                S.dma_in("sp", xT[:, kc, 0:T], xTp[kc * 128:(kc + 1) * 128, pas.idx * 512:(pas.idx + 1) * 512])
        for l in range(nlayers):
            prenorm(l, 0, pas)
            if dbg and pas.idx == 0 and l == 0 and not pas.samp:
                dump("hT0", hT.v())
            first = True
            for n, (name, fn) in enumerate((("gla", gla), ("ssd", ssd), ("gdn", gdn))):
                if name in stages:
                    del tails[:]
                    fn(pas, l)
                    if dbg and pas.idx == 0 and l == 0 and not pas.samp:
                        dump("yT_" + name, yT.v())
                    if dbg and pas.samp and l == 0:
                        dump("yTs_" + name, yT[:, :, 0:64])
                    run_gens([branch_merge(l, n, pas, first)] + list(tails))
                    del tails[:]
                    first = False
            out_and_ffn(pas, l)
        for kc in range(8):
            if pas.samp:
                S.dma_out("sp", yTs[kc * 128:(kc + 1) * 128, :], xT[:, kc, 0:T])
            else:
                S.dma_out("sp", yTp[kc * 128:(kc + 1) * 128, pas.idx * 512:(pas.idx + 1) * 512], xT[:, kc, 0:T])
    S.emit()
    es.close()
    return nc


def _c(a):
    return np.ascontiguousarray(a, dtype=np.float32)


def make_in_maps(inp):
    masks = _masks()
    rowv = np.concatenate([inp["g_gla_norm"], inp["ssd_dt_bias"], inp["ssd_a_log"], inp["ssd_d"], inp["g_ssd_norm"],
                           inp["gdn_dt_bias"], inp["gdn_a_log"], inp["g_gdn_norm"]], axis=1)[:, None, :]
    shared = {
        "w_ada": _c(inp["w_ada"]),
        "b_adaT": _c(inp["b_ada"].reshape(2, 48, 128).transpose(0, 2, 1)),
        "gainsT": _c(np.stack([inp["g_pre_mix"], inp["g_post_mix"], inp["g_pre_ffn"], inp["g_post_ffn"]], axis=1)
                     .reshape(2, 4, 8, 128).transpose(0, 3, 1, 2)),
        "w_in": _c(inp["w_in"]),
        "w_gate": _c(inp["w_gla_gate"]),
        "b_gate": _c(inp["b_gla_gate"][:, None, :]),
        "rowvec": _c(rowv),
        "w_sconvT": _c(inp["w_ssd_conv"].reshape(2, 4, 8, 128).transpose(0, 3, 2, 1)),
        "b_sconvT": _c(inp["b_ssd_conv"].reshape(2, 8, 128).transpose(0, 2, 1)),
        "w_dconvT": _c(inp["w_gdn_conv"].reshape(2, 4, 12, 128).transpose(0, 3, 2, 1)),
        "w_branch": _c(inp["w_branch"]),
        "w_out": _c(inp["w_out"]),
        "w_ffn_in": _c(inp["w_ffn_in"]),
        "w_ffn_out": _c(inp["w_ffn_out"]),
        "cst": masks,
    }
    maps = []
    for i in range(NCORE):
        sl = slice(NSEQ * i, NSEQ * (i + 1))
        m = dict(shared)
        m["xTp"] = _c(inp["x_prompt"][i].T)
        m["xTs"] = _c(inp["x_sample"][sl].reshape(NSEQ * LS, D).T)
        m["cT"] = _c(np.concatenate([inp["c_prompt"][i:i + 1], inp["c_sample"][sl]], axis=0).T)
        m["st_gla"] = _c(inp["state_gla"][:, sl])
        m["st_ssd"] = _c(inp["state_ssd"][:, sl])
        m["cv_ssd"] = _c(inp["cache_ssd_conv"][:, sl])
        m["st_gdn"] = _c(inp["state_gdn"][:, sl])
        m["cv_gdn"] = _c(inp["cache_gdn_conv"][:, sl])
        maps.append(m)
    return maps


def gather(results):
    n = len(results)
    y_p = np.stack([r["yTp"].T for r in results])
    y_s = np.concatenate([r["yTs"].T.reshape(NSEQ, LS, D) for r in results])

    def cat1(k):
        return np.concatenate([r[k][:, None] for r in results], axis=1)

    def cats(k):
        return np.concatenate([r[k] for r in results], axis=1)

    outs = (y_p, y_s, cat1("o_pg"), cat1("o_ps"), cat1("o_pcs"), cat1("o_pd"), cat1("o_pcd"),
            cats("o_sg"), cats("o_ss"), cats("o_scs"), cats("o_sd"), cats("o_scd"))
    return tuple(np.ascontiguousarray(o, dtype=np.float32) for o in outs)


def kernel(**inputs):
    inp = {k: np.asarray(v) for k, v in inputs.items()}
    nc = build()
    maps = make_in_maps(inp)
    res = run_bass_kernel_spmd(nc, maps, core_ids=list(range(NCORE)))
    return gather(res.results)
```
